# Optimizing a Trainium2 kernel written in Bass

```python
import jax, jax.numpy as jnp
from jax import lax
import numpy as np

D_MODEL = 1024
BATCH = 8
SEQ = 2048
DEPTH = 4
DEC_BATCH = 128
DEC_SEQ = 8
PAST_LEN = 16384
PAGE_SIZE = 128

N_MIXERS = 3
N_A = (DEPTH + 2) // 3
N_B = (DEPTH + 1) // 3
N_C = DEPTH // 3
CONV_W = 3
CHUNK = 128
B_HEADS = 4
B_HEAD_DIM = D_MODEL // B_HEADS
POOL_WINDOWS = (2, 4, 8, 16)
POOL_GROUPS = len(POOL_WINDOWS)
POOL_GROUP_DIM = D_MODEL // POOL_GROUPS
POOL_STATE = max(POOL_WINDOWS) - 1
D_FF = 2816
EPS = 1e-6

kernel_name = "hybrid_shortconv_chunkmlp_pool_convffn_step"


def rmsnorm(x, g):
    xf = x.astype(jnp.float32)
    y = xf * lax.rsqrt(jnp.mean(xf * xf, axis=-1, keepdims=True) + EPS)
    return (y * g.astype(jnp.float32)).astype(x.dtype)


def causal_dwconv3(x, prev, w):
    L = x.shape[1]
    xp = jnp.concatenate([prev, x], axis=1)
    y = w[0] * xp[:, 0:L]
    for k in range(1, CONV_W):
        y = y + w[k] * xp[:, k:k + L]
    return y, xp[:, -(CONV_W - 1):]


def short_conv_mixer(h, prev, w_in, w_conv, w_out):
    bg, cg, v = jnp.split(h @ w_in, 3, axis=-1)
    y, new_prev = causal_dwconv3(cg * v, prev, w_conv)
    return (bg * y) @ w_out, new_prev


def chunk_mlp_mixer(h, w_in, g_v, w_s, bias, w_out):
    Bn, L, _ = h.shape
    u, v = jnp.split(h @ w_in, 2, axis=-1)
    v = rmsnorm(v, g_v)
    n_chunks = -(-L // CHUNK)
    pad = n_chunks * CHUNK - L
    vp = jnp.pad(v, ((0, 0), (0, pad), (0, 0))).reshape(Bn, n_chunks, CHUNK, B_HEADS, B_HEAD_DIM)
    mask = jnp.tril(jnp.ones((CHUNK, CHUNK), dtype=bool))
    w_m = jnp.where(mask[None], w_s, 0)
    s = jnp.einsum('hts,bnshd->bnthd', w_m, vp) + bias.T[:, :, None]
    s = s.reshape(Bn, n_chunks * CHUNK, D_MODEL)[:, :L]
    return (u * s) @ w_out, v


def pool_mixer(h, prev, start_pos, w_group, scale):
    Bn, L, D = h.shape
    xp = jnp.concatenate([prev, h], axis=1)
    csum = jnp.cumsum(xp.astype(jnp.float32), axis=1)
    csum = jnp.pad(csum, ((0, 0), (1, 0), (0, 0)))
    pos = start_pos + jnp.arange(L)
    means = []
    for gi, w in enumerate(POOL_WINDOWS):
        sl = slice(gi * POOL_GROUP_DIM, (gi + 1) * POOL_GROUP_DIM)
        hi = csum[:, POOL_STATE + 1:POOL_STATE + 1 + L, sl]
        lo = csum[:, POOL_STATE + 1 - w:POOL_STATE + 1 - w + L, sl]
        cnt = jnp.minimum(pos + 1, w).astype(jnp.float32)[None, :, None]
        means.append((hi - lo) / cnt)
    pooled = jnp.concatenate(means, axis=-1).astype(h.dtype)
    d = (pooled - h).reshape(Bn, L, POOL_GROUPS, POOL_GROUP_DIM)
    y = jnp.einsum('blgc,gce->blge', d, w_group).reshape(Bn, L, D)
    return y * scale, xp[:, -POOL_STATE:]


def conv_ffn(h, prev, w_up, w_conv, b_conv, w_down):
    g, a = jnp.split(h @ w_up, 2, axis=-1)
    gc, new_prev = causal_dwconv3(g, prev, w_conv)
    return (jax.nn.silu(gc + b_conv) * a) @ w_down, new_prev


def trunk(x, conv_prev, pool_prev, ffn_prev, start_pos,
          g_mix, g_ffn, g_final,
          a_w_in, a_conv, a_w_out,
          b_w_in, b_g_v, b_w_s, b_bias, b_w_out,
          c_w_group, c_scale,
          f_w_up, f_conv, f_conv_b, f_w_down):
    new_conv, new_pool, new_ffn, new_v = [], [], [], []
    for i in range(DEPTH):
        kind, j = i % N_MIXERS, i // N_MIXERS
        h = rmsnorm(x, g_mix[i])
        if kind == 0:
            y, st = short_conv_mixer(h, conv_prev[j], a_w_in[j], a_conv[j], a_w_out[j])
            new_conv.append(st)
        elif kind == 1:
            y, v = chunk_mlp_mixer(h, b_w_in[j], b_g_v[j], b_w_s[j], b_bias[j], b_w_out[j])
            new_v.append(v)
        else:
            y, st = pool_mixer(h, pool_prev[j], start_pos, c_w_group[j], c_scale[j])
            new_pool.append(st)
        x = x + y
        h = rmsnorm(x, g_ffn[i])
        y, st = conv_ffn(h, ffn_prev[i], f_w_up[i], f_conv[i], f_conv_b[i], f_w_down[i])
        new_ffn.append(st)
        x = x + y
    return (rmsnorm(x, g_final), jnp.stack(new_conv), jnp.stack(new_pool),
            jnp.stack(new_ffn), jnp.stack(new_v))


def setup_inputs(seed: int = 0) -> dict:
    key = jax.random.key(seed)
    ks = jax.random.split(key, 24)
    nrm = lambda k, shape, s: jax.random.normal(k, shape, jnp.float32) * s
    D = D_MODEL
    return {
        "x_prompt": nrm(ks[0], (BATCH, SEQ, D), 1.0),
        "x_sample": nrm(ks[1], (DEC_BATCH, DEC_SEQ, D), 1.0),
        "state_shortconv": nrm(ks[2], (N_A, DEC_BATCH, CONV_W - 1, D), 1.0),
        "state_pool": nrm(ks[3], (N_C, DEC_BATCH, POOL_STATE, D), 1.0),
        "state_ffnconv": nrm(ks[4], (DEPTH, DEC_BATCH, CONV_W - 1, D_FF), 1.0),
        "g_mix": 1.0 + nrm(ks[5], (DEPTH, D), 0.05),
        "g_ffn": 1.0 + nrm(ks[6], (DEPTH, D), 0.05),
        "g_final": 1.0 + nrm(ks[7], (D,), 0.05),
        "a_w_in": nrm(ks[8], (N_A, D, 3 * D), D ** -0.5),
        "a_conv": nrm(ks[9], (N_A, CONV_W, D), CONV_W ** -0.5),
        "a_w_out": nrm(ks[10], (N_A, D, D), D ** -0.5),
        "b_w_in": nrm(ks[11], (N_B, D, 2 * D), D ** -0.5),
        "b_g_v": 1.0 + nrm(ks[12], (N_B, D), 0.05),
        "b_w_s": nrm(ks[13], (N_B, B_HEADS, CHUNK, CHUNK), CHUNK ** -0.5),
        "b_bias": 1.0 + nrm(ks[14], (N_B, B_HEADS, CHUNK), 0.1),
        "b_w_out": nrm(ks[15], (N_B, D, D), D ** -0.5),
        "c_w_group": nrm(ks[16], (N_C, POOL_GROUPS, POOL_GROUP_DIM, POOL_GROUP_DIM), POOL_GROUP_DIM ** -0.5),
        "c_scale": 1.0 + nrm(ks[17], (N_C, D), 0.1),
        "f_w_up": nrm(ks[18], (DEPTH, D, 2 * D_FF), D ** -0.5),
        "f_conv": nrm(ks[19], (DEPTH, CONV_W, D_FF), CONV_W ** -0.5),
        "f_conv_b": nrm(ks[20], (DEPTH, D_FF), 0.02),
        "f_w_down": nrm(ks[21], (DEPTH, D_FF, D), D_FF ** -0.5),
    }


def reference(x_prompt, x_sample, state_shortconv, state_pool, state_ffnconv,
              g_mix, g_ffn, g_final,
              a_w_in, a_conv, a_w_out,
              b_w_in, b_g_v, b_w_s, b_bias, b_w_out,
              c_w_group, c_scale,
              f_w_up, f_conv, f_conv_b, f_w_down):
    params = (g_mix, g_ffn, g_final, a_w_in, a_conv, a_w_out,
              b_w_in, b_g_v, b_w_s, b_bias, b_w_out, c_w_group, c_scale,
              f_w_up, f_conv, f_conv_b, f_w_down)
    dt = x_prompt.dtype
    conv0 = jnp.zeros((N_A, BATCH, CONV_W - 1, D_MODEL), dt)
    pool0 = jnp.zeros((N_C, BATCH, POOL_STATE, D_MODEL), dt)
    ffn0 = jnp.zeros((DEPTH, BATCH, CONV_W - 1, D_FF), dt)
    y_prompt, conv_p, pool_p, ffn_p, _ = trunk(x_prompt, conv0, pool0, ffn0, 0, *params)
    y_sample, conv_s, pool_s, ffn_s, chunkv_s = trunk(
        x_sample, state_shortconv, state_pool, state_ffnconv, PAST_LEN, *params)
    return (y_prompt, y_sample, conv_p, conv_s, pool_p, pool_s, ffn_p, ffn_s, chunkv_s)
```

```python
from contextlib import ExitStack

import numpy as np
import concourse.bass as bass
import concourse.mybir as mybir
from concourse.bass_utils import run_bass_kernel_spmd

F32 = mybir.dt.float32
BF16 = mybir.dt.bfloat16
ALU = mybir.AluOpType
AF = mybir.ActivationFunctionType

D = 1024
KC = 8
DFF = 2816
MC = 22
DEPTH = 4
EPS = 1e-6
SEQ = 2048
NSEQ = 16
DSEQ = 8
TP = 1024
T0 = TP + NSEQ * DSEQ
SLOT = 4096
NCORES = 8
WIN = (2, 4, 8, 16)
MERGE_PAIRS = False
SUM_ENG = ("dve", "dve", "dve", "dve", "dve", "pool", "pool", "pool")
CH_ORDER = (7, 6, 5, 4, 3, 2, 1, 0)

PV_GM = 0
PV_GF = 32
PV_GFIN = 64
PV_AC = 72
PV_CS = 120
PV_FC = 128
PV_FB = 392
PV_N = 480

COMPUTE = ("pe", "act", "dve", "pool")
QUEUES = ("sp",)
ENGS = ("pe", "act", "dve", "pool", "sp")


class Buf:
    __slots__ = ("name", "w", "r", "rd", "extra", "grp")

    def __init__(self, name, grp=None):
        self.name = name
        self.w = None
        self.r = {}
        self.rd = []
        self.extra = []
        self.grp = grp


class Op:
    __slots__ = ("eng", "fn", "deps", "signal", "chan", "val", "idx")

    def __init__(self, eng, fn, chan):
        self.eng = eng
        self.fn = fn
        self.chan = chan
        self.deps = []
        self.signal = False
        self.val = 0
        self.idx = 0


class Prog:
    def __init__(self):
        self.q = {e: [] for e in ENGS}
        self.active = {}
        self.grp_bufs = {}
        self.chans = {}
        self.n = 0

    def buf(self, name, grp=None):
        b = Buf(name, grp)
        if grp is not None:
            self.grp_bufs.setdefault(grp, []).append(b)
        return b

    def _activate(self, b):
        if b.grp is None:
            return
        region = b.grp[0]
        cur = self.active.get(region)
        if cur == b.grp:
            return
        if cur is not None:
            pend = {}
            for ob in self.grp_bufs[cur]:
                for o in ([ob.w] if ob.w is not None else []) + list(ob.r.values()) + ob.rd + ob.extra:
                    key = o.eng if o.chan is None else ("dma", id(o))
                    if key not in pend or pend[key].idx < o.idx:
                        pend[key] = o
            pl = list(pend.values())
            for nb in self.grp_bufs[b.grp]:
                nb.extra = list(pl)
        self.active[region] = b.grp

    def op(self, eng, fn, reads=(), writes=(), chan=None):
        o = Op(eng, fn, chan)
        o.idx = self.n
        self.n += 1
        for b in reads:
            self._activate(b)
        for b in writes:
            self._activate(b)
        deps = {}
        for b in reads:
            if b.w is not None:
                deps[id(b.w)] = b.w
            for x in b.extra:
                deps[id(x)] = x
        for b in writes:
            if b.w is not None:
                deps[id(b.w)] = b.w
            for x in b.r.values():
                deps[id(x)] = x
            for x in b.rd:
                deps[id(x)] = x
            for x in b.extra:
                deps[id(x)] = x
        for d in deps.values():
            if d is o:
                continue
            if d.chan is None and d.eng == "pe" and eng == "pe" and chan is None:
                continue
            o.deps.append(d)
            d.signal = True
        for b in reads:
            if chan is None:
                b.r[eng] = o
            else:
                b.rd.append(o)
        for b in writes:
            b.w = o
            b.r = {}
            b.rd = []
            b.extra = []
        self.q[eng].append(o)
        return o

    def emit(self, nc, block, engines, esem, csem, final_chans):
        for e in ENGS:
            cnt = 0
            for o in self.q[e]:
                if o.chan is None:
                    if o.signal:
                        cnt += 1
                    o.val = cnt
        allops = sorted((o for e in ENGS for o in self.q[e]), key=lambda o: o.idx)
        ccount = {}
        for o in allops:
            if o.chan is not None:
                ccount[o.chan] = ccount.get(o.chan, 0) + 16
                o.val = ccount[o.chan]

        know = {e: {} for e in ENGS}
        opknow = {}
        waits = {}
        for o in allops:
            ke = know[o.eng]
            wl = []
            for d in sorted(o.deps, key=lambda d: -d.idx):
                key = ("e", d.eng) if d.chan is None else ("c", d.chan)
                if ke.get(key, 0) >= d.val:
                    continue
                wl.append((key, d.val))
                ke[key] = d.val
                for k2, v2 in opknow[id(d)].items():
                    if ke.get(k2, 0) < v2:
                        ke[k2] = v2
            waits[id(o)] = wl
            if o.chan is not None or o.signal:
                opknow[id(o)] = dict(ke)

        def run(e, eng):
            for o in self.q[e]:
                for key, val in waits[id(o)]:
                    sem = esem[key[1]] if key[0] == "e" else csem[key[1]]
                    eng.wait_ge(sem, val)
                ins = o.fn(eng)
                if o.chan is not None:
                    ins.then_inc(csem[o.chan], 16)
                elif o.signal:
                    ins.then_inc(esem[e], 1)
            if e == "sp":
                for c in sorted(ccount):
                    eng.wait_ge(csem[c], ccount[c])

        for e in ENGS:
            deco = getattr(block, engines[e])
            deco(lambda eng, e=e: run(e, eng))
        self.nwaits = {e: sum(len(waits[id(o)]) for o in self.q[e]) for e in ENGS}


def _kp(w, cols):
    k = w.shape[0] // 128
    return np.ascontiguousarray(w[:, cols].reshape(k, 128, -1).transpose(1, 0, 2))


def _pad(a):
    a = a.reshape(128, -1)
    out = np.zeros((128, SLOT), np.float32)
    out[:, : a.shape[1]] = a
    return out


def slot_plan():
    plan = []
    for i in range(DEPTH):
        kind, j = i % 3, i // 3
        if kind == 0:
            for jc in range(8):
                plan.append(("a_in", i, jc, 8 * 384))
            for q in range(2):
                plan.append(("a_out", i, q, 8 * 512))
        elif kind == 1:
            for h in range(2):
                plan.append(("b_v", i, h, 8 * 512))
            for q in range(2):
                plan.append(("b_u", i, q, 8 * 512))
            for q in range(2):
                plan.append(("b_out", i, q, 8 * 512))
        else:
            plan.append(("c_g", i, 0, 2048))
        for q in range(11):
            plan.append(("f_up", i, q, 8 * 512))
        for mo in range(8):
            plan.append(("f_dn", i, mo, MC * 128))
    return plan


def pack_stream(inp):
    slots = []
    for kind, i, x, _ in slot_plan():
        j = i // 3
        if kind == "a_in":
            w = inp["a_w_in"][j]
            cols = np.concatenate([np.arange(x * 128, x * 128 + 128) + o for o in (0, 1024, 2048)])
            slots.append(_pad(_kp(w, cols)))
        elif kind == "a_out":
            slots.append(_pad(_kp(inp["a_w_out"][j], np.arange(x * 512, x * 512 + 512))))
        elif kind == "b_v":
            slots.append(_pad(_kp(inp["b_w_in"][j], np.arange(1024 + x * 512, 1024 + x * 512 + 512))))
        elif kind == "b_u":
            slots.append(_pad(_kp(inp["b_w_in"][j], np.arange(x * 512, x * 512 + 512))))
        elif kind == "b_out":
            slots.append(_pad(_kp(inp["b_w_out"][j], np.arange(x * 512, x * 512 + 512))))
        elif kind == "c_g":
            w = inp["c_w_group"][j]
            a = w.reshape(4, 2, 128, 256).transpose(2, 0, 1, 3)
            slots.append(_pad(np.ascontiguousarray(a)))
        elif kind == "f_up":
            w = inp["f_w_up"][i]
            cols = np.concatenate([
                np.arange(2 * x * 128, 2 * x * 128 + 256),
                np.arange(DFF + 2 * x * 128, DFF + 2 * x * 128 + 256),
            ])
            slots.append(_pad(_kp(w, cols)))
        elif kind == "f_dn":
            slots.append(_pad(_kp(inp["f_w_down"][i], np.arange(x * 128, x * 128 + 128))))
    return np.stack(slots)


def fm(v):
    n = v.shape[-1] // 128
    a = v.reshape(v.shape[:-1] + (n, 128))
    return np.moveaxis(a, -1, 0)


def pack_pvec(inp):
    pv = np.zeros((128, PV_N), np.float32)
    pv[:, PV_GM:PV_GM + 32] = fm(inp["g_mix"]).reshape(128, 32)
    pv[:, PV_GF:PV_GF + 32] = fm(inp["g_ffn"]).reshape(128, 32)
    pv[:, PV_GFIN:PV_GFIN + 8] = fm(inp["g_final"]).reshape(128, 8)
    pv[:, PV_AC:PV_AC + 48] = fm(inp["a_conv"]).reshape(128, 48)
    pv[:, PV_CS:PV_CS + 8] = fm(inp["c_scale"]).reshape(128, 8)
    pv[:, PV_FC:PV_FC + 264] = fm(inp["f_conv"]).reshape(128, 264)
    pv[:, PV_FB:PV_FB + 88] = fm(inp["f_conv_b"]).reshape(128, 88)
    return pv


def pack_pbc(inp):
    pb = np.zeros((128, 2048), np.float32)
    pb[:, 0:1024] = np.broadcast_to(inp["b_g_v"][0][None, :], (128, 1024))
    bias = inp["b_bias"][0]
    pb[:, 1024:1536] = np.broadcast_to(bias.reshape(1, 512), (128, 512))
    bs = np.tile(bias[:, 0:8], (1, 16))
    pb[:, 1536:2048] = np.broadcast_to(bs.reshape(1, 512), (128, 512))
    return pb


def build_nc(nslots):
    nc = bass.Bass("TRN2", target_bir_lowering=False)

    def din(name, shape):
        return nc.dram_tensor(name, list(shape), F32, kind="ExternalInput").ap()

    def dout(name, shape):
        return nc.dram_tensor(name, list(shape), F32, kind="ExternalOutput").ap()

    xT = din("xT", (128, KC, SEQ))
    xsT = din("xsT", (128, KC, 128))
    st_a = din("st_a", (128, 2, KC, 34))
    st_f = din("st_f", (128, DEPTH, MC, 34))
    st_p = din("st_p", (128, KC, 255))
    pvec = din("pvec", (128, PV_N))
    pbc = din("pbc", (128, 2048))
    wsT = din("wsT", (128, 4, 128))
    wstream = din("wstream", (nslots, 128, SLOT))
    yT = dout("yT", (128, KC, SEQ + 128))
    o_a = dout("o_a", (128, 2, KC, 34))
    o_f = dout("o_f", (128, DEPTH, MC, 34))
    o_p = dout("o_p", (128, KC, 255))
    o_cv = dout("o_cv", (128, D))

    P = Prog()
    plan = slot_plan()

    with ExitStack() as es:
        def sb(name, shape, dt=F32):
            return es.enter_context(nc.sbuf_tensor(name, list(shape), dt))

        X = sb("X", (128, KC, T0))
        H = sb("H", (128, KC, T0), BF16)
        RA = sb("RA", (128, 12672))
        RB = sb("RB", (128, 6528))
        NWS = 3
        WS = [sb(f"WS{r}", (128, SLOT), BF16) for r in range(NWS)]
        RS = [sb(f"RS{r}", (128, 512)) for r in range(3)]
        STA = sb("STA", (128, 2, KC, 34))
        STF = sb("STF", (128, DEPTH, MC, 34))
        STP = sb("STP", (128, KC, 255))
        PV = sb("PV", (128, PV_N))
        PBC = sb("PBC", (128, 2048))
        ONES = sb("ONES", (128, 128), BF16)
        CORR = sb("CORR", (128, 4, 16))
        IOT = sb("IOT", (128, 16))
        WMF = sb("WMF", (128, 4, 128))
        WMSF = sb("WMSF", (128, 4, 128))
        WMT = sb("WMT", (128, 4, 128), BF16)
        WMTS = sb("WMTS", (128, 4, 128), BF16)
        SS = sb("SS", (128, 9, 4))
        JUNK = sb("JUNK", (128, 2))
        HB16H = sb("HB16H", (128, KC, 15), BF16)
        IDS = sb("IDS", (128, 8, 128), BF16)
        IDF = sb("IDF", (128, 128))
        EPSB = sb("EPSB", (128, 1))
        PS = es.enter_context(nc.psum_tensor("PS", [128, 8, 512], F32))

        RAb = RA[:, :].bitcast(BF16)
        INNER = RAb.rearrange("p (m t) -> p m t", m=MC)
        VN = RAb[:, 8 * T0:16 * T0].rearrange("p (n f) -> p n f", f=D)
        HB16 = RA[:, 0:4156].bitcast(BF16).rearrange("p (c t) -> p c t", c=KC)
        HF30 = RA[:, 4156:4396].rearrange("p (c t) -> p c t", c=KC)
        HFS = RA[:, 4396:4396 + 8 * 368].rearrange("p (c s t) -> p c s t", c=KC, s=NSEQ)
        G = [RB[:, 0:1026], RB[:, 1186:2212]]
        GS = [RB[:, 1026:1186].rearrange("p (s t) -> p s t", t=10),
              RB[:, 2212:2372].rearrange("p (s t) -> p s t", t=10)]
        TT = [RB[:, 2372 + 512 * r:2372 + 512 * (r + 1)] for r in range(6)]
        VNF = RB[:, 5444:6468]
        PSET = [[RB[:, 1039 * (2 * si + r):1039 * (2 * si + r + 1)] for r in range(2)] for si in range(2)]
        PSETS = [[RB[:, 4156 + 368 * (2 * si + r):4156 + 368 * (2 * si + r + 1)].rearrange("p (s t) -> p s t", s=NSEQ)
                  for r in range(2)] for si in range(2)]
        GVB = PBC[:, 0:1024]
        BB = PBC[:, 1024:1536].rearrange("p (h t) -> p h t", h=4)
        BBS = PBC[:, 1536:2048].rearrange("p (h t) -> p h t", h=4)

        XB = [[P.buf(f"X{c}_{t}") for t in range(3)] for c in range(KC)]
        HB = [[P.buf(f"H{c}_{t}") for t in range(3)] for c in range(KC)]
        INB = [[P.buf(f"IN{m}_{t}", ("RA", "inner")) for t in range(3)] for m in range(MC)]
        OBB = [[P.buf(f"OB{m}_{t}", ("RA", "mixb")) for t in range(3)] for m in range(KC)]
        VNB = [P.buf(f"VN{n}", ("RA", "mixb")) for n in range(9)]
        HFB = [[P.buf(f"HF{c}_{t}", ("RA", "poolh")) for t in range(3)] for c in range(KC)]
        HFH = P.buf("HFhalo", ("RA", "poolh"))
        F30B = P.buf("HF30", ("RA", "poolh"))
        C_IDS = P.buf("c_ids")
        JUNKB = P.buf("junk")
        HB16HB = P.buf("HB16H")
        GB = [P.buf(f"G{r}", ("RB", "ffn")) for r in range(2)]
        TB = [P.buf(f"T{r}", ("RB", "ffn")) for r in range(6)]
        VNFB = P.buf("VNF", ("RB", "ffn"))
        PAB = [[P.buf(f"PA{si}_{r}", ("RB", "pool")) for r in range(4)] for si in range(2)]
        WSB = [P.buf(f"WS{r}") for r in range(NWS)]
        RSB = [P.buf(f"RS{r}") for r in range(3)]
        BANKB = [P.buf(f"BK{r}") for r in range(8)]
        STAB = [[P.buf(f"STA{j}_{c}") for c in range(KC)] for j in range(2)]
        STFB = [[P.buf(f"STF{i}_{m}") for m in range(MC)] for i in range(DEPTH)]
        STPB = P.buf("STP")
        C_PV = P.buf("c_pv")
        C_PBC = P.buf("c_pbc")
        C_WM = P.buf("c_wm")
        C_WMS = P.buf("c_wms")
        C_WMSI = [P.buf(f"c_wmsi{s}") for s in range(NSEQ)]
        C_ONES = P.buf("c_ones")
        C_CORR = P.buf("c_corr")
        C_EPS = P.buf("c_eps")
        ALLST = [b for r in STAB for b in r] + [b for r in STFB for b in r] + [STPB]
        SSB = [P.buf(f"SS{n}") for n in range(9)]

        state = {"slot": 0, "rs": 0, "tt": 0, "issued": 0, "sp": 0, "lp": 0, "dp": 0, "down": None}

        SHORT = [0, 1, 2, 3]
        LONG = [4, 5, 6, 7]

        def set_rings(nshort, sp=0):
            SHORT[:] = list(range(nshort))
            LONG[:] = list(range(nshort, 8))
            state["sp"] = sp
            state["lp"] = 0

        def bank(kind="short"):
            if state["down"] is not None:
                ring = state["down"]
                b = ring[state["dp"] % len(ring)]
                state["dp"] += 1
            elif kind == "long":
                b = LONG[state["lp"] % len(LONG)]
                state["lp"] += 1
            else:
                b = SHORT[state["sp"] % len(SHORT)]
                state["sp"] += 1
            return PS[:, b, :], BANKB[b]

        def stats_begin(pi):
            tbs = tblocks(pi)
            nst = len(tbs)
            state["down"] = SHORT[nst:] + [LONG[(state["lp"] + i) % len(LONG)] for i in range(len(LONG))]
            state["dp"] = 0
            return {"pend": [], "banks": {tbi: (PS[:, SHORT[j], :], BANKB[SHORT[j]]) for j, (tbi, _, _) in enumerate(tbs)}}

        def stats_add(acc, c, tbi, t0, n, first, last):
            P.op("act", lambda e: e.activation(out=H[:, c, t0:t0 + n], in_=X[:, c, t0:t0 + n], func=AF.Square),
                 reads=[XB[c][tbi]], writes=[HB[c][tbi]])
            bk, bkb = acc["banks"][tbi]
            acc["pend"].append(lambda: P.op("pe", lambda e: e.matmul(bk[:, 0:n], lhsT=ONES[:, :], rhs=H[:, c, t0:t0 + n],
                                                                    start=first, stop=last),
                                            reads=[HB[c][tbi], C_ONES], writes=[bkb]))

        def stats_flush(acc, keep=1):
            pend = acc["pend"]
            n = max(0, len(pend) - keep)
            for f in pend[:n]:
                f()
            acc["pend"] = pend[n:]

        def stats_end(nst):
            state["down"] = None
            state["sp"] = nst % len(SHORT)

        def rsbuf():
            r = state["rs"] % 3
            state["rs"] += 1
            return RS[r], RSB[r]

        def ttpair():
            r = state["tt"] % 3
            state["tt"] += 1
            return (TT[2 * r], TB[2 * r]), (TT[2 * r + 1], TB[2 * r + 1])

        def issue_slot(s):
            kind, _, _, used = plan[s % len(plan)]
            r = s % NWS
            src = wstream[s % len(plan), :, 0:used].rearrange("p (a b) -> p a b", a=2)
            dst = WS[r][:, 0:used].rearrange("p (a b) -> p a b", a=2)
            P.op("pool", lambda e, dst=dst, src=src: e.dma_start(out=dst, in_=src),
                 writes=[WSB[r]], chan=f"ws{r}")

        def prefetch(n):
            total = 2 * len(plan)
            while state["issued"] < min(state["slot"] + n, total):
                issue_slot(state["issued"])
                state["issued"] += 1

        def next_slot(expect_kind):
            s = state["slot"]
            state["slot"] += 1
            kind, _, _, used = plan[s % len(plan)]
            assert kind == expect_kind, (kind, expect_kind)
            while state["issued"] <= s:
                issue_slot(state["issued"])
                state["issued"] += 1
            r = s % NWS
            return WS[r], WSB[r]

        P.op("sp", lambda e: e.dma_start(out=PV[:, :], in_=pvec[:, :]), writes=[C_PV], chan="s_pv")
        P.op("pool", lambda e: e.memset(ONES[:, :], 1.0), writes=[C_ONES])
        P.op("pool", lambda e: e.memset(EPSB[:, :], EPS), writes=[C_EPS])

        def state_loads():
            P.op("sp", lambda e: e.dma_start(out=STA[:, :, :, :], in_=st_a[:, :, :, :]),
                 writes=[b for r in STAB for b in r], chan="s_sta")
            P.op("sp", lambda e: e.dma_start(out=STF[:, :, :, :], in_=st_f[:, :, :, :]),
                 reads=[HB[KC - 1][0]], writes=[b for r in STFB for b in r], chan="s_stf")
            P.op("sp", lambda e: e.dma_start(out=STP[:, :, :], in_=st_p[:, :, :]),
                 reads=[HB[KC - 1][0]], writes=[STPB], chan="s_stp")

        def late_setup():
            P.op("sp", lambda e: e.dma_start(out=PBC[:, :], in_=pbc[:, :]), writes=[C_PBC], chan="s_pbc")
            P.op("sp", lambda e: e.dma_start(out=WMF[:, :, :], in_=wsT[:, :, :]), writes=[C_WM], chan="s_wm")
            P.op("pool", lambda e: e.memset(WMSF[:, :, :], 0.0), writes=C_WMSI)
            for s in range(NSEQ):
                P.op("sp", lambda e, s=s: e.dma_start(out=WMSF[8 * s:8 * s + 8, :, 8 * s:8 * s + 8],
                                                      in_=wsT[0:8, :, 0:8]), writes=[C_WMSI[s]], chan=f"s_wms{s}")
            P.op("pool", lambda e: e.affine_select(out=WMT[:, :, :], in_=WMF[:, :, :], pattern=[[0, 4], [1, 128]],
                                                   compare_op=ALU.is_ge, fill=0.0, base=0, channel_multiplier=-1),
                 reads=[C_WM], writes=[C_WM])
            P.op("pool", lambda e: e.affine_select(out=WMTS[:, :, :], in_=WMSF[:, :, :], pattern=[[0, 4], [1, 128]],
                                                   compare_op=ALU.is_ge, fill=0.0, base=0, channel_multiplier=-1),
                 reads=C_WMSI, writes=[C_WMS])
            P.op("pool", lambda e: e.memset(HB16H[:, :, :], 0.0), writes=[HB16HB])
            for g in range(4):
                for r, val in enumerate((1.0 / WIN[g] - 1.0, 1.0 / WIN[g])):
                    P.op("pool", lambda e, val=val: e.memset(IDF[:, :], val), reads=[C_IDS], writes=[C_IDS])
                    P.op("pool", lambda e, g=g, r=r: e.affine_select(
                        out=IDS[:, 2 * g + r, :], in_=IDF[:, :], pattern=[[1, 128]], compare_op=ALU.is_equal,
                        fill=0.0, base=0, channel_multiplier=-1), reads=[C_IDS], writes=[C_IDS])
            P.op("pool", lambda e: e.iota(IOT[:, :], [[1, 16]], base=1, channel_multiplier=0,
                                          allow_small_or_imprecise_dtypes=True), writes=[C_CORR])
            P.op("dve", lambda e: e.reciprocal(out=IOT[:, :], in_=IOT[:, :]), reads=[C_CORR], writes=[C_CORR])
            for g in range(4):
                P.op("dve", lambda e, g=g: e.tensor_scalar(out=CORR[:, g, :], in0=IOT[:, :], scalar1=float(WIN[g]),
                                                           scalar2=None, op0=ALU.mult), reads=[C_CORR], writes=[C_CORR])

        CONST_READ = [C_PV]

        def pvc(off):
            return PV[:, off:off + 1]

        def tblocks(pi):
            t = [(0, 0, 512), (1, 512, 512)]
            if pi == 0:
                t.append((2, 1024, 128))
            return t

        def norm(pi, goff, kind, acc=None):
            for (tbi, t0, n) in tblocks(pi):
                hb = [HB[c][tbi] for c in range(KC)]
                xb = [XB[c][tbi] for c in range(KC)]
                if acc is not None:
                    bk, bkb = acc["banks"][tbi]
                else:
                    bk, bkb = bank()
                    for c in range(KC):
                        P.op("act", lambda e, c=c, t0=t0, n=n: e.activation(out=H[:, c, t0:t0 + n],
                                                                            in_=X[:, c, t0:t0 + n], func=AF.Square),
                             reads=[xb[c]], writes=[hb[c]])
                    for c in range(KC):
                        P.op("pe", lambda e, c=c, t0=t0, n=n, bk=bk: e.matmul(bk[:, 0:n], lhsT=ONES[:, :],
                                                                             rhs=H[:, c, t0:t0 + n],
                                                                             start=(c == 0), stop=(c == KC - 1)),
                             reads=[hb[c], C_ONES], writes=[bkb])
                rs, rsb = rsbuf()
                P.op("act", lambda e, n=n, bk=bk, rs=rs: e.activation(out=rs[:, 0:n], in_=bk[:, 0:n], func=AF.Ln,
                                                                      bias=EPSB[:, 0:1], scale=1.0 / D),
                     reads=[bkb, C_EPS], writes=[rsb])
                P.op("act", lambda e, n=n, rs=rs: e.activation(out=rs[:, 0:n], in_=rs[:, 0:n], func=AF.Exp,
                                                               scale=-0.5),
                     reads=[rsb], writes=[rsb])
                if kind == "y":
                    col = pi * TP + t0 if t0 < TP else SEQ + (t0 - TP)
                    for c in range(KC):
                        P.op("dve", lambda e, c=c, t0=t0, n=n, rs=rs: e.scalar_tensor_tensor(
                            out=X[:, c, t0:t0 + n], in0=X[:, c, t0:t0 + n], scalar=pvc(goff + c), in1=rs[:, 0:n],
                            op0=ALU.mult, op1=ALU.mult), reads=[xb[c], rsb] + CONST_READ, writes=[xb[c]])
                        P.op("sp", lambda e, c=c, t0=t0, n=n, col=col: e.dma_start(out=yT[:, c, col:col + n],
                                                                                 in_=X[:, c, t0:t0 + n]),
                             reads=[xb[c]], chan=f"y{c}_{tbi}")
                elif kind == "hf":
                    for c in range(KC):
                        if t0 < TP:
                            dst = HB16[:, c, 15 + t0:15 + t0 + n]
                            i1 = rs[:, 0:n]
                            i0 = X[:, c, t0:t0 + n]
                        else:
                            dst = HFS[:, c, :, 15:23]
                            i1 = rs[:, 0:n].rearrange("p (s t) -> p s t", t=DSEQ)
                            i0 = X[:, c, t0:t0 + n].rearrange("p (s t) -> p s t", t=DSEQ)
                        P.op("dve", lambda e, c=c, dst=dst, i0=i0, i1=i1: e.scalar_tensor_tensor(
                            out=dst, in0=i0, scalar=pvc(goff + c), in1=i1, op0=ALU.mult, op1=ALU.mult),
                            reads=[xb[c], rsb] + CONST_READ, writes=[HFB[c][tbi]])
                    if pi == 0 and t0 == 0:
                        for c in range(KC):
                            P.op("dve", lambda e, c=c, rs=rs: e.scalar_tensor_tensor(
                                out=HF30[:, c, 15:30], in0=X[:, c, 0:15], scalar=pvc(goff + c), in1=rs[:, 0:15],
                                op0=ALU.mult, op1=ALU.mult), reads=[xb[c], rsb] + CONST_READ, writes=[F30B])
                    if pi == 1 and t0 == TP - 512:
                        for c in range(KC):
                            P.op("dve", lambda e, c=c, rs=rs: e.scalar_tensor_tensor(
                                out=STP[:, c, 0:15], in0=X[:, c, TP - 15:TP], scalar=pvc(goff + c),
                                in1=rs[:, 512 - 15:512], op0=ALU.mult, op1=ALU.mult),
                                reads=[xb[c], rsb] + CONST_READ, writes=[STPB])
                else:
                    for c in range(KC):
                        P.op("dve", lambda e, c=c, t0=t0, n=n, rs=rs: e.scalar_tensor_tensor(
                            out=H[:, c, t0:t0 + n], in0=X[:, c, t0:t0 + n], scalar=pvc(goff + c), in1=rs[:, 0:n],
                            op0=ALU.mult, op1=ALU.mult), reads=[xb[c], rsb] + CONST_READ, writes=[hb[c]])
            if acc is not None:
                stats_end(len(tblocks(pi)))

        def linear_out(pi, kind, src, srcb, nk, resid_scale_off=None):
            acc = stats_begin(pi)
            for q in range(2):
                ws, wsb = next_slot(kind)
                wv = ws[:, :].rearrange("p (k n) -> p k n", k=nk)
                for mo4 in range(4):
                    mo = q * 4 + mo4
                    for (tbi, t0, n) in tblocks(pi):
                        bk, bkb = bank()
                        for k in range(nk):
                            P.op("pe", lambda e, k=k, mo4=mo4, t0=t0, n=n, bk=bk, wv=wv: e.matmul(
                                bk[:, 0:n], lhsT=wv[:, k, mo4 * 128:(mo4 + 1) * 128], rhs=src[:, k, t0:t0 + n],
                                start=(k == 0), stop=(k == nk - 1)), reads=[wsb, srcb[k][tbi]], writes=[bkb])
                        stats_flush(acc)
                        P.op("dve", lambda e, mo=mo, t0=t0, n=n, bk=bk: e.tensor_tensor(
                            out=X[:, mo, t0:t0 + n], in0=bk[:, 0:n], in1=X[:, mo, t0:t0 + n], op=ALU.add),
                            reads=[bkb, XB[mo][tbi]], writes=[XB[mo][tbi]])
                        stats_add(acc, mo, tbi, t0, n, mo == 0, mo == KC - 1)
            stats_flush(acc, keep=0)
            return acc

        def conv_tail(pi, tbi, t0, n, gi, w0, w1, ta, tab, tbb_ap, tbbb, extra_reads):
            if t0 < TP:
                g1 = G[gi][:, 1 + t0:1 + t0 + n]
                g0 = G[gi][:, t0:t0 + n]
                a_ = ta[:, 0:n]
                b_ = tbb_ap[:, 0:n]
            else:
                g1 = GS[gi][:, :, 1:9]
                g0 = GS[gi][:, :, 0:8]
                a_ = ta[:, 0:n].rearrange("p (s t) -> p s t", t=DSEQ)
                b_ = tbb_ap[:, 0:n].rearrange("p (s t) -> p s t", t=DSEQ)
            P.op("dve", lambda e: e.scalar_tensor_tensor(out=b_, in0=g1, scalar=w1, in1=a_, op0=ALU.mult, op1=ALU.add),
                 reads=[GB[gi], tab] + extra_reads, writes=[tbbb])
            P.op("dve", lambda e: e.scalar_tensor_tensor(out=a_, in0=g0, scalar=w0, in1=b_, op0=ALU.mult, op1=ALU.add),
                 reads=[GB[gi], tbbb] + extra_reads, writes=[tab])

        def halo_in(pi, gi, st_ap, stb):
            P.op("act", lambda e: e.copy(out=G[gi][:, 0:2], in_=st_ap[:, 0:2]), reads=[stb], writes=[GB[gi]])
            if pi == 0:
                P.op("act", lambda e: e.copy(out=GS[gi][:, :, 0:2],
                                             in_=st_ap[:, 2:34].rearrange("p (s t) -> p s t", t=2)),
                     reads=[stb], writes=[GB[gi]])

        def halo_out(pi, gi, st_ap, stb):
            P.op("act", lambda e: e.copy(out=st_ap[:, 0:2], in_=G[gi][:, TP:TP + 2]), reads=[GB[gi]], writes=[stb])
            if pi == 0:
                P.op("act", lambda e: e.copy(out=st_ap[:, 2:34].rearrange("p (s t) -> p s t", t=2),
                                             in_=GS[gi][:, :, 8:10]), reads=[GB[gi]], writes=[stb])

        def ffn(pi, i):
            tail = [None]

            def unit(ws_v, wsb, ms, tbi, t0, n):
                bks = []
                for (m, mm, gi) in ms:
                    bks.append((bank(), bank("long")))
                for k in range(KC):
                    for (m, mm, gi), ((bg, bgb), (ba, bab)) in zip(ms, bks):
                        P.op("pe", lambda e, k=k, mm=mm, bg=bg: e.matmul(
                            bg[:, 0:n], lhsT=ws_v[:, k, mm * 128:(mm + 1) * 128], rhs=H[:, k, t0:t0 + n],
                            start=(k == 0), stop=(k == KC - 1)), reads=[wsb, HB[k][tbi]], writes=[bgb])
                        P.op("pe", lambda e, k=k, mm=mm, ba=ba: e.matmul(
                            ba[:, 0:n], lhsT=ws_v[:, k, 256 + mm * 128:256 + (mm + 1) * 128], rhs=H[:, k, t0:t0 + n],
                            start=(k == 0), stop=(k == KC - 1)), reads=[wsb, HB[k][tbi]], writes=[bab])
                for (m, mm, gi), ((bg, bgb), (ba, bab)) in zip(ms, bks):
                    w0, w1, w2 = (pvc(PV_FC + i * 66 + k * 22 + m) for k in range(3))
                    bcol = pvc(PV_FB + i * 22 + m)
                    (ta, tab), (tb_, tbb) = ttpair()
                    if t0 < TP:
                        gdst = G[gi][:, 2 + t0:2 + t0 + n]
                        gsrc = bg[:, 0:n]
                    else:
                        gdst = GS[gi][:, :, 2:10]
                        gsrc = bg[:, 0:n].rearrange("p (s t) -> p s t", t=DSEQ)
                    P.op("act", lambda e, gdst=gdst, gsrc=gsrc: e.copy(out=gdst, in_=gsrc),
                         reads=[bgb], writes=[GB[gi]])
                    P.op("act", lambda e, bg=bg, ta=ta, w2=w2, bcol=bcol: e.activation(
                        out=ta[:, 0:n], in_=bg[:, 0:n], func=AF.Identity, bias=bcol, scale=w2),
                        reads=[bgb] + CONST_READ, writes=[tab])
                    conv_tail(pi, tbi, t0, n, gi, w0, w1, ta, tab, tb_, tbb, CONST_READ)
                    if tail[0] is not None:
                        tail[0]()

                    def mk_tail(m=m, ta=ta, tab=tab, tb_=tb_, tbb=tbb, ba=ba, bab=bab):
                        P.op("act", lambda e: e.activation(out=tb_[:, 0:n], in_=ta[:, 0:n], func=AF.Silu),
                             reads=[tab], writes=[tbb])
                        P.op("dve", lambda e: e.tensor_tensor(out=INNER[:, m, t0:t0 + n], in0=ba[:, 0:n],
                                                              in1=tb_[:, 0:n], op=ALU.mult),
                             reads=[bab, tbb], writes=[INB[m][tbi]])
                    tail[0] = mk_tail

            set_rings(3)
            for q in range(11):
                ws, wsb = next_slot("f_up")
                wv = ws[:, :].rearrange("p (k n) -> p k n", k=KC)
                ms = [(2 * q + mm, mm, mm) for mm in range(2)]
                if MERGE_PAIRS or q == 0:
                    for (m, mm, gi) in ms:
                        halo_in(pi, gi, STF[:, i, m, :], STFB[i][m])
                    for (tbi, t0, n) in tblocks(pi):
                        unit(wv, wsb, ms, tbi, t0, n)
                    for (m, mm, gi) in ms:
                        halo_out(pi, gi, STF[:, i, m, :], STFB[i][m])
                else:
                    for (m, mm, gi) in ms:
                        halo_in(pi, gi, STF[:, i, m, :], STFB[i][m])
                        for (tbi, t0, n) in tblocks(pi):
                            unit(wv, wsb, [(m, mm, gi)], tbi, t0, n)
                        halo_out(pi, gi, STF[:, i, m, :], STFB[i][m])
            tail[0]()
            if pi == 1 and i == DEPTH - 1:
                P.op("sp", lambda e: e.dma_start(out=o_f[:, :, :, :], in_=STF[:, :, :, :]),
                     reads=[b for r in STFB for b in r], chan="o_f")
            acc = stats_begin(pi)
            P.op("act", lambda e: e.activation(out=JUNK[:, 0:1], in_=EPSB[:, 0:1], func=AF.Ln),
                 reads=[C_EPS], writes=[JUNKB])
            for mo in range(KC):
                ws, wsb = next_slot("f_dn")
                wv = ws[:, 0:MC * 128].rearrange("p (m n) -> p m n", m=MC)
                for (tbi, t0, n) in tblocks(pi):
                    bk, bkb = bank()
                    for m in range(MC):
                        P.op("pe", lambda e, m=m, t0=t0, n=n, bk=bk, wv=wv: e.matmul(
                            bk[:, 0:n], lhsT=wv[:, m, :], rhs=INNER[:, m, t0:t0 + n],
                            start=(m == 0), stop=(m == MC - 1)), reads=[wsb, INB[m][tbi]], writes=[bkb])
                    stats_flush(acc, keep=0)
                    P.op("dve", lambda e, mo=mo, t0=t0, n=n, bk=bk: e.tensor_tensor(
                        out=X[:, mo, t0:t0 + n], in0=bk[:, 0:n], in1=X[:, mo, t0:t0 + n], op=ALU.add),
                        reads=[bkb, XB[mo][tbi]], writes=[XB[mo][tbi]])
                    stats_add(acc, mo, tbi, t0, n, mo == 0, mo == KC - 1)
            stats_flush(acc, keep=0)
            return acc

        def mixer_a(pi, i):
            j = i // 3
            tail = [None]
            set_rings(5, sp=len(tblocks(pi)) % 5)
            for jc in range(KC):
                ws, wsb = next_slot("a_in")
                wv = ws[:, 0:KC * 384].rearrange("p (k n) -> p k n", k=KC)
                gi = jc % 2
                st_ap = STA[:, j, jc, :]
                stb = STAB[j][jc]
                halo_in(pi, gi, st_ap, stb)
                w0, w1, w2 = (pvc(PV_AC + j * 24 + k * 8 + jc) for k in range(3))
                for (tbi, t0, n) in tblocks(pi):
                    banks = [bank(), bank(), bank("long")]
                    for k in range(KC):
                        for bi, (bk, bkb) in enumerate(banks):
                            P.op("pe", lambda e, k=k, bi=bi, t0=t0, n=n, bk=bk, wv=wv: e.matmul(
                                bk[:, 0:n], lhsT=wv[:, k, (2 - bi) * 128:(3 - bi) * 128], rhs=H[:, k, t0:t0 + n],
                                start=(k == 0), stop=(k == KC - 1)), reads=[wsb, HB[k][tbi]], writes=[bkb])
                    (pv_, pvb), (pc, pcb), (pb, pbb) = banks
                    (ta, tab), (tb_, tbb) = ttpair()
                    P.op("act", lambda e, n=n, ta=ta, pv_=pv_: e.copy(out=ta[:, 0:n], in_=pv_[:, 0:n]),
                         reads=[pvb], writes=[tab])
                    if t0 < TP:
                        gdst = G[gi][:, 2 + t0:2 + t0 + n]
                        csrc = pc[:, 0:n]
                        vsrc = ta[:, 0:n]
                    else:
                        gdst = GS[gi][:, :, 2:10]
                        csrc = pc[:, 0:n].rearrange("p (s t) -> p s t", t=DSEQ)
                        vsrc = ta[:, 0:n].rearrange("p (s t) -> p s t", t=DSEQ)
                    P.op("dve", lambda e, gdst=gdst, csrc=csrc, vsrc=vsrc: e.tensor_tensor(
                        out=gdst, in0=csrc, in1=vsrc, op=ALU.mult), reads=[pcb, tab], writes=[GB[gi]])
                    P.op("act", lambda e, gdst=gdst, ta=ta, n=n, w2=w2, t0=t0: e.activation(
                        out=(ta[:, 0:n] if t0 < TP else ta[:, 0:n].rearrange("p (s t) -> p s t", t=DSEQ)),
                        in_=gdst, func=AF.Identity, scale=w2), reads=[GB[gi]] + CONST_READ, writes=[tab])
                    if tail[0] is not None:
                        tail[0]()

                    def mk_tail(jc=jc, tbi=tbi, t0=t0, n=n, gi=gi, w0=w0, w1=w1, ta=ta, tab=tab, tb_=tb_, tbb=tbb,
                                pb=pb, pbb=pbb):
                        conv_tail(pi, tbi, t0, n, gi, w0, w1, ta, tab, tb_, tbb, CONST_READ)
                        P.op("dve", lambda e: e.tensor_tensor(out=INNER[:, jc, t0:t0 + n], in0=pb[:, 0:n],
                                                              in1=ta[:, 0:n], op=ALU.mult),
                             reads=[pbb, tab], writes=[INB[jc][tbi]])
                    tail[0] = mk_tail
                halo_out(pi, gi, st_ap, stb)
            tail[0]()
            return linear_out(pi, "a_out", INNER, INB, KC)

        def mixer_b(pi, i):
            OUTB = INNER
            wvs = []
            for h in range(2):
                ws, wsb = next_slot("b_v")
                wvs.append((ws[:, :].rearrange("p (k n) -> p k n", k=KC), wsb))
            ntiles = 9 if pi == 0 else 8
            for nt in range(ntiles):
                t0 = nt * 128
                tbi = t0 // 512
                halves = []
                for h in range(2):
                    bk, bkb = bank("short" if h == 0 else "long")
                    wv, wsb = wvs[h]
                    for k in range(KC):
                        P.op("pe", lambda e, k=k, t0=t0, bk=bk, wv=wv: e.matmul(
                            bk[:, :], lhsT=H[:, k, t0:t0 + 128], rhs=wv[:, k, :],
                            start=(k == 0), stop=(k == KC - 1)), reads=[wsb, HB[k][tbi]], writes=[bkb])
                    halves.append((bk, bkb))
                (ta, tab), (tb_, tbb) = ttpair()
                for h in range(2):
                    bk, bkb = halves[h]
                    junk, junkb = (ta, tab) if h == 0 else (tb_, tbb)
                    P.op("act", lambda e, h=h, bk=bk, junk=junk, nt=nt: e.activation(
                        out=junk[:, :], in_=bk[:, :], func=AF.Square, accum_out=SS[:, nt, h:h + 1]),
                        reads=[bkb], writes=[junkb, SSB[nt]])
                P.op("dve", lambda e, nt=nt: e.tensor_tensor(out=SS[:, nt, 2:3], in0=SS[:, nt, 0:1],
                                                             in1=SS[:, nt, 1:2], op=ALU.add),
                     reads=[SSB[nt]], writes=[SSB[nt]])
                P.op("act", lambda e, nt=nt: e.activation(out=SS[:, nt, 3:4], in_=SS[:, nt, 2:3], func=AF.Ln,
                                                          bias=EPSB[:, 0:1], scale=1.0 / D),
                     reads=[SSB[nt], C_EPS], writes=[SSB[nt]])
                P.op("act", lambda e, nt=nt: e.activation(out=SS[:, nt, 2:3], in_=SS[:, nt, 3:4], func=AF.Exp,
                                                          scale=-0.5),
                     reads=[SSB[nt]], writes=[SSB[nt]])
                for h in range(2):
                    bk, bkb = halves[h]
                    P.op("dve", lambda e, h=h, bk=bk, nt=nt: e.scalar_tensor_tensor(
                        out=VN[:, nt, h * 512:(h + 1) * 512], in0=bk[:, :], scalar=SS[:, nt, 2:3],
                        in1=GVB[:, h * 512:(h + 1) * 512], op0=ALU.mult, op1=ALU.mult),
                        reads=[bkb, SSB[nt], C_PBC], writes=[VNB[nt]])
                    if nt == 8:
                        P.op("dve", lambda e, h=h, bk=bk, nt=nt: e.scalar_tensor_tensor(
                            out=VNF[:, h * 512:(h + 1) * 512], in0=bk[:, :], scalar=SS[:, nt, 2:3],
                            in1=GVB[:, h * 512:(h + 1) * 512], op0=ALU.mult, op1=ALU.mult),
                            reads=[bkb, SSB[nt], C_PBC], writes=[VNFB])
                if nt == 8:
                    P.op("sp", lambda e: e.dma_start(out=o_cv[:, :], in_=VNF), reads=[VNFB], chan="cvout")
            for q in range(2):
                ws, wsb = next_slot("b_u")
                wv = ws[:, :].rearrange("p (k n) -> p k n", k=KC)
                for d4 in range(4):
                    dc = q * 4 + d4
                    hd = dc // 2
                    for (tbi, t0, n) in tblocks(pi):
                        bs, bsb = bank()
                        nsub = n // 128
                        for sub in range(nsub):
                            nt = t0 // 128 + sub
                            wm = WMT if t0 < TP else WMTS
                            P.op("pe", lambda e, sub=sub, nt=nt, dc=dc, hd=hd, bs=bs, wm=wm: e.matmul(
                                bs[:, sub * 128:(sub + 1) * 128], lhsT=VN[:, nt, dc * 128:(dc + 1) * 128],
                                rhs=wm[:, hd, :], start=True, stop=True),
                                reads=[VNB[nt], C_WM, C_WMS], writes=[bsb])
                        (ta, tab), _ = ttpair()
                        bbv = (BB if t0 < TP else BBS)[:, hd, :]
                        P.op("dve", lambda e, n=n, nsub=nsub, bs=bs, ta=ta, bbv=bbv: e.tensor_tensor(
                            out=ta[:, 0:n].rearrange("p (a t) -> p a t", a=nsub),
                            in0=bs[:, 0:n].rearrange("p (a t) -> p a t", a=nsub),
                            in1=bbv.unsqueeze(1).to_broadcast([128, nsub, 128]), op=ALU.add),
                            reads=[bsb, C_PBC], writes=[tab])
                        bu, bub = bank("long")
                        for k in range(KC):
                            P.op("pe", lambda e, k=k, d4=d4, t0=t0, n=n, bu=bu, wv=wv: e.matmul(
                                bu[:, 0:n], lhsT=wv[:, k, d4 * 128:(d4 + 1) * 128], rhs=H[:, k, t0:t0 + n],
                                start=(k == 0), stop=(k == KC - 1)), reads=[wsb, HB[k][tbi]], writes=[bub])
                        P.op("dve", lambda e, dc=dc, t0=t0, n=n, bu=bu, ta=ta: e.tensor_tensor(
                            out=OUTB[:, dc, t0:t0 + n], in0=bu[:, 0:n], in1=ta[:, 0:n], op=ALU.mult),
                            reads=[bub, tab], writes=[OBB[dc][tbi]])
            return linear_out(pi, "b_out", OUTB, OBB, KC)

        def mixer_c(pi, i):
            ws, wsb = next_slot("c_g")
            wg = ws[:, 0:2048].rearrange("p (g c e) -> p g c e", g=4, c=2)
            prefetch(2)
            P.op("act", lambda e: e.copy(out=HB16[:, :, 0:15], in_=HB16H[:, :, :]), reads=[HB16HB], writes=[HFH])
            if pi == 0:
                P.op("act", lambda e: e.copy(out=HFS[:, :, :, 0:15],
                                             in_=STP[:, :, 15:255].rearrange("p c (s t) -> p c s t", t=15)),
                     reads=[STPB], writes=[HFH])
                P.op("pool", lambda e: e.memset(HF30[:, :, 0:15], 0.0), writes=[F30B])
                P.op("act", lambda e: e.copy(out=HB16H[:, :, :], in_=HB16[:, :, TP:TP + 15]),
                     reads=[HFB[c][1] for c in range(KC)], writes=[HB16HB])
            for (tbi, t0, n) in tblocks(pi):
                if t0 >= TP:
                    continue
                for c in range(KC):
                    g = c // 2
                    w = WIN[g]
                    bk, bkb = bank()
                    for jj in range(w):
                        P.op("pe", lambda e, c=c, g=g, jj=jj, w=w, t0=t0, n=n, bk=bk: e.matmul(
                            bk[:, 0:n], lhsT=IDS[:, 2 * g + (0 if jj == 0 else 1), :],
                            rhs=HB16[:, c, 15 + t0 - jj:15 + t0 - jj + n], start=(jj == 0), stop=(jj == w - 1)),
                            reads=[HFB[c][tbi], HFH, C_IDS] + ([HFB[c][tbi - 1]] if tbi > 0 else []), writes=[bkb])
                    P.op("act", lambda e, c=c, t0=t0, n=n, bk=bk: e.copy(out=H[:, c, t0:t0 + n], in_=bk[:, 0:n]),
                         reads=[bkb], writes=[HB[c][tbi]])
            for c in CH_ORDER:
                g = c // 2
                w = WIN[g]
                eng = SUM_ENG[c]
                si = 0 if eng == "dve" else 1
                if pi == 0:
                    cur, curb = HF30[:, c, :], [F30B]
                    bufs = [(PSET[si][0], [PAB[si][0]]), (PSET[si][1], [PAB[si][1]])]
                    sh = 1
                    for st in range(g + 1):
                        dst, dstb = bufs[st % 2]
                        lo = 2 * sh - 1
                        P.op(eng, lambda e, dst=dst, cur=cur, lo=lo, sh=sh: e.tensor_tensor(
                            out=dst[:, lo:30], in0=cur[:, lo:30], in1=cur[:, lo - sh:30 - sh], op=ALU.add),
                            reads=curb, writes=dstb)
                        cur, curb = dst, dstb
                        sh *= 2
                    P.op(eng, lambda e, cur=cur, g=g, w=w: e.tensor_tensor(
                        out=cur[:, 15:15 + w - 1], in0=cur[:, 15:15 + w - 1], in1=CORR[:, g, 0:w - 1], op=ALU.mult),
                        reads=curb + [C_CORR], writes=curb)
                    P.op("dve", lambda e, cur=cur, c=c, w=w: e.scalar_tensor_tensor(
                        out=H[:, c, 0:15], in0=cur[:, 15:30], scalar=1.0 / w, in1=HF30[:, c, 15:30],
                        op0=ALU.mult, op1=ALU.subtract), reads=curb + [F30B], writes=[HB[c][0]])
            for c in CH_ORDER:
                g = c // 2
                w = WIN[g]
                eng = SUM_ENG[c]
                si = 0 if eng == "dve" else 1
                if pi == 0:
                    hfc = [HFB[c][2], HFH]
                    cur, curb = HFS[:, c, :, :], hfc
                    bufs = [(PSETS[si][0], [PAB[si][2]]), (PSETS[si][1], [PAB[si][3]])]
                    sh = 1
                    for st in range(g + 1):
                        dst, dstb = bufs[st % 2]
                        lo = 2 * sh - 1
                        P.op(eng, lambda e, dst=dst, cur=cur, lo=lo, sh=sh: e.tensor_tensor(
                            out=dst[:, :, lo:23], in0=cur[:, :, lo:23], in1=cur[:, :, lo - sh:23 - sh],
                            op=ALU.add), reads=curb, writes=dstb)
                        cur, curb = dst, dstb
                        sh *= 2
                    if eng == "pool":
                        P.op("pool", lambda e, cur=cur, w=w: e.tensor_scalar(
                            out=cur[:, :, 15:23], in0=cur[:, :, 15:23], scalar1=1.0 / w, scalar2=0.0,
                            op0=ALU.mult, op1=ALU.add), reads=curb, writes=curb)
                        P.op("pool", lambda e, cur=cur, c=c: e.tensor_tensor(
                            out=H[:, c, TP:TP + 128].rearrange("p (s t) -> p s t", t=DSEQ),
                            in0=cur[:, :, 15:23], in1=HFS[:, c, :, 15:23], op=ALU.subtract),
                            reads=curb + [HFB[c][2]], writes=[HB[c][2]])
                    else:
                        P.op("dve", lambda e, cur=cur, c=c, w=w: e.scalar_tensor_tensor(
                            out=H[:, c, TP:TP + 128].rearrange("p (s t) -> p s t", t=DSEQ),
                            in0=cur[:, :, 15:23], scalar=1.0 / w, in1=HFS[:, c, :, 15:23],
                            op0=ALU.mult, op1=ALU.subtract), reads=curb + [HFB[c][2]], writes=[HB[c][2]])
            if pi == 0:
                P.op("act", lambda e: e.copy(out=STP[:, :, 15:255].rearrange("p c (s t) -> p c s t", t=15),
                                             in_=HFS[:, :, :, 8:23]),
                     reads=[HFB[c][2] for c in range(KC)] + [HFH], writes=[STPB])
            acc = stats_begin(pi)
            tbs = tblocks(pi)
            order = [tbs[1], tbs[0]] + tbs[2:]
            for (tbi, t0, n) in order:
                for g in range(4):
                    bks = []
                    for eh in range(2):
                        bk, bkb = bank()
                        bks.append((bk, bkb))
                        for ch in range(2):
                            P.op("pe", lambda e, g=g, ch=ch, eh=eh, t0=t0, n=n, bk=bk: e.matmul(
                                bk[:, 0:n], lhsT=wg[:, g, ch, eh * 128:(eh + 1) * 128], rhs=H[:, 2 * g + ch, t0:t0 + n],
                                start=(ch == 0), stop=(ch == 1)), reads=[wsb, HB[2 * g + ch][tbi]], writes=[bkb])
                    stats_flush(acc)
                    for eh in range(2):
                        c = 2 * g + eh
                        bk, bkb = bks[eh]
                        P.op("dve", lambda e, c=c, t0=t0, n=n, bk=bk: e.scalar_tensor_tensor(
                            out=X[:, c, t0:t0 + n], in0=bk[:, 0:n], scalar=pvc(PV_CS + c), in1=X[:, c, t0:t0 + n],
                            op0=ALU.mult, op1=ALU.add), reads=[bkb, XB[c][tbi]] + CONST_READ, writes=[XB[c][tbi]])
                    for eh in range(2):
                        c = 2 * g + eh
                        stats_add(acc, c, tbi, t0, n, c == 0, c == KC - 1)
            stats_flush(acc, keep=0)
            return acc

        for pi in range(2):
            for (tbi, t0, n) in tblocks(pi):
                if t0 < TP:
                    src = xT[:, :, pi * TP + t0:pi * TP + t0 + n]
                else:
                    src = xsT[:, :, :]
                for c in range(KC):
                    gate = [XB[KC - 1][0]] if (pi == 0 and tbi > 0) else []
                    P.op("sp", lambda e, c=c, t0=t0, n=n, src=src: e.dma_start(out=X[:, c, t0:t0 + n], in_=src[:, c, :]),
                         reads=gate, writes=[XB[c][tbi]], chan=f"xin{c}_{tbi}")
            if pi == 0:
                prefetch(1)
            acc = None
            for i in range(DEPTH):
                kind = i % 3
                norm(pi, PV_GM + i * 8, "hf" if kind == 2 else "h", acc)
                if pi == 0 and i == 0:
                    state_loads()
                set_rings(4, sp=len(tblocks(pi)) % 4)
                if kind == 0:
                    acc = mixer_a(pi, i)
                    if pi == 0 and i == 0:
                        late_setup()
                    if pi == 1 and i == 3:
                        P.op("sp", lambda e: e.dma_start(out=o_a[:, :, :, :], in_=STA[:, :, :, :]),
                             reads=[b for r in STAB for b in r], chan="o_a")
                elif kind == 1:
                    acc = mixer_b(pi, i)
                else:
                    acc = mixer_c(pi, i)
                    if pi == 1:
                        P.op("sp", lambda e: e.dma_start(out=o_p[:, :, :], in_=STP[:, :, :]), reads=[STPB], chan="o_p")
                norm(pi, PV_GF + i * 8, "h", acc)
                acc = ffn(pi, i)
            norm(pi, PV_GFIN, "y", acc)

        esem = {e: es.enter_context(nc.semaphore(f"sem_{e}")) for e in ENGS}
        chans = sorted({o.chan for e in ENGS for o in P.q[e] if o.chan is not None})
        csem = {c: es.enter_context(nc.semaphore(f"sem_c_{c}")) for c in chans}
        block = es.enter_context(nc.Block())
        engines = {"pe": "tensor", "act": "scalar", "dve": "vector", "pool": "gpsimd", "sp": "sync"}
        P.emit(nc, block, engines, esem, csem, final_chans=["yout", "cvout", "stout"])
    return nc


_NC_CACHE = {}


def kernel(**inp):
    inp = {k: np.asarray(v) for k, v in inp.items()}
    wstream = pack_stream(inp)
    pvec = pack_pvec(inp)
    pbc = pack_pbc(inp)
    wsT = np.ascontiguousarray(inp["b_w_s"][0].transpose(2, 0, 1))
    nslots = wstream.shape[0]
    if nslots not in _NC_CACHE:
        _NC_CACHE[nslots] = build_nc(nslots)
    nc = _NC_CACHE[nslots]
    in_maps = []
    for b in range(NCORES):
        xs = inp["x_sample"][NSEQ * b:NSEQ * (b + 1)].reshape(128, D)
        sa = inp["state_shortconv"][:, NSEQ * b:NSEQ * (b + 1)]
        sf = inp["state_ffnconv"][:, NSEQ * b:NSEQ * (b + 1)]
        sp_ = inp["state_pool"][0, NSEQ * b:NSEQ * (b + 1)]
        in_maps.append({
            "xT": np.ascontiguousarray(fm(inp["x_prompt"][b]).transpose(0, 2, 1)),
            "xsT": np.ascontiguousarray(fm(xs).transpose(0, 2, 1)),
            "st_a": np.ascontiguousarray(np.pad(fm(sa.reshape(2, 32, D)).transpose(0, 1, 3, 2),
                                                ((0, 0), (0, 0), (0, 0), (2, 0)))),
            "st_f": np.ascontiguousarray(np.pad(fm(sf.reshape(DEPTH, 32, DFF)).transpose(0, 1, 3, 2),
                                                ((0, 0), (0, 0), (0, 0), (2, 0)))),
            "st_p": np.ascontiguousarray(np.pad(fm(sp_.reshape(240, D)).transpose(0, 2, 1),
                                                ((0, 0), (0, 0), (15, 0)))),
            "pvec": pvec, "pbc": pbc, "wsT": wsT, "wstream": wstream,
        })
    res = run_bass_kernel_spmd(nc, in_maps, core_ids=list(range(NCORES)))
    B = NCORES
    y_prompt = np.zeros((B, SEQ, D), np.float32)
    y_sample = np.zeros((B * NSEQ, DSEQ, D), np.float32)
    sc_p = np.zeros((2, B, 2, D), np.float32)
    sc_s = np.zeros((2, B * NSEQ, 2, D), np.float32)
    pl_p = np.zeros((1, B, 15, D), np.float32)
    pl_s = np.zeros((1, B * NSEQ, 15, D), np.float32)
    ff_p = np.zeros((DEPTH, B, 2, DFF), np.float32)
    ff_s = np.zeros((DEPTH, B * NSEQ, 2, DFF), np.float32)
    cv_s = np.zeros((1, B * NSEQ, DSEQ, D), np.float32)

    def unfm(a):
        a = np.moveaxis(a, 0, -1)
        a = np.swapaxes(a, -3, -2)
        return a.reshape(a.shape[:-2] + (-1,))

    for b in range(B):
        r = res.results[b]
        yt = unfm(r["yT"])
        y_prompt[b] = yt[:SEQ]
        y_sample[NSEQ * b:NSEQ * (b + 1)] = yt[SEQ:].reshape(NSEQ, DSEQ, D)
        oa = unfm(r["o_a"])
        sc_p[:, b] = oa[:, 0:2]
        sc_s[:, NSEQ * b:NSEQ * (b + 1)] = oa[:, 2:34].reshape(2, NSEQ, 2, D)
        of = unfm(r["o_f"])
        ff_p[:, b] = of[:, 0:2]
        ff_s[:, NSEQ * b:NSEQ * (b + 1)] = of[:, 2:34].reshape(DEPTH, NSEQ, 2, DFF)
        op_ = unfm(r["o_p"])
        pl_p[0, b] = op_[0:15]
        pl_s[0, NSEQ * b:NSEQ * (b + 1)] = op_[15:255].reshape(NSEQ, 15, D)
        cv_s[0, NSEQ * b:NSEQ * (b + 1)] = r["o_cv"].reshape(NSEQ, DSEQ, D)
    return (y_prompt, y_sample, sc_p, sc_s, pl_p, pl_s, ff_p, ff_s, cv_s)
```

```python
from contextlib import ExitStack

import numpy as np
import concourse.bass as bass
import concourse.mybir as mybir
from concourse.bass_utils import run_bass_kernel_spmd

F32 = mybir.dt.float32
BF16 = mybir.dt.bfloat16
ALU = mybir.AluOpType
AF = mybir.ActivationFunctionType

D = 1024
KC = 8
DFF = 2816
MC = 22
DEPTH = 4
EPS = 1e-6
SEQ = 2048
NSEQ = 16
DSEQ = 8
TP = 1024
T0 = TP + NSEQ * DSEQ
SLOT = 4096
NCORES = 8
WIN = (2, 4, 8, 16)
MERGE_PAIRS = False
SUM_ENG = ("dve", "dve", "dve", "dve", "dve", "pool", "pool", "pool")
CH_ORDER = (7, 6, 5, 4, 3, 2, 1, 0)

PV_GM = 0
PV_GF = 32
PV_GFIN = 64
PV_AC = 72
PV_CS = 120
PV_FC = 128
PV_FB = 392
PV_N = 480

COMPUTE = ("pe", "act", "dve", "pool")
QUEUES = ("sp",)
ENGS = ("pe", "act", "dve", "pool", "sp")


class Buf:
    __slots__ = ("name", "w", "r", "rd", "extra", "grp")

    def __init__(self, name, grp=None):
        self.name = name
        self.w = None
        self.r = {}
        self.rd = []
        self.extra = []
        self.grp = grp


class Op:
    __slots__ = ("eng", "fn", "deps", "signal", "chan", "val", "idx")

    def __init__(self, eng, fn, chan):
        self.eng = eng
        self.fn = fn
        self.chan = chan
        self.deps = []
        self.signal = False
        self.val = 0
        self.idx = 0


class Prog:
    def __init__(self):
        self.q = {e: [] for e in ENGS}
        self.active = {}
        self.grp_bufs = {}
        self.chans = {}
        self.n = 0

    def buf(self, name, grp=None):
        b = Buf(name, grp)
        if grp is not None:
            self.grp_bufs.setdefault(grp, []).append(b)
        return b

    def _activate(self, b):
        if b.grp is None:
            return
        region = b.grp[0]
        cur = self.active.get(region)
        if cur == b.grp:
            return
        if cur is not None:
            pend = {}
            for ob in self.grp_bufs[cur]:
                for o in ([ob.w] if ob.w is not None else []) + list(ob.r.values()) + ob.rd + ob.extra:
                    key = o.eng if o.chan is None else ("dma", id(o))
                    if key not in pend or pend[key].idx < o.idx:
                        pend[key] = o
            pl = list(pend.values())
            for nb in self.grp_bufs[b.grp]:
                nb.extra = list(pl)
        self.active[region] = b.grp

    def op(self, eng, fn, reads=(), writes=(), chan=None):
        o = Op(eng, fn, chan)
        o.idx = self.n
        self.n += 1
        for b in reads:
            self._activate(b)
        for b in writes:
            self._activate(b)
        deps = {}
        for b in reads:
            if b.w is not None:
                deps[id(b.w)] = b.w
            for x in b.extra:
                deps[id(x)] = x
        for b in writes:
            if b.w is not None:
                deps[id(b.w)] = b.w
            for x in b.r.values():
                deps[id(x)] = x
            for x in b.rd:
                deps[id(x)] = x
            for x in b.extra:
                deps[id(x)] = x
        for d in deps.values():
            if d is o:
                continue
            if d.chan is None and d.eng == "pe" and eng == "pe" and chan is None:
                continue
            o.deps.append(d)
            d.signal = True
        for b in reads:
            if chan is None:
                b.r[eng] = o
            else:
                b.rd.append(o)
        for b in writes:
            b.w = o
            b.r = {}
            b.rd = []
            b.extra = []
        self.q[eng].append(o)
        return o

    def emit(self, nc, block, engines, esem, csem, final_chans):
        for e in ENGS:
            cnt = 0
            for o in self.q[e]:
                if o.chan is None:
                    if o.signal:
                        cnt += 1
                    o.val = cnt
        allops = sorted((o for e in ENGS for o in self.q[e]), key=lambda o: o.idx)
        ccount = {}
        for o in allops:
            if o.chan is not None:
                ccount[o.chan] = ccount.get(o.chan, 0) + 16
                o.val = ccount[o.chan]

        know = {e: {} for e in ENGS}
        opknow = {}
        waits = {}
        for o in allops:
            ke = know[o.eng]
            wl = []
            for d in sorted(o.deps, key=lambda d: -d.idx):
                key = ("e", d.eng) if d.chan is None else ("c", d.chan)
                if ke.get(key, 0) >= d.val:
                    continue
                wl.append((key, d.val))
                ke[key] = d.val
                for k2, v2 in opknow[id(d)].items():
                    if ke.get(k2, 0) < v2:
                        ke[k2] = v2
            waits[id(o)] = wl
            if o.chan is not None or o.signal:
                opknow[id(o)] = dict(ke)

        def run(e, eng):
            for o in self.q[e]:
                for key, val in waits[id(o)]:
                    sem = esem[key[1]] if key[0] == "e" else csem[key[1]]
                    eng.wait_ge(sem, val)
                ins = o.fn(eng)
                if o.chan is not None:
                    ins.then_inc(csem[o.chan], 16)
                elif o.signal:
                    ins.then_inc(esem[e], 1)
            if e == "sp":
                for c in sorted(ccount):
                    eng.wait_ge(csem[c], ccount[c])

        for e in ENGS:
            deco = getattr(block, engines[e])
            deco(lambda eng, e=e: run(e, eng))
        self.nwaits = {e: sum(len(waits[id(o)]) for o in self.q[e]) for e in ENGS}


def _kp(w, cols):
    k = w.shape[0] // 128
    return np.ascontiguousarray(w[:, cols].reshape(k, 128, -1).transpose(1, 0, 2))


def _pad(a):
    a = a.reshape(128, -1)
    out = np.zeros((128, SLOT), np.float32)
    out[:, : a.shape[1]] = a
    return out


def slot_plan():
    plan = []
    for i in range(DEPTH):
        kind, j = i % 3, i // 3
        if kind == 0:
            for jc in range(8):
                plan.append(("a_in", i, jc, 8 * 384))
            for q in range(2):
                plan.append(("a_out", i, q, 8 * 512))
        elif kind == 1:
            for h in range(2):
                plan.append(("b_v", i, h, 8 * 512))
            for q in range(2):
                plan.append(("b_u", i, q, 8 * 512))
            for q in range(2):
                plan.append(("b_out", i, q, 8 * 512))
        else:
            plan.append(("c_g", i, 0, 2048))
        for q in range(11):
            plan.append(("f_up", i, q, 8 * 512))
        for mo in range(8):
            plan.append(("f_dn", i, mo, MC * 128))
    return plan


def pack_stream(inp):
    slots = []
    for kind, i, x, _ in slot_plan():
        j = i // 3
        if kind == "a_in":
            w = inp["a_w_in"][j]
            cols = np.concatenate([np.arange(x * 128, x * 128 + 128) + o for o in (0, 1024, 2048)])
            slots.append(_pad(_kp(w, cols)))
        elif kind == "a_out":
            slots.append(_pad(_kp(inp["a_w_out"][j], np.arange(x * 512, x * 512 + 512))))
        elif kind == "b_v":
            slots.append(_pad(_kp(inp["b_w_in"][j], np.arange(1024 + x * 512, 1024 + x * 512 + 512))))
        elif kind == "b_u":
            slots.append(_pad(_kp(inp["b_w_in"][j], np.arange(x * 512, x * 512 + 512))))
        elif kind == "b_out":
            slots.append(_pad(_kp(inp["b_w_out"][j], np.arange(x * 512, x * 512 + 512))))
        elif kind == "c_g":
            w = inp["c_w_group"][j]
            a = w.reshape(4, 2, 128, 256).transpose(2, 0, 1, 3)
            slots.append(_pad(np.ascontiguousarray(a)))
        elif kind == "f_up":
            w = inp["f_w_up"][i]
            cols = np.concatenate([
                np.arange(2 * x * 128, 2 * x * 128 + 256),
                np.arange(DFF + 2 * x * 128, DFF + 2 * x * 128 + 256),
            ])
            slots.append(_pad(_kp(w, cols)))
        elif kind == "f_dn":
            slots.append(_pad(_kp(inp["f_w_down"][i], np.arange(x * 128, x * 128 + 128))))
    return np.stack(slots)


def fm(v):
    n = v.shape[-1] // 128
    a = v.reshape(v.shape[:-1] + (n, 128))
    return np.moveaxis(a, -1, 0)


def pack_pvec(inp):
    pv = np.zeros((128, PV_N), np.float32)
    pv[:, PV_GM:PV_GM + 32] = fm(inp["g_mix"]).reshape(128, 32)
    pv[:, PV_GF:PV_GF + 32] = fm(inp["g_ffn"]).reshape(128, 32)
    pv[:, PV_GFIN:PV_GFIN + 8] = fm(inp["g_final"]).reshape(128, 8)
    pv[:, PV_AC:PV_AC + 48] = fm(inp["a_conv"]).reshape(128, 48)
    pv[:, PV_CS:PV_CS + 8] = fm(inp["c_scale"]).reshape(128, 8)
    pv[:, PV_FC:PV_FC + 264] = fm(inp["f_conv"]).reshape(128, 264)
    pv[:, PV_FB:PV_FB + 88] = fm(inp["f_conv_b"]).reshape(128, 88)
    return pv


def pack_pbc(inp):
    pb = np.zeros((128, 2048), np.float32)
    pb[:, 0:1024] = np.broadcast_to(inp["b_g_v"][0][None, :], (128, 1024))
    bias = inp["b_bias"][0]
    pb[:, 1024:1536] = np.broadcast_to(bias.reshape(1, 512), (128, 512))
    bs = np.tile(bias[:, 0:8], (1, 16))
    pb[:, 1536:2048] = np.broadcast_to(bs.reshape(1, 512), (128, 512))
    return pb


def build_nc(nslots):
    nc = bass.Bass("TRN2", target_bir_lowering=False)

    def din(name, shape):
        return nc.dram_tensor(name, list(shape), F32, kind="ExternalInput").ap()

    def dout(name, shape):
        return nc.dram_tensor(name, list(shape), F32, kind="ExternalOutput").ap()

    xT = din("xT", (128, KC, SEQ))
    xsT = din("xsT", (128, KC, 128))
    st_a = din("st_a", (128, 2, KC, 34))
    st_f = din("st_f", (128, DEPTH, MC, 34))
    st_p = din("st_p", (128, KC, 255))
    pvec = din("pvec", (128, PV_N))
    pbc = din("pbc", (128, 2048))
    wsT = din("wsT", (128, 4, 128))
    wstream = din("wstream", (nslots, 128, SLOT))
    yT = dout("yT", (128, KC, SEQ + 128))
    o_a = dout("o_a", (128, 2, KC, 34))
    o_f = dout("o_f", (128, DEPTH, MC, 34))
    o_p = dout("o_p", (128, KC, 255))
    o_cv = dout("o_cv", (128, D))

    P = Prog()
    plan = slot_plan()

    with ExitStack() as es:
        def sb(name, shape, dt=F32):
            return es.enter_context(nc.sbuf_tensor(name, list(shape), dt))

        X = sb("X", (128, KC, T0))
        H = sb("H", (128, KC, T0), BF16)
        RA = sb("RA", (128, 12672))
        RB = sb("RB", (128, 6528))
        NWS = 3
        WS = [sb(f"WS{r}", (128, SLOT), BF16) for r in range(NWS)]
        RS = [sb(f"RS{r}", (128, 512)) for r in range(3)]
        STA = sb("STA", (128, 2, KC, 34))
        STF = sb("STF", (128, DEPTH, MC, 34))
        STP = sb("STP", (128, KC, 255))
        PV = sb("PV", (128, PV_N))
        PBC = sb("PBC", (128, 2048))
        ONES = sb("ONES", (128, 128), BF16)
        CORR = sb("CORR", (128, 4, 16))
        IOT = sb("IOT", (128, 16))
        WMF = sb("WMF", (128, 4, 128))
        WMSF = sb("WMSF", (128, 4, 128))
        WMT = sb("WMT", (128, 4, 128), BF16)
        WMTS = sb("WMTS", (128, 4, 128), BF16)
        SS = sb("SS", (128, 9, 4))
        JUNK = sb("JUNK", (128, 2))
        HB16H = sb("HB16H", (128, KC, 15), BF16)
        IDS = sb("IDS", (128, 8, 128), BF16)
        IDF = sb("IDF", (128, 128))
        EPSB = sb("EPSB", (128, 1))
        PS = es.enter_context(nc.psum_tensor("PS", [128, 8, 512], F32))

        RAb = RA[:, :].bitcast(BF16)
        INNER = RAb.rearrange("p (m t) -> p m t", m=MC)
        VN = RAb[:, 8 * T0:16 * T0].rearrange("p (n f) -> p n f", f=D)
        HB16 = RA[:, 0:4156].bitcast(BF16).rearrange("p (c t) -> p c t", c=KC)
        HF30 = RA[:, 4156:4396].rearrange("p (c t) -> p c t", c=KC)
        HFS = RA[:, 4396:4396 + 8 * 368].rearrange("p (c s t) -> p c s t", c=KC, s=NSEQ)
        G = [RB[:, 0:1026], RB[:, 1186:2212]]
        GS = [RB[:, 1026:1186].rearrange("p (s t) -> p s t", t=10),
              RB[:, 2212:2372].rearrange("p (s t) -> p s t", t=10)]
        TT = [RB[:, 2372 + 512 * r:2372 + 512 * (r + 1)] for r in range(6)]
        VNF = RB[:, 5444:6468]
        PSET = [[RB[:, 1039 * (2 * si + r):1039 * (2 * si + r + 1)] for r in range(2)] for si in range(2)]
        PSETS = [[RB[:, 4156 + 368 * (2 * si + r):4156 + 368 * (2 * si + r + 1)].rearrange("p (s t) -> p s t", s=NSEQ)
                  for r in range(2)] for si in range(2)]
        GVB = PBC[:, 0:1024]
        BB = PBC[:, 1024:1536].rearrange("p (h t) -> p h t", h=4)
        BBS = PBC[:, 1536:2048].rearrange("p (h t) -> p h t", h=4)

        XB = [[P.buf(f"X{c}_{t}") for t in range(3)] for c in range(KC)]
        HB = [[P.buf(f"H{c}_{t}") for t in range(3)] for c in range(KC)]
        INB = [[P.buf(f"IN{m}_{t}", ("RA", "inner")) for t in range(3)] for m in range(MC)]
        OBB = [[P.buf(f"OB{m}_{t}", ("RA", "mixb")) for t in range(3)] for m in range(KC)]
        VNB = [P.buf(f"VN{n}", ("RA", "mixb")) for n in range(9)]
        HFB = [[P.buf(f"HF{c}_{t}", ("RA", "poolh")) for t in range(3)] for c in range(KC)]
        HFH = P.buf("HFhalo", ("RA", "poolh"))
        F30B = P.buf("HF30", ("RA", "poolh"))
        C_IDS = P.buf("c_ids")
        JUNKB = P.buf("junk")
        HB16HB = P.buf("HB16H")
        GB = [P.buf(f"G{r}", ("RB", "ffn")) for r in range(2)]
        TB = [P.buf(f"T{r}", ("RB", "ffn")) for r in range(6)]
        VNFB = P.buf("VNF", ("RB", "ffn"))
        PAB = [[P.buf(f"PA{si}_{r}", ("RB", "pool")) for r in range(4)] for si in range(2)]
        WSB = [P.buf(f"WS{r}") for r in range(NWS)]
        RSB = [P.buf(f"RS{r}") for r in range(3)]
        BANKB = [P.buf(f"BK{r}") for r in range(8)]
        STAB = [[P.buf(f"STA{j}_{c}") for c in range(KC)] for j in range(2)]
        STFB = [[P.buf(f"STF{i}_{m}") for m in range(MC)] for i in range(DEPTH)]
        STPB = P.buf("STP")
        C_PV = P.buf("c_pv")
        C_PBC = P.buf("c_pbc")
        C_WM = P.buf("c_wm")
        C_WMS = P.buf("c_wms")
        C_WMSI = [P.buf(f"c_wmsi{s}") for s in range(NSEQ)]
        C_ONES = P.buf("c_ones")
        C_CORR = P.buf("c_corr")
        C_EPS = P.buf("c_eps")
        ALLST = [b for r in STAB for b in r] + [b for r in STFB for b in r] + [STPB]
        SSB = [P.buf(f"SS{n}") for n in range(9)]

        state = {"slot": 0, "rs": 0, "tt": 0, "issued": 0, "sp": 0, "lp": 0, "dp": 0, "down": None}

        SHORT = [0, 1, 2, 3]
        LONG = [4, 5, 6, 7]

        def set_rings(nshort, sp=0):
            SHORT[:] = list(range(nshort))
            LONG[:] = list(range(nshort, 8))
            state["sp"] = sp
            state["lp"] = 0

        def bank(kind="short"):
            if state["down"] is not None:
                ring = state["down"]
                b = ring[state["dp"] % len(ring)]
                state["dp"] += 1
            elif kind == "long":
                b = LONG[state["lp"] % len(LONG)]
                state["lp"] += 1
            else:
                b = SHORT[state["sp"] % len(SHORT)]
                state["sp"] += 1
            return PS[:, b, :], BANKB[b]

        def stats_begin(pi):
            tbs = tblocks(pi)
            nst = len(tbs)
            state["down"] = SHORT[nst:] + [LONG[(state["lp"] + i) % len(LONG)] for i in range(len(LONG))]
            state["dp"] = 0
            return {"pend": [], "banks": {tbi: (PS[:, SHORT[j], :], BANKB[SHORT[j]]) for j, (tbi, _, _) in enumerate(tbs)}}

        def stats_add(acc, c, tbi, t0, n, first, last):
            P.op("act", lambda e: e.activation(out=H[:, c, t0:t0 + n], in_=X[:, c, t0:t0 + n], func=AF.Square),
                 reads=[XB[c][tbi]], writes=[HB[c][tbi]])
            bk, bkb = acc["banks"][tbi]
            acc["pend"].append(lambda: P.op("pe", lambda e: e.matmul(bk[:, 0:n], lhsT=ONES[:, :], rhs=H[:, c, t0:t0 + n],
                                                                    start=first, stop=last),
                                            reads=[HB[c][tbi], C_ONES], writes=[bkb]))

        def stats_flush(acc, keep=1):
            pend = acc["pend"]
            n = max(0, len(pend) - keep)
            for f in pend[:n]:
                f()
            acc["pend"] = pend[n:]

        def stats_end(nst):
            state["down"] = None
            state["sp"] = nst % len(SHORT)

        def rsbuf():
            r = state["rs"] % 3
            state["rs"] += 1
            return RS[r], RSB[r]

        def ttpair():
            r = state["tt"] % 3
            state["tt"] += 1
            return (TT[2 * r], TB[2 * r]), (TT[2 * r + 1], TB[2 * r + 1])

        def issue_slot(s):
            kind, _, _, used = plan[s % len(plan)]
            r = s % NWS
            src = wstream[s % len(plan), :, 0:used].rearrange("p (a b) -> p a b", a=2)
            dst = WS[r][:, 0:used].rearrange("p (a b) -> p a b", a=2)
            P.op("pool", lambda e, dst=dst, src=src: e.dma_start(out=dst, in_=src),
                 writes=[WSB[r]], chan=f"ws{r}")

        def prefetch(n):
            total = 2 * len(plan)
            while state["issued"] < min(state["slot"] + n, total):
                issue_slot(state["issued"])
                state["issued"] += 1

        def next_slot(expect_kind):
            s = state["slot"]
            state["slot"] += 1
            kind, _, _, used = plan[s % len(plan)]
            assert kind == expect_kind, (kind, expect_kind)
            while state["issued"] <= s:
                issue_slot(state["issued"])
                state["issued"] += 1
            r = s % NWS
            return WS[r], WSB[r]

        P.op("sp", lambda e: e.dma_start(out=PV[:, :], in_=pvec[:, :]), writes=[C_PV], chan="s_pv")
        P.op("pool", lambda e: e.memset(ONES[:, :], 1.0), writes=[C_ONES])
        P.op("pool", lambda e: e.memset(EPSB[:, :], EPS), writes=[C_EPS])

        def state_loads():
            P.op("sp", lambda e: e.dma_start(out=STA[:, :, :, :], in_=st_a[:, :, :, :]),
                 writes=[b for r in STAB for b in r], chan="s_sta")
            P.op("sp", lambda e: e.dma_start(out=STF[:, :, :, :], in_=st_f[:, :, :, :]),
                 reads=[HB[KC - 1][0]], writes=[b for r in STFB for b in r], chan="s_stf")
            P.op("sp", lambda e: e.dma_start(out=STP[:, :, :], in_=st_p[:, :, :]),
                 reads=[HB[KC - 1][0]], writes=[STPB], chan="s_stp")

        def late_setup():
            P.op("sp", lambda e: e.dma_start(out=PBC[:, :], in_=pbc[:, :]), writes=[C_PBC], chan="s_pbc")
            P.op("sp", lambda e: e.dma_start(out=WMF[:, :, :], in_=wsT[:, :, :]), writes=[C_WM], chan="s_wm")
            P.op("pool", lambda e: e.memset(WMSF[:, :, :], 0.0), writes=C_WMSI)
            for s in range(NSEQ):
                P.op("sp", lambda e, s=s: e.dma_start(out=WMSF[8 * s:8 * s + 8, :, 8 * s:8 * s + 8],
                                                      in_=wsT[0:8, :, 0:8]), writes=[C_WMSI[s]], chan=f"s_wms{s}")
            P.op("pool", lambda e: e.affine_select(out=WMT[:, :, :], in_=WMF[:, :, :], pattern=[[0, 4], [1, 128]],
                                                   compare_op=ALU.is_ge, fill=0.0, base=0, channel_multiplier=-1),
                 reads=[C_WM], writes=[C_WM])
            P.op("pool", lambda e: e.affine_select(out=WMTS[:, :, :], in_=WMSF[:, :, :], pattern=[[0, 4], [1, 128]],
                                                   compare_op=ALU.is_ge, fill=0.0, base=0, channel_multiplier=-1),
                 reads=C_WMSI, writes=[C_WMS])
            P.op("pool", lambda e: e.memset(HB16H[:, :, :], 0.0), writes=[HB16HB])
            for g in range(4):
                for r, val in enumerate((1.0 / WIN[g] - 1.0, 1.0 / WIN[g])):
                    P.op("pool", lambda e, val=val: e.memset(IDF[:, :], val), reads=[C_IDS], writes=[C_IDS])
                    P.op("pool", lambda e, g=g, r=r: e.affine_select(
                        out=IDS[:, 2 * g + r, :], in_=IDF[:, :], pattern=[[1, 128]], compare_op=ALU.is_equal,
                        fill=0.0, base=0, channel_multiplier=-1), reads=[C_IDS], writes=[C_IDS])
            P.op("pool", lambda e: e.iota(IOT[:, :], [[1, 16]], base=1, channel_multiplier=0,
                                          allow_small_or_imprecise_dtypes=True), writes=[C_CORR])
            P.op("dve", lambda e: e.reciprocal(out=IOT[:, :], in_=IOT[:, :]), reads=[C_CORR], writes=[C_CORR])
            for g in range(4):
                P.op("dve", lambda e, g=g: e.tensor_scalar(out=CORR[:, g, :], in0=IOT[:, :], scalar1=float(WIN[g]),
                                                           scalar2=None, op0=ALU.mult), reads=[C_CORR], writes=[C_CORR])

        CONST_READ = [C_PV]

        def pvc(off):
            return PV[:, off:off + 1]

        def tblocks(pi):
            t = [(0, 0, 512), (1, 512, 512)]
            if pi == 0:
                t.append((2, 1024, 128))
            return t

        def emit_xload(pi, tbi, t0, n):
            if t0 < TP:
                src = xT[:, :, pi * TP + t0:pi * TP + t0 + n]
            else:
                src = xsT[:, :, :]
            for c in range(KC):
                gate = [XB[KC - 1][0]] if (pi == 0 and tbi > 0) else []
                P.op("sp", lambda e, c=c, src=src: e.dma_start(out=X[:, c, t0:t0 + n], in_=src[:, c, :]),
                     reads=gate, writes=[XB[c][tbi]], chan=f"xin{c}_{tbi}")

        def norm(pi, goff, kind, acc=None):
            for (tbi, t0, n) in tblocks(pi):
                hb = [HB[c][tbi] for c in range(KC)]
                xb = [XB[c][tbi] for c in range(KC)]
                if acc is not None:
                    bk, bkb = acc["banks"][tbi]
                else:
                    bk, bkb = bank()
                    for c in range(KC):
                        P.op("act", lambda e, c=c, t0=t0, n=n: e.activation(out=H[:, c, t0:t0 + n],
                                                                            in_=X[:, c, t0:t0 + n], func=AF.Square),
                             reads=[xb[c]], writes=[hb[c]])
                    for c in range(KC):
                        P.op("pe", lambda e, c=c, t0=t0, n=n, bk=bk: e.matmul(bk[:, 0:n], lhsT=ONES[:, :],
                                                                             rhs=H[:, c, t0:t0 + n],
                                                                             start=(c == 0), stop=(c == KC - 1)),
                             reads=[hb[c], C_ONES], writes=[bkb])
                rs, rsb = rsbuf()
                P.op("act", lambda e, n=n, bk=bk, rs=rs: e.activation(out=rs[:, 0:n], in_=bk[:, 0:n], func=AF.Ln,
                                                                      bias=EPSB[:, 0:1], scale=1.0 / D),
                     reads=[bkb, C_EPS], writes=[rsb])
                P.op("act", lambda e, n=n, rs=rs: e.activation(out=rs[:, 0:n], in_=rs[:, 0:n], func=AF.Exp,
                                                               scale=-0.5),
                     reads=[rsb], writes=[rsb])
                if kind == "y":
                    col = pi * TP + t0 if t0 < TP else SEQ + (t0 - TP)
                    for c in range(KC):
                        P.op("dve", lambda e, c=c, t0=t0, n=n, rs=rs: e.scalar_tensor_tensor(
                            out=X[:, c, t0:t0 + n], in0=X[:, c, t0:t0 + n], scalar=pvc(goff + c), in1=rs[:, 0:n],
                            op0=ALU.mult, op1=ALU.mult), reads=[xb[c], rsb] + CONST_READ, writes=[xb[c]])
                        P.op("sp", lambda e, c=c, t0=t0, n=n, col=col: e.dma_start(out=yT[:, c, col:col + n],
                                                                                 in_=X[:, c, t0:t0 + n]),
                             reads=[xb[c]], chan=f"y{c}_{tbi}")
                    if pi == 0 and t0 < TP:
                        emit_xload(1, tbi, t0, n)
                elif kind == "hf":
                    for c in range(KC):
                        if t0 < TP:
                            dst = HB16[:, c, 15 + t0:15 + t0 + n]
                            i1 = rs[:, 0:n]
                            i0 = X[:, c, t0:t0 + n]
                        else:
                            dst = HFS[:, c, :, 15:23]
                            i1 = rs[:, 0:n].rearrange("p (s t) -> p s t", t=DSEQ)
                            i0 = X[:, c, t0:t0 + n].rearrange("p (s t) -> p s t", t=DSEQ)
                        P.op("dve", lambda e, c=c, dst=dst, i0=i0, i1=i1: e.scalar_tensor_tensor(
                            out=dst, in0=i0, scalar=pvc(goff + c), in1=i1, op0=ALU.mult, op1=ALU.mult),
                            reads=[xb[c], rsb] + CONST_READ, writes=[HFB[c][tbi]])
                    if pi == 0 and t0 == 0:
                        for c in range(KC):
                            P.op("dve", lambda e, c=c, rs=rs: e.scalar_tensor_tensor(
                                out=HF30[:, c, 15:30], in0=X[:, c, 0:15], scalar=pvc(goff + c), in1=rs[:, 0:15],
                                op0=ALU.mult, op1=ALU.mult), reads=[xb[c], rsb] + CONST_READ, writes=[F30B])
                    if pi == 1 and t0 == TP - 512:
                        for c in range(KC):
                            P.op("dve", lambda e, c=c, rs=rs: e.scalar_tensor_tensor(
                                out=STP[:, c, 0:15], in0=X[:, c, TP - 15:TP], scalar=pvc(goff + c),
                                in1=rs[:, 512 - 15:512], op0=ALU.mult, op1=ALU.mult),
                                reads=[xb[c], rsb] + CONST_READ, writes=[STPB])
                else:
                    for c in range(KC):
                        P.op("dve", lambda e, c=c, t0=t0, n=n, rs=rs: e.scalar_tensor_tensor(
                            out=H[:, c, t0:t0 + n], in0=X[:, c, t0:t0 + n], scalar=pvc(goff + c), in1=rs[:, 0:n],
                            op0=ALU.mult, op1=ALU.mult), reads=[xb[c], rsb] + CONST_READ, writes=[hb[c]])
            if acc is not None:
                stats_end(len(tblocks(pi)))

        def linear_out(pi, kind, src, srcb, nk, resid_scale_off=None):
            acc = stats_begin(pi)
            for q in range(2):
                ws, wsb = next_slot(kind)
                wv = ws[:, :].rearrange("p (k n) -> p k n", k=nk)
                for mo4 in range(4):
                    mo = q * 4 + mo4
                    for (tbi, t0, n) in tblocks(pi):
                        bk, bkb = bank()
                        for k in range(nk):
                            P.op("pe", lambda e, k=k, mo4=mo4, t0=t0, n=n, bk=bk, wv=wv: e.matmul(
                                bk[:, 0:n], lhsT=wv[:, k, mo4 * 128:(mo4 + 1) * 128], rhs=src[:, k, t0:t0 + n],
                                start=(k == 0), stop=(k == nk - 1)), reads=[wsb, srcb[k][tbi]], writes=[bkb])
                        stats_flush(acc)
                        P.op("dve", lambda e, mo=mo, t0=t0, n=n, bk=bk: e.tensor_tensor(
                            out=X[:, mo, t0:t0 + n], in0=bk[:, 0:n], in1=X[:, mo, t0:t0 + n], op=ALU.add),
                            reads=[bkb, XB[mo][tbi]], writes=[XB[mo][tbi]])
                        stats_add(acc, mo, tbi, t0, n, mo == 0, mo == KC - 1)
            stats_flush(acc, keep=0)
            return acc

        def conv_tail(pi, tbi, t0, n, gi, w0, w1, ta, tab, tbb_ap, tbbb, extra_reads):
            if t0 < TP:
                g1 = G[gi][:, 1 + t0:1 + t0 + n]
                g0 = G[gi][:, t0:t0 + n]
                a_ = ta[:, 0:n]
                b_ = tbb_ap[:, 0:n]
            else:
                g1 = GS[gi][:, :, 1:9]
                g0 = GS[gi][:, :, 0:8]
                a_ = ta[:, 0:n].rearrange("p (s t) -> p s t", t=DSEQ)
                b_ = tbb_ap[:, 0:n].rearrange("p (s t) -> p s t", t=DSEQ)
            P.op("dve", lambda e: e.scalar_tensor_tensor(out=b_, in0=g1, scalar=w1, in1=a_, op0=ALU.mult, op1=ALU.add),
                 reads=[GB[gi], tab] + extra_reads, writes=[tbbb])
            P.op("dve", lambda e: e.scalar_tensor_tensor(out=a_, in0=g0, scalar=w0, in1=b_, op0=ALU.mult, op1=ALU.add),
                 reads=[GB[gi], tbbb] + extra_reads, writes=[tab])

        def halo_in(pi, gi, st_ap, stb):
            P.op("act", lambda e: e.copy(out=G[gi][:, 0:2], in_=st_ap[:, 0:2]), reads=[stb], writes=[GB[gi]])
            if pi == 0:
                P.op("act", lambda e: e.copy(out=GS[gi][:, :, 0:2],
                                             in_=st_ap[:, 2:34].rearrange("p (s t) -> p s t", t=2)),
                     reads=[stb], writes=[GB[gi]])

        def halo_out(pi, gi, st_ap, stb):
            P.op("act", lambda e: e.copy(out=st_ap[:, 0:2], in_=G[gi][:, TP:TP + 2]), reads=[GB[gi]], writes=[stb])
            if pi == 0:
                P.op("act", lambda e: e.copy(out=st_ap[:, 2:34].rearrange("p (s t) -> p s t", t=2),
                                             in_=GS[gi][:, :, 8:10]), reads=[GB[gi]], writes=[stb])

        def ffn(pi, i):
            tail = [None]

            def unit(ws_v, wsb, ms, tbi, t0, n):
                bks = []
                for (m, mm, gi) in ms:
                    bks.append((bank(), bank("long")))
                for k in range(KC):
                    for (m, mm, gi), ((bg, bgb), (ba, bab)) in zip(ms, bks):
                        P.op("pe", lambda e, k=k, mm=mm, bg=bg: e.matmul(
                            bg[:, 0:n], lhsT=ws_v[:, k, mm * 128:(mm + 1) * 128], rhs=H[:, k, t0:t0 + n],
                            start=(k == 0), stop=(k == KC - 1)), reads=[wsb, HB[k][tbi]], writes=[bgb])
                        P.op("pe", lambda e, k=k, mm=mm, ba=ba: e.matmul(
                            ba[:, 0:n], lhsT=ws_v[:, k, 256 + mm * 128:256 + (mm + 1) * 128], rhs=H[:, k, t0:t0 + n],
                            start=(k == 0), stop=(k == KC - 1)), reads=[wsb, HB[k][tbi]], writes=[bab])
                for (m, mm, gi), ((bg, bgb), (ba, bab)) in zip(ms, bks):
                    w0, w1, w2 = (pvc(PV_FC + i * 66 + k * 22 + m) for k in range(3))
                    bcol = pvc(PV_FB + i * 22 + m)
                    (ta, tab), (tb_, tbb) = ttpair()
                    if t0 < TP:
                        gdst = G[gi][:, 2 + t0:2 + t0 + n]
                        gsrc = bg[:, 0:n]
                    else:
                        gdst = GS[gi][:, :, 2:10]
                        gsrc = bg[:, 0:n].rearrange("p (s t) -> p s t", t=DSEQ)
                    P.op("act", lambda e, gdst=gdst, gsrc=gsrc: e.copy(out=gdst, in_=gsrc),
                         reads=[bgb], writes=[GB[gi]])
                    P.op("act", lambda e, bg=bg, ta=ta, w2=w2, bcol=bcol: e.activation(
                        out=ta[:, 0:n], in_=bg[:, 0:n], func=AF.Identity, bias=bcol, scale=w2),
                        reads=[bgb] + CONST_READ, writes=[tab])
                    conv_tail(pi, tbi, t0, n, gi, w0, w1, ta, tab, tb_, tbb, CONST_READ)
                    if tail[0] is not None:
                        tail[0]()

                    def mk_tail(m=m, ta=ta, tab=tab, tb_=tb_, tbb=tbb, ba=ba, bab=bab):
                        P.op("act", lambda e: e.activation(out=tb_[:, 0:n], in_=ta[:, 0:n], func=AF.Silu),
                             reads=[tab], writes=[tbb])
                        P.op("dve", lambda e: e.tensor_tensor(out=INNER[:, m, t0:t0 + n], in0=ba[:, 0:n],
                                                              in1=tb_[:, 0:n], op=ALU.mult),
                             reads=[bab, tbb], writes=[INB[m][tbi]])
                    tail[0] = mk_tail

            set_rings(3)
            for q in range(11):
                ws, wsb = next_slot("f_up")
                wv = ws[:, :].rearrange("p (k n) -> p k n", k=KC)
                ms = [(2 * q + mm, mm, mm) for mm in range(2)]
                if MERGE_PAIRS or q == 0:
                    for (m, mm, gi) in ms:
                        halo_in(pi, gi, STF[:, i, m, :], STFB[i][m])
                    for (tbi, t0, n) in tblocks(pi):
                        unit(wv, wsb, ms, tbi, t0, n)
                    for (m, mm, gi) in ms:
                        halo_out(pi, gi, STF[:, i, m, :], STFB[i][m])
                else:
                    for (m, mm, gi) in ms:
                        halo_in(pi, gi, STF[:, i, m, :], STFB[i][m])
                        for (tbi, t0, n) in tblocks(pi):
                            unit(wv, wsb, [(m, mm, gi)], tbi, t0, n)
                        halo_out(pi, gi, STF[:, i, m, :], STFB[i][m])
            tail[0]()
            if pi == 1 and i == DEPTH - 1:
                P.op("sp", lambda e: e.dma_start(out=o_f[:, :, :, :], in_=STF[:, :, :, :]),
                     reads=[b for r in STFB for b in r], chan="o_f")
            acc = stats_begin(pi)
            P.op("act", lambda e: e.activation(out=JUNK[:, 0:1], in_=EPSB[:, 0:1], func=AF.Ln),
                 reads=[C_EPS], writes=[JUNKB])
            for mo in range(KC):
                ws, wsb = next_slot("f_dn")
                wv = ws[:, 0:MC * 128].rearrange("p (m n) -> p m n", m=MC)
                for (tbi, t0, n) in tblocks(pi):
                    bk, bkb = bank()
                    for m in range(MC):
                        P.op("pe", lambda e, m=m, t0=t0, n=n, bk=bk, wv=wv: e.matmul(
                            bk[:, 0:n], lhsT=wv[:, m, :], rhs=INNER[:, m, t0:t0 + n],
                            start=(m == 0), stop=(m == MC - 1)), reads=[wsb, INB[m][tbi]], writes=[bkb])
                    stats_flush(acc, keep=0)
                    P.op("dve", lambda e, mo=mo, t0=t0, n=n, bk=bk: e.tensor_tensor(
                        out=X[:, mo, t0:t0 + n], in0=bk[:, 0:n], in1=X[:, mo, t0:t0 + n], op=ALU.add),
                        reads=[bkb, XB[mo][tbi]], writes=[XB[mo][tbi]])
                    stats_add(acc, mo, tbi, t0, n, mo == 0, mo == KC - 1)
            stats_flush(acc, keep=0)
            return acc

        def mixer_a(pi, i):
            j = i // 3
            tail = [None]
            set_rings(5, sp=len(tblocks(pi)) % 5)
            for jc in range(KC):
                ws, wsb = next_slot("a_in")
                wv = ws[:, 0:KC * 384].rearrange("p (k n) -> p k n", k=KC)
                gi = jc % 2
                st_ap = STA[:, j, jc, :]
                stb = STAB[j][jc]
                halo_in(pi, gi, st_ap, stb)
                w0, w1, w2 = (pvc(PV_AC + j * 24 + k * 8 + jc) for k in range(3))
                for (tbi, t0, n) in tblocks(pi):
                    banks = [bank(), bank(), bank("long")]
                    for k in range(KC):
                        for bi, (bk, bkb) in enumerate(banks):
                            P.op("pe", lambda e, k=k, bi=bi, t0=t0, n=n, bk=bk, wv=wv: e.matmul(
                                bk[:, 0:n], lhsT=wv[:, k, (2 - bi) * 128:(3 - bi) * 128], rhs=H[:, k, t0:t0 + n],
                                start=(k == 0), stop=(k == KC - 1)), reads=[wsb, HB[k][tbi]], writes=[bkb])
                    (pv_, pvb), (pc, pcb), (pb, pbb) = banks
                    (ta, tab), (tb_, tbb) = ttpair()
                    P.op("act", lambda e, n=n, ta=ta, pv_=pv_: e.copy(out=ta[:, 0:n], in_=pv_[:, 0:n]),
                         reads=[pvb], writes=[tab])
                    if t0 < TP:
                        gdst = G[gi][:, 2 + t0:2 + t0 + n]
                        csrc = pc[:, 0:n]
                        vsrc = ta[:, 0:n]
                    else:
                        gdst = GS[gi][:, :, 2:10]
                        csrc = pc[:, 0:n].rearrange("p (s t) -> p s t", t=DSEQ)
                        vsrc = ta[:, 0:n].rearrange("p (s t) -> p s t", t=DSEQ)
                    P.op("dve", lambda e, gdst=gdst, csrc=csrc, vsrc=vsrc: e.tensor_tensor(
                        out=gdst, in0=csrc, in1=vsrc, op=ALU.mult), reads=[pcb, tab], writes=[GB[gi]])
                    P.op("act", lambda e, gdst=gdst, ta=ta, n=n, w2=w2, t0=t0: e.activation(
                        out=(ta[:, 0:n] if t0 < TP else ta[:, 0:n].rearrange("p (s t) -> p s t", t=DSEQ)),
                        in_=gdst, func=AF.Identity, scale=w2), reads=[GB[gi]] + CONST_READ, writes=[tab])
                    if tail[0] is not None:
                        tail[0]()

                    def mk_tail(jc=jc, tbi=tbi, t0=t0, n=n, gi=gi, w0=w0, w1=w1, ta=ta, tab=tab, tb_=tb_, tbb=tbb,
                                pb=pb, pbb=pbb):
                        conv_tail(pi, tbi, t0, n, gi, w0, w1, ta, tab, tb_, tbb, CONST_READ)
                        P.op("dve", lambda e: e.tensor_tensor(out=INNER[:, jc, t0:t0 + n], in0=pb[:, 0:n],
                                                              in1=ta[:, 0:n], op=ALU.mult),
                             reads=[pbb, tab], writes=[INB[jc][tbi]])
                    tail[0] = mk_tail
                halo_out(pi, gi, st_ap, stb)
            tail[0]()
            return linear_out(pi, "a_out", INNER, INB, KC)

        def mixer_b(pi, i):
            OUTB = INNER
            wvs = []
            for h in range(2):
                ws, wsb = next_slot("b_v")
                wvs.append((ws[:, :].rearrange("p (k n) -> p k n", k=KC), wsb))
            ntiles = 9 if pi == 0 else 8
            for nt in range(ntiles):
                t0 = nt * 128
                tbi = t0 // 512
                halves = []
                for h in range(2):
                    bk, bkb = bank("short" if h == 0 else "long")
                    wv, wsb = wvs[h]
                    for k in range(KC):
                        P.op("pe", lambda e, k=k, t0=t0, bk=bk, wv=wv: e.matmul(
                            bk[:, :], lhsT=H[:, k, t0:t0 + 128], rhs=wv[:, k, :],
                            start=(k == 0), stop=(k == KC - 1)), reads=[wsb, HB[k][tbi]], writes=[bkb])
                    halves.append((bk, bkb))
                (ta, tab), (tb_, tbb) = ttpair()
                for h in range(2):
                    bk, bkb = halves[h]
                    junk, junkb = (ta, tab) if h == 0 else (tb_, tbb)
                    P.op("act", lambda e, h=h, bk=bk, junk=junk, nt=nt: e.activation(
                        out=junk[:, :], in_=bk[:, :], func=AF.Square, accum_out=SS[:, nt, h:h + 1]),
                        reads=[bkb], writes=[junkb, SSB[nt]])
                P.op("dve", lambda e, nt=nt: e.tensor_tensor(out=SS[:, nt, 2:3], in0=SS[:, nt, 0:1],
                                                             in1=SS[:, nt, 1:2], op=ALU.add),
                     reads=[SSB[nt]], writes=[SSB[nt]])
                P.op("act", lambda e, nt=nt: e.activation(out=SS[:, nt, 3:4], in_=SS[:, nt, 2:3], func=AF.Ln,
                                                          bias=EPSB[:, 0:1], scale=1.0 / D),
                     reads=[SSB[nt], C_EPS], writes=[SSB[nt]])
                P.op("act", lambda e, nt=nt: e.activation(out=SS[:, nt, 2:3], in_=SS[:, nt, 3:4], func=AF.Exp,
                                                          scale=-0.5),
                     reads=[SSB[nt]], writes=[SSB[nt]])
                for h in range(2):
                    bk, bkb = halves[h]
                    P.op("dve", lambda e, h=h, bk=bk, nt=nt: e.scalar_tensor_tensor(
                        out=VN[:, nt, h * 512:(h + 1) * 512], in0=bk[:, :], scalar=SS[:, nt, 2:3],
                        in1=GVB[:, h * 512:(h + 1) * 512], op0=ALU.mult, op1=ALU.mult),
                        reads=[bkb, SSB[nt], C_PBC], writes=[VNB[nt]])
                    if nt == 8:
                        P.op("dve", lambda e, h=h, bk=bk, nt=nt: e.scalar_tensor_tensor(
                            out=VNF[:, h * 512:(h + 1) * 512], in0=bk[:, :], scalar=SS[:, nt, 2:3],
                            in1=GVB[:, h * 512:(h + 1) * 512], op0=ALU.mult, op1=ALU.mult),
                            reads=[bkb, SSB[nt], C_PBC], writes=[VNFB])
                if nt == 8:
                    P.op("sp", lambda e: e.dma_start(out=o_cv[:, :], in_=VNF), reads=[VNFB], chan="cvout")
            for q in range(2):
                ws, wsb = next_slot("b_u")
                wv = ws[:, :].rearrange("p (k n) -> p k n", k=KC)
                for d4 in range(4):
                    dc = q * 4 + d4
                    hd = dc // 2
                    for (tbi, t0, n) in tblocks(pi):
                        bs, bsb = bank()
                        nsub = n // 128
                        for sub in range(nsub):
                            nt = t0 // 128 + sub
                            wm = WMT if t0 < TP else WMTS
                            P.op("pe", lambda e, sub=sub, nt=nt, dc=dc, hd=hd, bs=bs, wm=wm: e.matmul(
                                bs[:, sub * 128:(sub + 1) * 128], lhsT=VN[:, nt, dc * 128:(dc + 1) * 128],
                                rhs=wm[:, hd, :], start=True, stop=True),
                                reads=[VNB[nt], C_WM, C_WMS], writes=[bsb])
                        (ta, tab), _ = ttpair()
                        bbv = (BB if t0 < TP else BBS)[:, hd, :]
                        P.op("dve", lambda e, n=n, nsub=nsub, bs=bs, ta=ta, bbv=bbv: e.tensor_tensor(
                            out=ta[:, 0:n].rearrange("p (a t) -> p a t", a=nsub),
                            in0=bs[:, 0:n].rearrange("p (a t) -> p a t", a=nsub),
                            in1=bbv.unsqueeze(1).to_broadcast([128, nsub, 128]), op=ALU.add),
                            reads=[bsb, C_PBC], writes=[tab])
                        bu, bub = bank("long")
                        for k in range(KC):
                            P.op("pe", lambda e, k=k, d4=d4, t0=t0, n=n, bu=bu, wv=wv: e.matmul(
                                bu[:, 0:n], lhsT=wv[:, k, d4 * 128:(d4 + 1) * 128], rhs=H[:, k, t0:t0 + n],
                                start=(k == 0), stop=(k == KC - 1)), reads=[wsb, HB[k][tbi]], writes=[bub])
                        P.op("dve", lambda e, dc=dc, t0=t0, n=n, bu=bu, ta=ta: e.tensor_tensor(
                            out=OUTB[:, dc, t0:t0 + n], in0=bu[:, 0:n], in1=ta[:, 0:n], op=ALU.mult),
                            reads=[bub, tab], writes=[OBB[dc][tbi]])
            return linear_out(pi, "b_out", OUTB, OBB, KC)

        def mixer_c(pi, i):
            ws, wsb = next_slot("c_g")
            wg = ws[:, 0:2048].rearrange("p (g c e) -> p g c e", g=4, c=2)
            prefetch(2)
            P.op("act", lambda e: e.copy(out=HB16[:, :, 0:15], in_=HB16H[:, :, :]), reads=[HB16HB], writes=[HFH])
            if pi == 0:
                P.op("act", lambda e: e.copy(out=HFS[:, :, :, 0:15],
                                             in_=STP[:, :, 15:255].rearrange("p c (s t) -> p c s t", t=15)),
                     reads=[STPB], writes=[HFH])
                P.op("pool", lambda e: e.memset(HF30[:, :, 0:15], 0.0), writes=[F30B])
                P.op("act", lambda e: e.copy(out=HB16H[:, :, :], in_=HB16[:, :, TP:TP + 15]),
                     reads=[HFB[c][1] for c in range(KC)], writes=[HB16HB])
            for (tbi, t0, n) in tblocks(pi):
                if t0 >= TP:
                    continue
                for c in range(KC):
                    g = c // 2
                    w = WIN[g]
                    bk, bkb = bank()
                    for jj in range(w):
                        P.op("pe", lambda e, c=c, g=g, jj=jj, w=w, t0=t0, n=n, bk=bk: e.matmul(
                            bk[:, 0:n], lhsT=IDS[:, 2 * g + (0 if jj == 0 else 1), :],
                            rhs=HB16[:, c, 15 + t0 - jj:15 + t0 - jj + n], start=(jj == 0), stop=(jj == w - 1)),
                            reads=[HFB[c][tbi], HFH, C_IDS] + ([HFB[c][tbi - 1]] if tbi > 0 else []), writes=[bkb])
                    P.op("act", lambda e, c=c, t0=t0, n=n, bk=bk: e.copy(out=H[:, c, t0:t0 + n], in_=bk[:, 0:n]),
                         reads=[bkb], writes=[HB[c][tbi]])
            for c in CH_ORDER:
                g = c // 2
                w = WIN[g]
                eng = SUM_ENG[c]
                si = 0 if eng == "dve" else 1
                if pi == 0:
                    cur, curb = HF30[:, c, :], [F30B]
                    bufs = [(PSET[si][0], [PAB[si][0]]), (PSET[si][1], [PAB[si][1]])]
                    sh = 1
                    for st in range(g + 1):
                        dst, dstb = bufs[st % 2]
                        lo = 2 * sh - 1
                        P.op(eng, lambda e, dst=dst, cur=cur, lo=lo, sh=sh: e.tensor_tensor(
                            out=dst[:, lo:30], in0=cur[:, lo:30], in1=cur[:, lo - sh:30 - sh], op=ALU.add),
                            reads=curb, writes=dstb)
                        cur, curb = dst, dstb
                        sh *= 2
                    P.op(eng, lambda e, cur=cur, g=g, w=w: e.tensor_tensor(
                        out=cur[:, 15:15 + w - 1], in0=cur[:, 15:15 + w - 1], in1=CORR[:, g, 0:w - 1], op=ALU.mult),
                        reads=curb + [C_CORR], writes=curb)
                    P.op("dve", lambda e, cur=cur, c=c, w=w: e.scalar_tensor_tensor(
                        out=H[:, c, 0:15], in0=cur[:, 15:30], scalar=1.0 / w, in1=HF30[:, c, 15:30],
                        op0=ALU.mult, op1=ALU.subtract), reads=curb + [F30B], writes=[HB[c][0]])
            for c in CH_ORDER:
                g = c // 2
                w = WIN[g]
                eng = SUM_ENG[c]
                si = 0 if eng == "dve" else 1
                if pi == 0:
                    hfc = [HFB[c][2], HFH]
                    cur, curb = HFS[:, c, :, :], hfc
                    bufs = [(PSETS[si][0], [PAB[si][2]]), (PSETS[si][1], [PAB[si][3]])]
                    sh = 1
                    for st in range(g + 1):
                        dst, dstb = bufs[st % 2]
                        lo = 2 * sh - 1
                        P.op(eng, lambda e, dst=dst, cur=cur, lo=lo, sh=sh: e.tensor_tensor(
                            out=dst[:, :, lo:23], in0=cur[:, :, lo:23], in1=cur[:, :, lo - sh:23 - sh],
                            op=ALU.add), reads=curb, writes=dstb)
                        cur, curb = dst, dstb
                        sh *= 2
                    if eng == "pool":
                        P.op("pool", lambda e, cur=cur, w=w: e.tensor_scalar(
                            out=cur[:, :, 15:23], in0=cur[:, :, 15:23], scalar1=1.0 / w, scalar2=0.0,
                            op0=ALU.mult, op1=ALU.add), reads=curb, writes=curb)
                        P.op("pool", lambda e, cur=cur, c=c: e.tensor_tensor(
                            out=H[:, c, TP:TP + 128].rearrange("p (s t) -> p s t", t=DSEQ),
                            in0=cur[:, :, 15:23], in1=HFS[:, c, :, 15:23], op=ALU.subtract),
                            reads=curb + [HFB[c][2]], writes=[HB[c][2]])
                    else:
                        P.op("dve", lambda e, cur=cur, c=c, w=w: e.scalar_tensor_tensor(
                            out=H[:, c, TP:TP + 128].rearrange("p (s t) -> p s t", t=DSEQ),
                            in0=cur[:, :, 15:23], scalar=1.0 / w, in1=HFS[:, c, :, 15:23],
                            op0=ALU.mult, op1=ALU.subtract), reads=curb + [HFB[c][2]], writes=[HB[c][2]])
            if pi == 0:
                P.op("act", lambda e: e.copy(out=STP[:, :, 15:255].rearrange("p c (s t) -> p c s t", t=15),
                                             in_=HFS[:, :, :, 8:23]),
                     reads=[HFB[c][2] for c in range(KC)] + [HFH], writes=[STPB])
            acc = stats_begin(pi)
            tbs = tblocks(pi)
            order = [tbs[1], tbs[0]] + tbs[2:]
            for (tbi, t0, n) in order:
                for g in range(4):
                    bks = []
                    for eh in range(2):
                        bk, bkb = bank()
                        bks.append((bk, bkb))
                        for ch in range(2):
                            P.op("pe", lambda e, g=g, ch=ch, eh=eh, t0=t0, n=n, bk=bk: e.matmul(
                                bk[:, 0:n], lhsT=wg[:, g, ch, eh * 128:(eh + 1) * 128], rhs=H[:, 2 * g + ch, t0:t0 + n],
                                start=(ch == 0), stop=(ch == 1)), reads=[wsb, HB[2 * g + ch][tbi]], writes=[bkb])
                    stats_flush(acc)
                    for eh in range(2):
                        c = 2 * g + eh
                        bk, bkb = bks[eh]
                        P.op("dve", lambda e, c=c, t0=t0, n=n, bk=bk: e.scalar_tensor_tensor(
                            out=X[:, c, t0:t0 + n], in0=bk[:, 0:n], scalar=pvc(PV_CS + c), in1=X[:, c, t0:t0 + n],
                            op0=ALU.mult, op1=ALU.add), reads=[bkb, XB[c][tbi]] + CONST_READ, writes=[XB[c][tbi]])
                    for eh in range(2):
                        c = 2 * g + eh
                        stats_add(acc, c, tbi, t0, n, c == 0, c == KC - 1)
            stats_flush(acc, keep=0)
            return acc

        for pi in range(2):
            if pi == 0:
                for (tbi, t0, n) in tblocks(pi):
                    emit_xload(pi, tbi, t0, n)
            if pi == 0:
                prefetch(1)
            acc = None
            for i in range(DEPTH):
                kind = i % 3
                norm(pi, PV_GM + i * 8, "hf" if kind == 2 else "h", acc)
                if pi == 0 and i == 0:
                    state_loads()
                set_rings(4, sp=len(tblocks(pi)) % 4)
                if kind == 0:
                    acc = mixer_a(pi, i)
                    if pi == 0 and i == 0:
                        late_setup()
                    if pi == 1 and i == 3:
                        P.op("sp", lambda e: e.dma_start(out=o_a[:, :, :, :], in_=STA[:, :, :, :]),
                             reads=[b for r in STAB for b in r], chan="o_a")
                elif kind == 1:
                    acc = mixer_b(pi, i)
                else:
                    acc = mixer_c(pi, i)
                    if pi == 1:
                        P.op("sp", lambda e: e.dma_start(out=o_p[:, :, :], in_=STP[:, :, :]), reads=[STPB], chan="o_p")
                norm(pi, PV_GF + i * 8, "h", acc)
                acc = ffn(pi, i)
            norm(pi, PV_GFIN, "y", acc)

        esem = {e: es.enter_context(nc.semaphore(f"sem_{e}")) for e in ENGS}
        chans = sorted({o.chan for e in ENGS for o in P.q[e] if o.chan is not None})
        csem = {c: es.enter_context(nc.semaphore(f"sem_c_{c}")) for c in chans}
        block = es.enter_context(nc.Block())
        engines = {"pe": "tensor", "act": "scalar", "dve": "vector", "pool": "gpsimd", "sp": "sync"}
        P.emit(nc, block, engines, esem, csem, final_chans=["yout", "cvout", "stout"])
    return nc


_NC_CACHE = {}


def kernel(**inp):
    inp = {k: np.asarray(v) for k, v in inp.items()}
    wstream = pack_stream(inp)
    pvec = pack_pvec(inp)
    pbc = pack_pbc(inp)
    wsT = np.ascontiguousarray(inp["b_w_s"][0].transpose(2, 0, 1))
    nslots = wstream.shape[0]
    if nslots not in _NC_CACHE:
        _NC_CACHE[nslots] = build_nc(nslots)
    nc = _NC_CACHE[nslots]
    in_maps = []
    for b in range(NCORES):
        xs = inp["x_sample"][NSEQ * b:NSEQ * (b + 1)].reshape(128, D)
        sa = inp["state_shortconv"][:, NSEQ * b:NSEQ * (b + 1)]
        sf = inp["state_ffnconv"][:, NSEQ * b:NSEQ * (b + 1)]
        sp_ = inp["state_pool"][0, NSEQ * b:NSEQ * (b + 1)]
        in_maps.append({
            "xT": np.ascontiguousarray(fm(inp["x_prompt"][b]).transpose(0, 2, 1)),
            "xsT": np.ascontiguousarray(fm(xs).transpose(0, 2, 1)),
            "st_a": np.ascontiguousarray(np.pad(fm(sa.reshape(2, 32, D)).transpose(0, 1, 3, 2),
                                                ((0, 0), (0, 0), (0, 0), (2, 0)))),
            "st_f": np.ascontiguousarray(np.pad(fm(sf.reshape(DEPTH, 32, DFF)).transpose(0, 1, 3, 2),
                                                ((0, 0), (0, 0), (0, 0), (2, 0)))),
            "st_p": np.ascontiguousarray(np.pad(fm(sp_.reshape(240, D)).transpose(0, 2, 1),
                                                ((0, 0), (0, 0), (15, 0)))),
            "pvec": pvec, "pbc": pbc, "wsT": wsT, "wstream": wstream,
        })
    res = run_bass_kernel_spmd(nc, in_maps, core_ids=list(range(NCORES)))
    B = NCORES
    y_prompt = np.zeros((B, SEQ, D), np.float32)
    y_sample = np.zeros((B * NSEQ, DSEQ, D), np.float32)
    sc_p = np.zeros((2, B, 2, D), np.float32)
    sc_s = np.zeros((2, B * NSEQ, 2, D), np.float32)
    pl_p = np.zeros((1, B, 15, D), np.float32)
    pl_s = np.zeros((1, B * NSEQ, 15, D), np.float32)
    ff_p = np.zeros((DEPTH, B, 2, DFF), np.float32)
    ff_s = np.zeros((DEPTH, B * NSEQ, 2, DFF), np.float32)
    cv_s = np.zeros((1, B * NSEQ, DSEQ, D), np.float32)

    def unfm(a):
        a = np.moveaxis(a, 0, -1)
        a = np.swapaxes(a, -3, -2)
        return a.reshape(a.shape[:-2] + (-1,))

    for b in range(B):
        r = res.results[b]
        yt = unfm(r["yT"])
        y_prompt[b] = yt[:SEQ]
        y_sample[NSEQ * b:NSEQ * (b + 1)] = yt[SEQ:].reshape(NSEQ, DSEQ, D)
        oa = unfm(r["o_a"])
        sc_p[:, b] = oa[:, 0:2]
        sc_s[:, NSEQ * b:NSEQ * (b + 1)] = oa[:, 2:34].reshape(2, NSEQ, 2, D)
        of = unfm(r["o_f"])
        ff_p[:, b] = of[:, 0:2]
        ff_s[:, NSEQ * b:NSEQ * (b + 1)] = of[:, 2:34].reshape(DEPTH, NSEQ, 2, DFF)
        op_ = unfm(r["o_p"])
        pl_p[0, b] = op_[0:15]
        pl_s[0, NSEQ * b:NSEQ * (b + 1)] = op_[15:255].reshape(NSEQ, 15, D)
        cv_s[0, NSEQ * b:NSEQ * (b + 1)] = r["o_cv"].reshape(NSEQ, DSEQ, D)
    return (y_prompt, y_sample, sc_p, sc_s, pl_p, pl_s, ff_p, ff_s, cv_s)
```

```python
from contextlib import ExitStack

import numpy as np
import concourse.bass as bass
import concourse.mybir as mybir
from concourse.bass_utils import run_bass_kernel_spmd

F32 = mybir.dt.float32
BF16 = mybir.dt.bfloat16
ALU = mybir.AluOpType
AF = mybir.ActivationFunctionType

D = 1024
KC = 8
DFF = 2816
MC = 22
DEPTH = 4
EPS = 1e-6
SEQ = 2048
NSEQ = 16
DSEQ = 8
TP = 1024
T0 = TP + NSEQ * DSEQ
SLOT = 4096
NCORES = 8
WIN = (2, 4, 8, 16)
MERGE_PAIRS = False
SUM_ENG = ("dve", "dve", "dve", "dve", "dve", "pool", "pool", "pool")
CH_ORDER = (7, 6, 5, 4, 3, 2, 1, 0)

PV_GM = 0
PV_GF = 32
PV_GFIN = 64
PV_AC = 72
PV_CS = 120
PV_FC = 128
PV_FB = 392
PV_N = 480

COMPUTE = ("pe", "act", "dve", "pool")
QUEUES = ("sp",)
ENGS = ("pe", "act", "dve", "pool", "sp")


class Buf:
    __slots__ = ("name", "w", "r", "rd", "extra", "grp")

    def __init__(self, name, grp=None):
        self.name = name
        self.w = None
        self.r = {}
        self.rd = []
        self.extra = []
        self.grp = grp


class Op:
    __slots__ = ("eng", "fn", "deps", "signal", "chan", "val", "idx")

    def __init__(self, eng, fn, chan):
        self.eng = eng
        self.fn = fn
        self.chan = chan
        self.deps = []
        self.signal = False
        self.val = 0
        self.idx = 0


class Prog:
    def __init__(self):
        self.q = {e: [] for e in ENGS}
        self.active = {}
        self.grp_bufs = {}
        self.chans = {}
        self.n = 0

    def buf(self, name, grp=None):
        b = Buf(name, grp)
        if grp is not None:
            self.grp_bufs.setdefault(grp, []).append(b)
        return b

    def _activate(self, b):
        if b.grp is None:
            return
        region = b.grp[0]
        cur = self.active.get(region)
        if cur == b.grp:
            return
        if cur is not None:
            pend = {}
            for ob in self.grp_bufs[cur]:
                for o in ([ob.w] if ob.w is not None else []) + list(ob.r.values()) + ob.rd + ob.extra:
                    key = o.eng if o.chan is None else ("dma", id(o))
                    if key not in pend or pend[key].idx < o.idx:
                        pend[key] = o
            pl = list(pend.values())
            for nb in self.grp_bufs[b.grp]:
                nb.extra = list(pl)
        self.active[region] = b.grp

    def op(self, eng, fn, reads=(), writes=(), chan=None):
        o = Op(eng, fn, chan)
        o.idx = self.n
        self.n += 1
        for b in reads:
            self._activate(b)
        for b in writes:
            self._activate(b)
        deps = {}
        for b in reads:
            if b.w is not None:
                deps[id(b.w)] = b.w
            for x in b.extra:
                deps[id(x)] = x
        for b in writes:
            if b.w is not None:
                deps[id(b.w)] = b.w
            for x in b.r.values():
                deps[id(x)] = x
            for x in b.rd:
                deps[id(x)] = x
            for x in b.extra:
                deps[id(x)] = x
        for d in deps.values():
            if d is o:
                continue
            if d.chan is None and d.eng == "pe" and eng == "pe" and chan is None:
                continue
            o.deps.append(d)
            d.signal = True
        for b in reads:
            if chan is None:
                b.r[eng] = o
            else:
                b.rd.append(o)
        for b in writes:
            b.w = o
            b.r = {}
            b.rd = []
            b.extra = []
        self.q[eng].append(o)
        return o

    def emit(self, nc, block, engines, esem, csem, final_chans):
        for e in ENGS:
            cnt = 0
            for o in self.q[e]:
                if o.chan is None:
                    if o.signal:
                        cnt += 1
                    o.val = cnt
        allops = sorted((o for e in ENGS for o in self.q[e]), key=lambda o: o.idx)
        ccount = {}
        for o in allops:
            if o.chan is not None:
                ccount[o.chan] = ccount.get(o.chan, 0) + 16
                o.val = ccount[o.chan]

        know = {e: {} for e in ENGS}
        opknow = {}
        waits = {}
        for o in allops:
            ke = know[o.eng]
            wl = []
            for d in sorted(o.deps, key=lambda d: -d.idx):
                key = ("e", d.eng) if d.chan is None else ("c", d.chan)
                if ke.get(key, 0) >= d.val:
                    continue
                wl.append((key, d.val))
                ke[key] = d.val
                for k2, v2 in opknow[id(d)].items():
                    if ke.get(k2, 0) < v2:
                        ke[k2] = v2
            waits[id(o)] = wl
            if o.chan is not None or o.signal:
                opknow[id(o)] = dict(ke)

        def run(e, eng):
            for o in self.q[e]:
                for key, val in waits[id(o)]:
                    sem = esem[key[1]] if key[0] == "e" else csem[key[1]]
                    eng.wait_ge(sem, val)
                ins = o.fn(eng)
                if o.chan is not None:
                    ins.then_inc(csem[o.chan], 16)
                elif o.signal:
                    ins.then_inc(esem[e], 1)
            if e == "sp":
                for c in sorted(ccount):
                    eng.wait_ge(csem[c], ccount[c])

        for e in ENGS:
            deco = getattr(block, engines[e])
            deco(lambda eng, e=e: run(e, eng))
        self.nwaits = {e: sum(len(waits[id(o)]) for o in self.q[e]) for e in ENGS}


def _kp(w, cols):
    k = w.shape[0] // 128
    return np.ascontiguousarray(w[:, cols].reshape(k, 128, -1).transpose(1, 0, 2))


def _pad(a):
    a = a.reshape(128, -1)
    out = np.zeros((128, SLOT), np.float32)
    out[:, : a.shape[1]] = a
    return out


def slot_plan():
    plan = []
    for i in range(DEPTH):
        kind, j = i % 3, i // 3
        if kind == 0:
            for jc in range(8):
                plan.append(("a_in", i, jc, 8 * 384))
            for q in range(2):
                plan.append(("a_out", i, q, 8 * 512))
        elif kind == 1:
            for h in range(2):
                plan.append(("b_v", i, h, 8 * 512))
            for q in range(2):
                plan.append(("b_u", i, q, 8 * 512))
            for q in range(2):
                plan.append(("b_out", i, q, 8 * 512))
        else:
            plan.append(("c_g", i, 0, 2048))
        for q in range(11):
            plan.append(("f_up", i, q, 8 * 512))
        for mo in range(8):
            plan.append(("f_dn", i, mo, MC * 128))
    return plan


def pack_stream(inp):
    slots = []
    for kind, i, x, _ in slot_plan():
        j = i // 3
        if kind == "a_in":
            w = inp["a_w_in"][j]
            cols = np.concatenate([np.arange(x * 128, x * 128 + 128) + o for o in (0, 1024, 2048)])
            slots.append(_pad(_kp(w, cols)))
        elif kind == "a_out":
            slots.append(_pad(_kp(inp["a_w_out"][j], np.arange(x * 512, x * 512 + 512))))
        elif kind == "b_v":
            slots.append(_pad(_kp(inp["b_w_in"][j], np.arange(1024 + x * 512, 1024 + x * 512 + 512))))
        elif kind == "b_u":
            slots.append(_pad(_kp(inp["b_w_in"][j], np.arange(x * 512, x * 512 + 512))))
        elif kind == "b_out":
            slots.append(_pad(_kp(inp["b_w_out"][j], np.arange(x * 512, x * 512 + 512))))
        elif kind == "c_g":
            w = inp["c_w_group"][j]
            a = w.reshape(4, 2, 128, 256).transpose(2, 0, 1, 3)
            slots.append(_pad(np.ascontiguousarray(a)))
        elif kind == "f_up":
            w = inp["f_w_up"][i]
            cols = np.concatenate([
                np.arange(2 * x * 128, 2 * x * 128 + 256),
                np.arange(DFF + 2 * x * 128, DFF + 2 * x * 128 + 256),
            ])
            slots.append(_pad(_kp(w, cols)))
        elif kind == "f_dn":
            slots.append(_pad(_kp(inp["f_w_down"][i], np.arange(x * 128, x * 128 + 128))))
    return np.stack(slots)


def fm(v):
    n = v.shape[-1] // 128
    a = v.reshape(v.shape[:-1] + (n, 128))
    return np.moveaxis(a, -1, 0)


def pack_pvec(inp):
    pv = np.zeros((128, PV_N), np.float32)
    pv[:, PV_GM:PV_GM + 32] = fm(inp["g_mix"]).reshape(128, 32)
    pv[:, PV_GF:PV_GF + 32] = fm(inp["g_ffn"]).reshape(128, 32)
    pv[:, PV_GFIN:PV_GFIN + 8] = fm(inp["g_final"]).reshape(128, 8)
    pv[:, PV_AC:PV_AC + 48] = fm(inp["a_conv"]).reshape(128, 48)
    pv[:, PV_CS:PV_CS + 8] = fm(inp["c_scale"]).reshape(128, 8)
    pv[:, PV_FC:PV_FC + 264] = fm(inp["f_conv"]).reshape(128, 264)
    pv[:, PV_FB:PV_FB + 88] = fm(inp["f_conv_b"]).reshape(128, 88)
    return pv


def pack_pbc(inp):
    pb = np.zeros((128, 2048), np.float32)
    pb[:, 0:1024] = np.broadcast_to(inp["b_g_v"][0][None, :], (128, 1024))
    bias = inp["b_bias"][0]
    pb[:, 1024:1536] = np.broadcast_to(bias.reshape(1, 512), (128, 512))
    bs = np.tile(bias[:, 0:8], (1, 16))
    pb[:, 1536:2048] = np.broadcast_to(bs.reshape(1, 512), (128, 512))
    return pb


def build_nc(nslots):
    nc = bass.Bass("TRN2", target_bir_lowering=False)

    def din(name, shape):
        return nc.dram_tensor(name, list(shape), F32, kind="ExternalInput").ap()

    def dout(name, shape):
        return nc.dram_tensor(name, list(shape), F32, kind="ExternalOutput").ap()

    xT = din("xT", (128, KC, SEQ))
    xsT = din("xsT", (128, KC, 128))
    st_a = din("st_a", (128, 2, KC, 34))
    st_f = din("st_f", (128, DEPTH, MC, 34))
    st_p = din("st_p", (128, KC, 255))
    pvec = din("pvec", (128, PV_N))
    pbc = din("pbc", (128, 2048))
    wsT = din("wsT", (128, 4, 128))
    wstream = din("wstream", (nslots, 128, SLOT))
    yT = dout("yT", (128, KC, SEQ + 128))
    o_a = dout("o_a", (128, 2, KC, 34))
    o_f = dout("o_f", (128, DEPTH, MC, 34))
    o_p = dout("o_p", (128, KC, 255))
    o_cv = dout("o_cv", (128, D))

    P = Prog()
    plan = slot_plan()

    with ExitStack() as es:
        def sb(name, shape, dt=F32):
            return es.enter_context(nc.sbuf_tensor(name, list(shape), dt))

        X = sb("X", (128, KC, T0))
        H = sb("H", (128, KC, T0), BF16)
        RA = sb("RA", (128, 12672))
        RB = sb("RB", (128, 6528))
        NWS = 3
        WS = [sb(f"WS{r}", (128, SLOT), BF16) for r in range(NWS)]
        RS = [sb(f"RS{r}", (128, 512)) for r in range(3)]
        STA = sb("STA", (128, 2, KC, 34))
        STF = sb("STF", (128, DEPTH, MC, 34))
        STP = sb("STP", (128, KC, 255))
        PV = sb("PV", (128, PV_N))
        PBC = sb("PBC", (128, 2048))
        ONES = sb("ONES", (128, 128), BF16)
        CORR = sb("CORR", (128, 4, 16))
        IOT = sb("IOT", (128, 16))
        WMF = sb("WMF", (128, 4, 128))
        WMSF = sb("WMSF", (128, 4, 128))
        WMT = sb("WMT", (128, 4, 128), BF16)
        WMTS = sb("WMTS", (128, 4, 128), BF16)
        SS = sb("SS", (128, 9, 4))
        JUNK = sb("JUNK", (128, 2))
        HB16H = sb("HB16H", (128, KC, 15), BF16)
        IDS = sb("IDS", (128, 8, 128), BF16)
        IDF = sb("IDF", (128, 128))
        EPSB = sb("EPSB", (128, 1))
        PS = es.enter_context(nc.psum_tensor("PS", [128, 8, 512], F32))

        RAb = RA[:, :].bitcast(BF16)
        INNER = RAb.rearrange("p (m t) -> p m t", m=MC)
        VN = RAb[:, 8 * T0:16 * T0].rearrange("p (n f) -> p n f", f=D)
        HB16 = RA[:, 0:4156].bitcast(BF16).rearrange("p (c t) -> p c t", c=KC)
        HF30 = RA[:, 4156:4396].rearrange("p (c t) -> p c t", c=KC)
        HFS = RA[:, 4396:4396 + 8 * 368].rearrange("p (c s t) -> p c s t", c=KC, s=NSEQ)
        G = [RB[:, 0:1026], RB[:, 1186:2212]]
        GS = [RB[:, 1026:1186].rearrange("p (s t) -> p s t", t=10),
              RB[:, 2212:2372].rearrange("p (s t) -> p s t", t=10)]
        TT = [RB[:, 2372 + 512 * r:2372 + 512 * (r + 1)] for r in range(6)]
        VNF = RB[:, 5444:6468]
        PSET = [[RB[:, 1039 * (2 * si + r):1039 * (2 * si + r + 1)] for r in range(2)] for si in range(2)]
        PSETS = [[RB[:, 4156 + 368 * (2 * si + r):4156 + 368 * (2 * si + r + 1)].rearrange("p (s t) -> p s t", s=NSEQ)
                  for r in range(2)] for si in range(2)]
        GVB = PBC[:, 0:1024]
        BB = PBC[:, 1024:1536].rearrange("p (h t) -> p h t", h=4)
        BBS = PBC[:, 1536:2048].rearrange("p (h t) -> p h t", h=4)

        XB = [[P.buf(f"X{c}_{t}") for t in range(3)] for c in range(KC)]
        HB = [[P.buf(f"H{c}_{t}") for t in range(3)] for c in range(KC)]
        INB = [[P.buf(f"IN{m}_{t}", ("RA", "inner")) for t in range(3)] for m in range(MC)]
        OBB = [[P.buf(f"OB{m}_{t}", ("RA", "mixb")) for t in range(3)] for m in range(KC)]
        VNB = [P.buf(f"VN{n}", ("RA", "mixb")) for n in range(9)]
        HFB = [[P.buf(f"HF{c}_{t}", ("RA", "poolh")) for t in range(3)] for c in range(KC)]
        HFH = P.buf("HFhalo", ("RA", "poolh"))
        F30B = P.buf("HF30", ("RA", "poolh"))
        C_IDS = P.buf("c_ids")
        JUNKB = P.buf("junk")
        HB16HB = P.buf("HB16H")
        GB = [P.buf(f"G{r}", ("RB", "ffn")) for r in range(2)]
        TB = [P.buf(f"T{r}", ("RB", "ffn")) for r in range(6)]
        VNFB = P.buf("VNF", ("RB", "ffn"))
        PAB = [[P.buf(f"PA{si}_{r}", ("RB", "pool")) for r in range(4)] for si in range(2)]
        WSB = [P.buf(f"WS{r}") for r in range(NWS)]
        RSB = [P.buf(f"RS{r}") for r in range(3)]
        BANKB = [P.buf(f"BK{r}") for r in range(8)]
        STAB = [[P.buf(f"STA{j}_{c}") for c in range(KC)] for j in range(2)]
        STFB = [[P.buf(f"STF{i}_{m}") for m in range(MC)] for i in range(DEPTH)]
        STPB = P.buf("STP")
        C_PV = P.buf("c_pv")
        C_PBC = P.buf("c_pbc")
        C_WM = P.buf("c_wm")
        C_WMS = P.buf("c_wms")
        C_WMSI = [P.buf(f"c_wmsi{s}") for s in range(NSEQ)]
        C_ONES = P.buf("c_ones")
        C_CORR = P.buf("c_corr")
        C_EPS = P.buf("c_eps")
        ALLST = [b for r in STAB for b in r] + [b for r in STFB for b in r] + [STPB]
        SSB = [P.buf(f"SS{n}") for n in range(9)]

        state = {"slot": 0, "rs": 0, "tt": 0, "issued": 0, "sp": 0, "lp": 0, "dp": 0, "down": None}

        SHORT = [0, 1, 2, 3]
        LONG = [4, 5, 6, 7]

        def set_rings(nshort, sp=0):
            SHORT[:] = list(range(nshort))
            LONG[:] = list(range(nshort, 8))
            state["sp"] = sp
            state["lp"] = 0

        def bank(kind="short"):
            if state["down"] is not None:
                ring = state["down"]
                b = ring[state["dp"] % len(ring)]
                state["dp"] += 1
            elif kind == "long":
                b = LONG[state["lp"] % len(LONG)]
                state["lp"] += 1
            else:
                b = SHORT[state["sp"] % len(SHORT)]
                state["sp"] += 1
            return PS[:, b, :], BANKB[b]

        def stats_begin(pi):
            tbs = tblocks(pi)
            nst = len(tbs)
            state["down"] = SHORT[nst:] + [LONG[(state["lp"] + i) % len(LONG)] for i in range(len(LONG))]
            state["dp"] = 0
            return {"pend": [], "banks": {tbi: (PS[:, SHORT[j], :], BANKB[SHORT[j]]) for j, (tbi, _, _) in enumerate(tbs)}}

        def stats_add(acc, c, tbi, t0, n, first, last):
            P.op("act", lambda e: e.activation(out=H[:, c, t0:t0 + n], in_=X[:, c, t0:t0 + n], func=AF.Square),
                 reads=[XB[c][tbi]], writes=[HB[c][tbi]])
            bk, bkb = acc["banks"][tbi]
            acc["pend"].append(lambda: P.op("pe", lambda e: e.matmul(bk[:, 0:n], lhsT=ONES[:, :], rhs=H[:, c, t0:t0 + n],
                                                                    start=first, stop=last),
                                            reads=[HB[c][tbi], C_ONES], writes=[bkb]))

        def stats_flush(acc, keep=1):
            pend = acc["pend"]
            n = max(0, len(pend) - keep)
            for f in pend[:n]:
                f()
            acc["pend"] = pend[n:]

        def stats_end(nst):
            state["down"] = None
            state["sp"] = nst % len(SHORT)

        def rsbuf():
            r = state["rs"] % 3
            state["rs"] += 1
            return RS[r], RSB[r]

        def ttpair():
            r = state["tt"] % 3
            state["tt"] += 1
            return (TT[2 * r], TB[2 * r]), (TT[2 * r + 1], TB[2 * r + 1])

        def issue_slot(s):
            kind, _, _, used = plan[s % len(plan)]
            r = s % NWS
            src = wstream[s % len(plan), :, 0:used].rearrange("p (a b) -> p a b", a=2)
            dst = WS[r][:, 0:used].rearrange("p (a b) -> p a b", a=2)
            P.op("pool", lambda e, dst=dst, src=src: e.dma_start(out=dst, in_=src),
                 writes=[WSB[r]], chan=f"ws{r}")

        def prefetch(n):
            total = 2 * len(plan)
            while state["issued"] < min(state["slot"] + n, total):
                issue_slot(state["issued"])
                state["issued"] += 1

        def next_slot(expect_kind):
            s = state["slot"]
            state["slot"] += 1
            kind, _, _, used = plan[s % len(plan)]
            assert kind == expect_kind, (kind, expect_kind)
            while state["issued"] <= s:
                issue_slot(state["issued"])
                state["issued"] += 1
            r = s % NWS
            return WS[r], WSB[r]

        P.op("sp", lambda e: e.dma_start(out=PV[:, :], in_=pvec[:, :]), writes=[C_PV], chan="s_pv")
        P.op("pool", lambda e: e.memset(ONES[:, :], 1.0), writes=[C_ONES])
        P.op("pool", lambda e: e.memset(EPSB[:, :], EPS), writes=[C_EPS])

        def state_loads():
            P.op("sp", lambda e: e.dma_start(out=STA[:, :, :, :], in_=st_a[:, :, :, :]),
                 writes=[b for r in STAB for b in r], chan="s_sta")
            P.op("sp", lambda e: e.dma_start(out=STF[:, :, :, :], in_=st_f[:, :, :, :]),
                 reads=[HB[KC - 1][0]], writes=[b for r in STFB for b in r], chan="s_stf")
            P.op("sp", lambda e: e.dma_start(out=STP[:, :, :], in_=st_p[:, :, :]),
                 reads=[HB[KC - 1][0]], writes=[STPB], chan="s_stp")

        def late_setup():
            P.op("sp", lambda e: e.dma_start(out=PBC[:, :], in_=pbc[:, :]), writes=[C_PBC], chan="s_pbc")
            P.op("sp", lambda e: e.dma_start(out=WMF[:, :, :], in_=wsT[:, :, :]), writes=[C_WM], chan="s_wm")
            P.op("pool", lambda e: e.memset(WMSF[:, :, :], 0.0), writes=C_WMSI)
            for s in range(NSEQ):
                P.op("sp", lambda e, s=s: e.dma_start(out=WMSF[8 * s:8 * s + 8, :, 8 * s:8 * s + 8],
                                                      in_=wsT[0:8, :, 0:8]), writes=[C_WMSI[s]], chan=f"s_wms{s}")
            P.op("pool", lambda e: e.affine_select(out=WMT[:, :, :], in_=WMF[:, :, :], pattern=[[0, 4], [1, 128]],
                                                   compare_op=ALU.is_ge, fill=0.0, base=0, channel_multiplier=-1),
                 reads=[C_WM], writes=[C_WM])
            P.op("pool", lambda e: e.affine_select(out=WMTS[:, :, :], in_=WMSF[:, :, :], pattern=[[0, 4], [1, 128]],
                                                   compare_op=ALU.is_ge, fill=0.0, base=0, channel_multiplier=-1),
                 reads=C_WMSI, writes=[C_WMS])
            P.op("pool", lambda e: e.memset(HB16H[:, :, :], 0.0), writes=[HB16HB])
            for g in range(4):
                for r, val in enumerate((1.0 / WIN[g] - 1.0, 1.0 / WIN[g])):
                    P.op("pool", lambda e, val=val: e.memset(IDF[:, :], val), reads=[C_IDS], writes=[C_IDS])
                    P.op("pool", lambda e, g=g, r=r: e.affine_select(
                        out=IDS[:, 2 * g + r, :], in_=IDF[:, :], pattern=[[1, 128]], compare_op=ALU.is_equal,
                        fill=0.0, base=0, channel_multiplier=-1), reads=[C_IDS], writes=[C_IDS])
            P.op("pool", lambda e: e.iota(IOT[:, :], [[1, 16]], base=1, channel_multiplier=0,
                                          allow_small_or_imprecise_dtypes=True), writes=[C_CORR])
            P.op("dve", lambda e: e.reciprocal(out=IOT[:, :], in_=IOT[:, :]), reads=[C_CORR], writes=[C_CORR])
            for g in range(4):
                P.op("dve", lambda e, g=g: e.tensor_scalar(out=CORR[:, g, :], in0=IOT[:, :], scalar1=float(WIN[g]),
                                                           scalar2=None, op0=ALU.mult), reads=[C_CORR], writes=[C_CORR])

        CONST_READ = [C_PV]

        def pvc(off):
            return PV[:, off:off + 1]

        def tblocks(pi):
            t = [(0, 0, 512), (1, 512, 512)]
            if pi == 0:
                t.append((2, 1024, 128))
            return t

        def norm(pi, goff, kind, acc=None):
            for (tbi, t0, n) in tblocks(pi):
                hb = [HB[c][tbi] for c in range(KC)]
                xb = [XB[c][tbi] for c in range(KC)]
                if acc is not None:
                    bk, bkb = acc["banks"][tbi]
                else:
                    bk, bkb = bank()
                    for c in range(KC):
                        P.op("act", lambda e, c=c, t0=t0, n=n: e.activation(out=H[:, c, t0:t0 + n],
                                                                            in_=X[:, c, t0:t0 + n], func=AF.Square),
                             reads=[xb[c]], writes=[hb[c]])
                    for c in range(KC):
                        P.op("pe", lambda e, c=c, t0=t0, n=n, bk=bk: e.matmul(bk[:, 0:n], lhsT=ONES[:, :],
                                                                             rhs=H[:, c, t0:t0 + n],
                                                                             start=(c == 0), stop=(c == KC - 1)),
                             reads=[hb[c], C_ONES], writes=[bkb])
                rs, rsb = rsbuf()
                P.op("act", lambda e, n=n, bk=bk, rs=rs: e.activation(out=rs[:, 0:n], in_=bk[:, 0:n], func=AF.Ln,
                                                                      bias=EPSB[:, 0:1], scale=1.0 / D),
                     reads=[bkb, C_EPS], writes=[rsb])
                P.op("act", lambda e, n=n, rs=rs: e.activation(out=rs[:, 0:n], in_=rs[:, 0:n], func=AF.Exp,
                                                               scale=-0.5),
                     reads=[rsb], writes=[rsb])
                if kind == "y":
                    col = pi * TP + t0 if t0 < TP else SEQ + (t0 - TP)
                    for c in range(KC):
                        P.op("dve", lambda e, c=c, t0=t0, n=n, rs=rs: e.scalar_tensor_tensor(
                            out=X[:, c, t0:t0 + n], in0=X[:, c, t0:t0 + n], scalar=pvc(goff + c), in1=rs[:, 0:n],
                            op0=ALU.mult, op1=ALU.mult), reads=[xb[c], rsb] + CONST_READ, writes=[xb[c]])
                        P.op("sp", lambda e, c=c, t0=t0, n=n, col=col: e.dma_start(out=yT[:, c, col:col + n],
                                                                                 in_=X[:, c, t0:t0 + n]),
                             reads=[xb[c]], chan=f"y{c}_{tbi}")
                elif kind == "hf":
                    for c in range(KC):
                        if t0 < TP:
                            dst = HB16[:, c, 15 + t0:15 + t0 + n]
                            i1 = rs[:, 0:n]
                            i0 = X[:, c, t0:t0 + n]
                        else:
                            dst = HFS[:, c, :, 15:23]
                            i1 = rs[:, 0:n].rearrange("p (s t) -> p s t", t=DSEQ)
                            i0 = X[:, c, t0:t0 + n].rearrange("p (s t) -> p s t", t=DSEQ)
                        P.op("dve", lambda e, c=c, dst=dst, i0=i0, i1=i1: e.scalar_tensor_tensor(
                            out=dst, in0=i0, scalar=pvc(goff + c), in1=i1, op0=ALU.mult, op1=ALU.mult),
                            reads=[xb[c], rsb] + CONST_READ, writes=[HFB[c][tbi]])
                    if pi == 0 and t0 == 0:
                        for c in range(KC):
                            P.op("dve", lambda e, c=c, rs=rs: e.scalar_tensor_tensor(
                                out=HF30[:, c, 15:30], in0=X[:, c, 0:15], scalar=pvc(goff + c), in1=rs[:, 0:15],
                                op0=ALU.mult, op1=ALU.mult), reads=[xb[c], rsb] + CONST_READ, writes=[F30B])
                    if pi == 1 and t0 == TP - 512:
                        for c in range(KC):
                            P.op("dve", lambda e, c=c, rs=rs: e.scalar_tensor_tensor(
                                out=STP[:, c, 0:15], in0=X[:, c, TP - 15:TP], scalar=pvc(goff + c),
                                in1=rs[:, 512 - 15:512], op0=ALU.mult, op1=ALU.mult),
                                reads=[xb[c], rsb] + CONST_READ, writes=[STPB])
                else:
                    for c in range(KC):
                        P.op("dve", lambda e, c=c, t0=t0, n=n, rs=rs: e.scalar_tensor_tensor(
                            out=H[:, c, t0:t0 + n], in0=X[:, c, t0:t0 + n], scalar=pvc(goff + c), in1=rs[:, 0:n],
                            op0=ALU.mult, op1=ALU.mult), reads=[xb[c], rsb] + CONST_READ, writes=[hb[c]])
            if acc is not None:
                stats_end(len(tblocks(pi)))

        def linear_out(pi, kind, src, srcb, nk, resid_scale_off=None):
            acc = stats_begin(pi)
            for q in range(2):
                ws, wsb = next_slot(kind)
                wv = ws[:, :].rearrange("p (k n) -> p k n", k=nk)
                for mo4 in range(4):
                    mo = q * 4 + mo4
                    for (tbi, t0, n) in tblocks(pi):
                        bk, bkb = bank()
                        for k in range(nk):
                            P.op("pe", lambda e, k=k, mo4=mo4, t0=t0, n=n, bk=bk, wv=wv: e.matmul(
                                bk[:, 0:n], lhsT=wv[:, k, mo4 * 128:(mo4 + 1) * 128], rhs=src[:, k, t0:t0 + n],
                                start=(k == 0), stop=(k == nk - 1)), reads=[wsb, srcb[k][tbi]], writes=[bkb])
                        stats_flush(acc)
                        P.op("dve", lambda e, mo=mo, t0=t0, n=n, bk=bk: e.tensor_tensor(
                            out=X[:, mo, t0:t0 + n], in0=bk[:, 0:n], in1=X[:, mo, t0:t0 + n], op=ALU.add),
                            reads=[bkb, XB[mo][tbi]], writes=[XB[mo][tbi]])
                        stats_add(acc, mo, tbi, t0, n, mo == 0, mo == KC - 1)
            stats_flush(acc, keep=0)
            return acc

        def conv_tail(pi, tbi, t0, n, gi, w0, w1, ta, tab, tbb_ap, tbbb, extra_reads):
            if t0 < TP:
                g1 = G[gi][:, 1 + t0:1 + t0 + n]
                g0 = G[gi][:, t0:t0 + n]
                a_ = ta[:, 0:n]
                b_ = tbb_ap[:, 0:n]
            else:
                g1 = GS[gi][:, :, 1:9]
                g0 = GS[gi][:, :, 0:8]
                a_ = ta[:, 0:n].rearrange("p (s t) -> p s t", t=DSEQ)
                b_ = tbb_ap[:, 0:n].rearrange("p (s t) -> p s t", t=DSEQ)
            P.op("dve", lambda e: e.scalar_tensor_tensor(out=b_, in0=g1, scalar=w1, in1=a_, op0=ALU.mult, op1=ALU.add),
                 reads=[GB[gi], tab] + extra_reads, writes=[tbbb])
            P.op("dve", lambda e: e.scalar_tensor_tensor(out=a_, in0=g0, scalar=w0, in1=b_, op0=ALU.mult, op1=ALU.add),
                 reads=[GB[gi], tbbb] + extra_reads, writes=[tab])

        def halo_in(pi, gi, st_ap, stb):
            P.op("act", lambda e: e.copy(out=G[gi][:, 0:2], in_=st_ap[:, 0:2]), reads=[stb], writes=[GB[gi]])
            if pi == 0:
                P.op("act", lambda e: e.copy(out=GS[gi][:, :, 0:2],
                                             in_=st_ap[:, 2:34].rearrange("p (s t) -> p s t", t=2)),
                     reads=[stb], writes=[GB[gi]])

        def halo_out(pi, gi, st_ap, stb):
            P.op("act", lambda e: e.copy(out=st_ap[:, 0:2], in_=G[gi][:, TP:TP + 2]), reads=[GB[gi]], writes=[stb])
            if pi == 0:
                P.op("act", lambda e: e.copy(out=st_ap[:, 2:34].rearrange("p (s t) -> p s t", t=2),
                                             in_=GS[gi][:, :, 8:10]), reads=[GB[gi]], writes=[stb])

        def ffn(pi, i):
            tail = [None]

            def unit(ws_v, wsb, ms, tbi, t0, n):
                bks = []
                for (m, mm, gi) in ms:
                    bks.append((bank(), bank("long")))
                for k in range(KC):
                    for (m, mm, gi), ((bg, bgb), (ba, bab)) in zip(ms, bks):
                        P.op("pe", lambda e, k=k, mm=mm, bg=bg: e.matmul(
                            bg[:, 0:n], lhsT=ws_v[:, k, mm * 128:(mm + 1) * 128], rhs=H[:, k, t0:t0 + n],
                            start=(k == 0), stop=(k == KC - 1)), reads=[wsb, HB[k][tbi]], writes=[bgb])
                        P.op("pe", lambda e, k=k, mm=mm, ba=ba: e.matmul(
                            ba[:, 0:n], lhsT=ws_v[:, k, 256 + mm * 128:256 + (mm + 1) * 128], rhs=H[:, k, t0:t0 + n],
                            start=(k == 0), stop=(k == KC - 1)), reads=[wsb, HB[k][tbi]], writes=[bab])
                for (m, mm, gi), ((bg, bgb), (ba, bab)) in zip(ms, bks):
                    w0, w1, w2 = (pvc(PV_FC + i * 66 + k * 22 + m) for k in range(3))
                    bcol = pvc(PV_FB + i * 22 + m)
                    (ta, tab), (tb_, tbb) = ttpair()
                    if t0 < TP:
                        gdst = G[gi][:, 2 + t0:2 + t0 + n]
                        gsrc = bg[:, 0:n]
                    else:
                        gdst = GS[gi][:, :, 2:10]
                        gsrc = bg[:, 0:n].rearrange("p (s t) -> p s t", t=DSEQ)
                    P.op("act", lambda e, gdst=gdst, gsrc=gsrc: e.copy(out=gdst, in_=gsrc),
                         reads=[bgb], writes=[GB[gi]])
                    P.op("act", lambda e, bg=bg, ta=ta, w2=w2, bcol=bcol: e.activation(
                        out=ta[:, 0:n], in_=bg[:, 0:n], func=AF.Identity, bias=bcol, scale=w2),
                        reads=[bgb] + CONST_READ, writes=[tab])
                    conv_tail(pi, tbi, t0, n, gi, w0, w1, ta, tab, tb_, tbb, CONST_READ)
                    if tail[0] is not None:
                        tail[0]()

                    def mk_tail(m=m, ta=ta, tab=tab, tb_=tb_, tbb=tbb, ba=ba, bab=bab):
                        P.op("act", lambda e: e.activation(out=tb_[:, 0:n], in_=ta[:, 0:n], func=AF.Silu),
                             reads=[tab], writes=[tbb])
                        P.op("dve", lambda e: e.tensor_tensor(out=INNER[:, m, t0:t0 + n], in0=ba[:, 0:n],
                                                              in1=tb_[:, 0:n], op=ALU.mult),
                             reads=[bab, tbb], writes=[INB[m][tbi]])
                    tail[0] = mk_tail

            set_rings(3)
            for q in range(11):
                ws, wsb = next_slot("f_up")
                wv = ws[:, :].rearrange("p (k n) -> p k n", k=KC)
                ms = [(2 * q + mm, mm, mm) for mm in range(2)]
                if MERGE_PAIRS or q == 0:
                    for (m, mm, gi) in ms:
                        halo_in(pi, gi, STF[:, i, m, :], STFB[i][m])
                    for (tbi, t0, n) in tblocks(pi):
                        unit(wv, wsb, ms, tbi, t0, n)
                    for (m, mm, gi) in ms:
                        halo_out(pi, gi, STF[:, i, m, :], STFB[i][m])
                else:
                    for (m, mm, gi) in ms:
                        halo_in(pi, gi, STF[:, i, m, :], STFB[i][m])
                        for (tbi, t0, n) in tblocks(pi):
                            unit(wv, wsb, [(m, mm, gi)], tbi, t0, n)
                        halo_out(pi, gi, STF[:, i, m, :], STFB[i][m])
            tail[0]()
            if pi == 1 and i == DEPTH - 1:
                P.op("sp", lambda e: e.dma_start(out=o_f[:, :, :, :], in_=STF[:, :, :, :]),
                     reads=[b for r in STFB for b in r], chan="o_f")
            acc = stats_begin(pi)
            P.op("act", lambda e: e.activation(out=JUNK[:, 0:1], in_=EPSB[:, 0:1], func=AF.Ln),
                 reads=[C_EPS], writes=[JUNKB])
            for mo in range(KC):
                ws, wsb = next_slot("f_dn")
                wv = ws[:, 0:MC * 128].rearrange("p (m n) -> p m n", m=MC)
                for (tbi, t0, n) in tblocks(pi):
                    bk, bkb = bank()
                    for m in range(MC):
                        P.op("pe", lambda e, m=m, t0=t0, n=n, bk=bk, wv=wv: e.matmul(
                            bk[:, 0:n], lhsT=wv[:, m, :], rhs=INNER[:, m, t0:t0 + n],
                            start=(m == 0), stop=(m == MC - 1)), reads=[wsb, INB[m][tbi]], writes=[bkb])
                    stats_flush(acc, keep=0)
                    P.op("dve", lambda e, mo=mo, t0=t0, n=n, bk=bk: e.tensor_tensor(
                        out=X[:, mo, t0:t0 + n], in0=bk[:, 0:n], in1=X[:, mo, t0:t0 + n], op=ALU.add),
                        reads=[bkb, XB[mo][tbi]], writes=[XB[mo][tbi]])
                    stats_add(acc, mo, tbi, t0, n, mo == 0, mo == KC - 1)
            stats_flush(acc, keep=0)
            return acc

        def mixer_a(pi, i):
            j = i // 3
            tail = [None]
            for jc in range(KC):
                ws, wsb = next_slot("a_in")
                wv = ws[:, 0:KC * 384].rearrange("p (k n) -> p k n", k=KC)
                gi = jc % 2
                st_ap = STA[:, j, jc, :]
                stb = STAB[j][jc]
                halo_in(pi, gi, st_ap, stb)
                w0, w1, w2 = (pvc(PV_AC + j * 24 + k * 8 + jc) for k in range(3))
                for (tbi, t0, n) in tblocks(pi):
                    banks = [bank(), bank(), bank("long")]
                    for k in range(KC):
                        for bi, (bk, bkb) in enumerate(banks):
                            P.op("pe", lambda e, k=k, bi=bi, t0=t0, n=n, bk=bk, wv=wv: e.matmul(
                                bk[:, 0:n], lhsT=wv[:, k, (2 - bi) * 128:(3 - bi) * 128], rhs=H[:, k, t0:t0 + n],
                                start=(k == 0), stop=(k == KC - 1)), reads=[wsb, HB[k][tbi]], writes=[bkb])
                    (pv_, pvb), (pc, pcb), (pb, pbb) = banks
                    (ta, tab), (tb_, tbb) = ttpair()
                    P.op("act", lambda e, n=n, ta=ta, pv_=pv_: e.copy(out=ta[:, 0:n], in_=pv_[:, 0:n]),
                         reads=[pvb], writes=[tab])
                    if t0 < TP:
                        gdst = G[gi][:, 2 + t0:2 + t0 + n]
                        csrc = pc[:, 0:n]
                        vsrc = ta[:, 0:n]
                    else:
                        gdst = GS[gi][:, :, 2:10]
                        csrc = pc[:, 0:n].rearrange("p (s t) -> p s t", t=DSEQ)
                        vsrc = ta[:, 0:n].rearrange("p (s t) -> p s t", t=DSEQ)
                    P.op("dve", lambda e, gdst=gdst, csrc=csrc, vsrc=vsrc: e.tensor_tensor(
                        out=gdst, in0=csrc, in1=vsrc, op=ALU.mult), reads=[pcb, tab], writes=[GB[gi]])
                    P.op("act", lambda e, gdst=gdst, ta=ta, n=n, w2=w2, t0=t0: e.activation(
                        out=(ta[:, 0:n] if t0 < TP else ta[:, 0:n].rearrange("p (s t) -> p s t", t=DSEQ)),
                        in_=gdst, func=AF.Identity, scale=w2), reads=[GB[gi]] + CONST_READ, writes=[tab])
                    if tail[0] is not None:
                        tail[0]()

                    def mk_tail(jc=jc, tbi=tbi, t0=t0, n=n, gi=gi, w0=w0, w1=w1, ta=ta, tab=tab, tb_=tb_, tbb=tbb,
                                pb=pb, pbb=pbb):
                        conv_tail(pi, tbi, t0, n, gi, w0, w1, ta, tab, tb_, tbb, CONST_READ)
                        P.op("dve", lambda e: e.tensor_tensor(out=INNER[:, jc, t0:t0 + n], in0=pb[:, 0:n],
                                                              in1=ta[:, 0:n], op=ALU.mult),
                             reads=[pbb, tab], writes=[INB[jc][tbi]])
                    tail[0] = mk_tail
                halo_out(pi, gi, st_ap, stb)
            tail[0]()
            return linear_out(pi, "a_out", INNER, INB, KC)

        def mixer_b(pi, i):
            OUTB = INNER
            wvs = []
            for h in range(2):
                ws, wsb = next_slot("b_v")
                wvs.append((ws[:, :].rearrange("p (k n) -> p k n", k=KC), wsb))
            ntiles = 9 if pi == 0 else 8
            for nt in range(ntiles):
                t0 = nt * 128
                tbi = t0 // 512
                halves = []
                for h in range(2):
                    bk, bkb = bank("short" if h == 0 else "long")
                    wv, wsb = wvs[h]
                    for k in range(KC):
                        P.op("pe", lambda e, k=k, t0=t0, bk=bk, wv=wv: e.matmul(
                            bk[:, :], lhsT=H[:, k, t0:t0 + 128], rhs=wv[:, k, :],
                            start=(k == 0), stop=(k == KC - 1)), reads=[wsb, HB[k][tbi]], writes=[bkb])
                    halves.append((bk, bkb))
                (ta, tab), (tb_, tbb) = ttpair()
                for h in range(2):
                    bk, bkb = halves[h]
                    junk, junkb = (ta, tab) if h == 0 else (tb_, tbb)
                    P.op("act", lambda e, h=h, bk=bk, junk=junk, nt=nt: e.activation(
                        out=junk[:, :], in_=bk[:, :], func=AF.Square, accum_out=SS[:, nt, h:h + 1]),
                        reads=[bkb], writes=[junkb, SSB[nt]])
                P.op("dve", lambda e, nt=nt: e.tensor_tensor(out=SS[:, nt, 2:3], in0=SS[:, nt, 0:1],
                                                             in1=SS[:, nt, 1:2], op=ALU.add),
                     reads=[SSB[nt]], writes=[SSB[nt]])
                P.op("act", lambda e, nt=nt: e.activation(out=SS[:, nt, 3:4], in_=SS[:, nt, 2:3], func=AF.Ln,
                                                          bias=EPSB[:, 0:1], scale=1.0 / D),
                     reads=[SSB[nt], C_EPS], writes=[SSB[nt]])
                P.op("act", lambda e, nt=nt: e.activation(out=SS[:, nt, 2:3], in_=SS[:, nt, 3:4], func=AF.Exp,
                                                          scale=-0.5),
                     reads=[SSB[nt]], writes=[SSB[nt]])
                for h in range(2):
                    bk, bkb = halves[h]
                    P.op("dve", lambda e, h=h, bk=bk, nt=nt: e.scalar_tensor_tensor(
                        out=VN[:, nt, h * 512:(h + 1) * 512], in0=bk[:, :], scalar=SS[:, nt, 2:3],
                        in1=GVB[:, h * 512:(h + 1) * 512], op0=ALU.mult, op1=ALU.mult),
                        reads=[bkb, SSB[nt], C_PBC], writes=[VNB[nt]])
                    if nt == 8:
                        P.op("dve", lambda e, h=h, bk=bk, nt=nt: e.scalar_tensor_tensor(
                            out=VNF[:, h * 512:(h + 1) * 512], in0=bk[:, :], scalar=SS[:, nt, 2:3],
                            in1=GVB[:, h * 512:(h + 1) * 512], op0=ALU.mult, op1=ALU.mult),
                            reads=[bkb, SSB[nt], C_PBC], writes=[VNFB])
                if nt == 8:
                    P.op("sp", lambda e: e.dma_start(out=o_cv[:, :], in_=VNF), reads=[VNFB], chan="cvout")
            for q in range(2):
                ws, wsb = next_slot("b_u")
                wv = ws[:, :].rearrange("p (k n) -> p k n", k=KC)
                for d4 in range(4):
                    dc = q * 4 + d4
                    hd = dc // 2
                    for (tbi, t0, n) in tblocks(pi):
                        bs, bsb = bank()
                        nsub = n // 128
                        for sub in range(nsub):
                            nt = t0 // 128 + sub
                            wm = WMT if t0 < TP else WMTS
                            P.op("pe", lambda e, sub=sub, nt=nt, dc=dc, hd=hd, bs=bs, wm=wm: e.matmul(
                                bs[:, sub * 128:(sub + 1) * 128], lhsT=VN[:, nt, dc * 128:(dc + 1) * 128],
                                rhs=wm[:, hd, :], start=True, stop=True),
                                reads=[VNB[nt], C_WM, C_WMS], writes=[bsb])
                        (ta, tab), _ = ttpair()
                        bbv = (BB if t0 < TP else BBS)[:, hd, :]
                        P.op("dve", lambda e, n=n, nsub=nsub, bs=bs, ta=ta, bbv=bbv: e.tensor_tensor(
                            out=ta[:, 0:n].rearrange("p (a t) -> p a t", a=nsub),
                            in0=bs[:, 0:n].rearrange("p (a t) -> p a t", a=nsub),
                            in1=bbv.unsqueeze(1).to_broadcast([128, nsub, 128]), op=ALU.add),
                            reads=[bsb, C_PBC], writes=[tab])
                        bu, bub = bank("long")
                        for k in range(KC):
                            P.op("pe", lambda e, k=k, d4=d4, t0=t0, n=n, bu=bu, wv=wv: e.matmul(
                                bu[:, 0:n], lhsT=wv[:, k, d4 * 128:(d4 + 1) * 128], rhs=H[:, k, t0:t0 + n],
                                start=(k == 0), stop=(k == KC - 1)), reads=[wsb, HB[k][tbi]], writes=[bub])
                        P.op("dve", lambda e, dc=dc, t0=t0, n=n, bu=bu, ta=ta: e.tensor_tensor(
                            out=OUTB[:, dc, t0:t0 + n], in0=bu[:, 0:n], in1=ta[:, 0:n], op=ALU.mult),
                            reads=[bub, tab], writes=[OBB[dc][tbi]])
            return linear_out(pi, "b_out", OUTB, OBB, KC)

        def mixer_c(pi, i):
            ws, wsb = next_slot("c_g")
            wg = ws[:, 0:2048].rearrange("p (g c e) -> p g c e", g=4, c=2)
            prefetch(2)
            P.op("act", lambda e: e.copy(out=HB16[:, :, 0:15], in_=HB16H[:, :, :]), reads=[HB16HB], writes=[HFH])
            if pi == 0:
                P.op("act", lambda e: e.copy(out=HFS[:, :, :, 0:15],
                                             in_=STP[:, :, 15:255].rearrange("p c (s t) -> p c s t", t=15)),
                     reads=[STPB], writes=[HFH])
                P.op("pool", lambda e: e.memset(HF30[:, :, 0:15], 0.0), writes=[F30B])
                P.op("act", lambda e: e.copy(out=HB16H[:, :, :], in_=HB16[:, :, TP:TP + 15]),
                     reads=[HFB[c][1] for c in range(KC)], writes=[HB16HB])
            for (tbi, t0, n) in tblocks(pi):
                if t0 >= TP:
                    continue
                for c in range(KC):
                    g = c // 2
                    w = WIN[g]
                    bk, bkb = bank()
                    for jj in range(w):
                        P.op("pe", lambda e, c=c, g=g, jj=jj, w=w, t0=t0, n=n, bk=bk: e.matmul(
                            bk[:, 0:n], lhsT=IDS[:, 2 * g + (0 if jj == 0 else 1), :],
                            rhs=HB16[:, c, 15 + t0 - jj:15 + t0 - jj + n], start=(jj == 0), stop=(jj == w - 1)),
                            reads=[HFB[c][tbi], HFH, C_IDS] + ([HFB[c][tbi - 1]] if tbi > 0 else []), writes=[bkb])
                    P.op("act", lambda e, c=c, t0=t0, n=n, bk=bk: e.copy(out=H[:, c, t0:t0 + n], in_=bk[:, 0:n]),
                         reads=[bkb], writes=[HB[c][tbi]])
            for c in CH_ORDER:
                g = c // 2
                w = WIN[g]
                eng = SUM_ENG[c]
                si = 0 if eng == "dve" else 1
                if pi == 0:
                    cur, curb = HF30[:, c, :], [F30B]
                    bufs = [(PSET[si][0], [PAB[si][0]]), (PSET[si][1], [PAB[si][1]])]
                    sh = 1
                    for st in range(g + 1):
                        dst, dstb = bufs[st % 2]
                        lo = 2 * sh - 1
                        P.op(eng, lambda e, dst=dst, cur=cur, lo=lo, sh=sh: e.tensor_tensor(
                            out=dst[:, lo:30], in0=cur[:, lo:30], in1=cur[:, lo - sh:30 - sh], op=ALU.add),
                            reads=curb, writes=dstb)
                        cur, curb = dst, dstb
                        sh *= 2
                    P.op(eng, lambda e, cur=cur, g=g, w=w: e.tensor_tensor(
                        out=cur[:, 15:15 + w - 1], in0=cur[:, 15:15 + w - 1], in1=CORR[:, g, 0:w - 1], op=ALU.mult),
                        reads=curb + [C_CORR], writes=curb)
                    P.op("dve", lambda e, cur=cur, c=c, w=w: e.scalar_tensor_tensor(
                        out=H[:, c, 0:15], in0=cur[:, 15:30], scalar=1.0 / w, in1=HF30[:, c, 15:30],
                        op0=ALU.mult, op1=ALU.subtract), reads=curb + [F30B], writes=[HB[c][0]])
            for c in CH_ORDER:
                g = c // 2
                w = WIN[g]
                eng = SUM_ENG[c]
                si = 0 if eng == "dve" else 1
                if pi == 0:
                    hfc = [HFB[c][2], HFH]
                    cur, curb = HFS[:, c, :, :], hfc
                    bufs = [(PSETS[si][0], [PAB[si][2]]), (PSETS[si][1], [PAB[si][3]])]
                    sh = 1
                    for st in range(g + 1):
                        dst, dstb = bufs[st % 2]
                        lo = 2 * sh - 1
                        P.op(eng, lambda e, dst=dst, cur=cur, lo=lo, sh=sh: e.tensor_tensor(
                            out=dst[:, :, lo:23], in0=cur[:, :, lo:23], in1=cur[:, :, lo - sh:23 - sh],
                            op=ALU.add), reads=curb, writes=dstb)
                        cur, curb = dst, dstb
                        sh *= 2
                    if eng == "pool":
                        P.op("pool", lambda e, cur=cur, w=w: e.tensor_scalar(
                            out=cur[:, :, 15:23], in0=cur[:, :, 15:23], scalar1=1.0 / w, scalar2=0.0,
                            op0=ALU.mult, op1=ALU.add), reads=curb, writes=curb)
                        P.op("pool", lambda e, cur=cur, c=c: e.tensor_tensor(
                            out=H[:, c, TP:TP + 128].rearrange("p (s t) -> p s t", t=DSEQ),
                            in0=cur[:, :, 15:23], in1=HFS[:, c, :, 15:23], op=ALU.subtract),
                            reads=curb + [HFB[c][2]], writes=[HB[c][2]])
                    else:
                        P.op("dve", lambda e, cur=cur, c=c, w=w: e.scalar_tensor_tensor(
                            out=H[:, c, TP:TP + 128].rearrange("p (s t) -> p s t", t=DSEQ),
                            in0=cur[:, :, 15:23], scalar=1.0 / w, in1=HFS[:, c, :, 15:23],
                            op0=ALU.mult, op1=ALU.subtract), reads=curb + [HFB[c][2]], writes=[HB[c][2]])
            if pi == 0:
                P.op("act", lambda e: e.copy(out=STP[:, :, 15:255].rearrange("p c (s t) -> p c s t", t=15),
                                             in_=HFS[:, :, :, 8:23]),
                     reads=[HFB[c][2] for c in range(KC)] + [HFH], writes=[STPB])
            acc = stats_begin(pi)
            tbs = tblocks(pi)
            order = [tbs[1], tbs[0]] + tbs[2:]
            for (tbi, t0, n) in order:
                for g in range(4):
                    bks = []
                    for eh in range(2):
                        bk, bkb = bank()
                        bks.append((bk, bkb))
                        for ch in range(2):
                            P.op("pe", lambda e, g=g, ch=ch, eh=eh, t0=t0, n=n, bk=bk: e.matmul(
                                bk[:, 0:n], lhsT=wg[:, g, ch, eh * 128:(eh + 1) * 128], rhs=H[:, 2 * g + ch, t0:t0 + n],
                                start=(ch == 0), stop=(ch == 1)), reads=[wsb, HB[2 * g + ch][tbi]], writes=[bkb])
                    stats_flush(acc)
                    for eh in range(2):
                        c = 2 * g + eh
                        bk, bkb = bks[eh]
                        P.op("dve", lambda e, c=c, t0=t0, n=n, bk=bk: e.scalar_tensor_tensor(
                            out=X[:, c, t0:t0 + n], in0=bk[:, 0:n], scalar=pvc(PV_CS + c), in1=X[:, c, t0:t0 + n],
                            op0=ALU.mult, op1=ALU.add), reads=[bkb, XB[c][tbi]] + CONST_READ, writes=[XB[c][tbi]])
                    for eh in range(2):
                        c = 2 * g + eh
                        stats_add(acc, c, tbi, t0, n, c == 0, c == KC - 1)
            stats_flush(acc, keep=0)
            return acc

        for pi in range(2):
            for (tbi, t0, n) in tblocks(pi):
                if t0 < TP:
                    src = xT[:, :, pi * TP + t0:pi * TP + t0 + n]
                else:
                    src = xsT[:, :, :]
                for c in range(KC):
                    gate = [XB[KC - 1][0]] if (pi == 0 and tbi > 0) else []
                    P.op("sp", lambda e, c=c, t0=t0, n=n, src=src: e.dma_start(out=X[:, c, t0:t0 + n], in_=src[:, c, :]),
                         reads=gate, writes=[XB[c][tbi]], chan=f"xin{c}_{tbi}")
            if pi == 0:
                prefetch(1)
            acc = None
            for i in range(DEPTH):
                kind = i % 3
                norm(pi, PV_GM + i * 8, "hf" if kind == 2 else "h", acc)
                if pi == 0 and i == 0:
                    state_loads()
                set_rings(4, sp=len(tblocks(pi)) % 4)
                if kind == 0:
                    acc = mixer_a(pi, i)
                    if pi == 0 and i == 0:
                        late_setup()
                    if pi == 1 and i == 3:
                        P.op("sp", lambda e: e.dma_start(out=o_a[:, :, :, :], in_=STA[:, :, :, :]),
                             reads=[b for r in STAB for b in r], chan="o_a")
                elif kind == 1:
                    acc = mixer_b(pi, i)
                else:
                    acc = mixer_c(pi, i)
                    if pi == 1:
                        P.op("sp", lambda e: e.dma_start(out=o_p[:, :, :], in_=STP[:, :, :]), reads=[STPB], chan="o_p")
                norm(pi, PV_GF + i * 8, "h", acc)
                P.op("act", lambda e: e.activation(out=JUNK[:, 1:2], in_=EPSB[:, 0:1], func=AF.Silu),
                     reads=[C_EPS], writes=[JUNKB])
                acc = ffn(pi, i)
            norm(pi, PV_GFIN, "y", acc)

        esem = {e: es.enter_context(nc.semaphore(f"sem_{e}")) for e in ENGS}
        chans = sorted({o.chan for e in ENGS for o in P.q[e] if o.chan is not None})
        csem = {c: es.enter_context(nc.semaphore(f"sem_c_{c}")) for c in chans}
        block = es.enter_context(nc.Block())
        engines = {"pe": "tensor", "act": "scalar", "dve": "vector", "pool": "gpsimd", "sp": "sync"}
        P.emit(nc, block, engines, esem, csem, final_chans=["yout", "cvout", "stout"])
    return nc


_NC_CACHE = {}


def kernel(**inp):
    inp = {k: np.asarray(v) for k, v in inp.items()}
    wstream = pack_stream(inp)
    pvec = pack_pvec(inp)
    pbc = pack_pbc(inp)
    wsT = np.ascontiguousarray(inp["b_w_s"][0].transpose(2, 0, 1))
    nslots = wstream.shape[0]
    if nslots not in _NC_CACHE:
        _NC_CACHE[nslots] = build_nc(nslots)
    nc = _NC_CACHE[nslots]
    in_maps = []
    for b in range(NCORES):
        xs = inp["x_sample"][NSEQ * b:NSEQ * (b + 1)].reshape(128, D)
        sa = inp["state_shortconv"][:, NSEQ * b:NSEQ * (b + 1)]
        sf = inp["state_ffnconv"][:, NSEQ * b:NSEQ * (b + 1)]
        sp_ = inp["state_pool"][0, NSEQ * b:NSEQ * (b + 1)]
        in_maps.append({
            "xT": np.ascontiguousarray(fm(inp["x_prompt"][b]).transpose(0, 2, 1)),
            "xsT": np.ascontiguousarray(fm(xs).transpose(0, 2, 1)),
            "st_a": np.ascontiguousarray(np.pad(fm(sa.reshape(2, 32, D)).transpose(0, 1, 3, 2),
                                                ((0, 0), (0, 0), (0, 0), (2, 0)))),
            "st_f": np.ascontiguousarray(np.pad(fm(sf.reshape(DEPTH, 32, DFF)).transpose(0, 1, 3, 2),
                                                ((0, 0), (0, 0), (0, 0), (2, 0)))),
            "st_p": np.ascontiguousarray(np.pad(fm(sp_.reshape(240, D)).transpose(0, 2, 1),
                                                ((0, 0), (0, 0), (15, 0)))),
            "pvec": pvec, "pbc": pbc, "wsT": wsT, "wstream": wstream,
        })
    res = run_bass_kernel_spmd(nc, in_maps, core_ids=list(range(NCORES)))
    B = NCORES
    y_prompt = np.zeros((B, SEQ, D), np.float32)
    y_sample = np.zeros((B * NSEQ, DSEQ, D), np.float32)
    sc_p = np.zeros((2, B, 2, D), np.float32)
    sc_s = np.zeros((2, B * NSEQ, 2, D), np.float32)
    pl_p = np.zeros((1, B, 15, D), np.float32)
    pl_s = np.zeros((1, B * NSEQ, 15, D), np.float32)
    ff_p = np.zeros((DEPTH, B, 2, DFF), np.float32)
    ff_s = np.zeros((DEPTH, B * NSEQ, 2, DFF), np.float32)
    cv_s = np.zeros((1, B * NSEQ, DSEQ, D), np.float32)

    def unfm(a):
        a = np.moveaxis(a, 0, -1)
        a = np.swapaxes(a, -3, -2)
        return a.reshape(a.shape[:-2] + (-1,))

    for b in range(B):
        r = res.results[b]
        yt = unfm(r["yT"])
        y_prompt[b] = yt[:SEQ]
        y_sample[NSEQ * b:NSEQ * (b + 1)] = yt[SEQ:].reshape(NSEQ, DSEQ, D)
        oa = unfm(r["o_a"])
        sc_p[:, b] = oa[:, 0:2]
        sc_s[:, NSEQ * b:NSEQ * (b + 1)] = oa[:, 2:34].reshape(2, NSEQ, 2, D)
        of = unfm(r["o_f"])
        ff_p[:, b] = of[:, 0:2]
        ff_s[:, NSEQ * b:NSEQ * (b + 1)] = of[:, 2:34].reshape(DEPTH, NSEQ, 2, DFF)
        op_ = unfm(r["o_p"])
        pl_p[0, b] = op_[0:15]
        pl_s[0, NSEQ * b:NSEQ * (b + 1)] = op_[15:255].reshape(NSEQ, 15, D)
        cv_s[0, NSEQ * b:NSEQ * (b + 1)] = r["o_cv"].reshape(NSEQ, DSEQ, D)
    return (y_prompt, y_sample, sc_p, sc_s, pl_p, pl_s, ff_p, ff_s, cv_s)
```

```python
from contextlib import ExitStack

import numpy as np
import concourse.bass as bass
import concourse.mybir as mybir
from concourse.bass_utils import run_bass_kernel_spmd

F32 = mybir.dt.float32
BF16 = mybir.dt.bfloat16
ALU = mybir.AluOpType
AF = mybir.ActivationFunctionType

D = 1024
KC = 8
DFF = 2816
MC = 22
DEPTH = 4
EPS = 1e-6
SEQ = 2048
NSEQ = 16
DSEQ = 8
TP = 1024
T0 = TP + NSEQ * DSEQ
SLOT = 4096
NCORES = 8
WIN = (2, 4, 8, 16)
MERGE_PAIRS = False
SUM_ENG = ("dve", "dve", "dve", "dve", "dve", "pool", "pool", "pool")
CH_ORDER = (7, 6, 5, 4, 3, 2, 1, 0)

PV_GM = 0
PV_GF = 32
PV_GFIN = 64
PV_AC = 72
PV_CS = 120
PV_FC = 128
PV_FB = 392
PV_N = 480

COMPUTE = ("pe", "act", "dve", "pool")
QUEUES = ("sp",)
ENGS = ("pe", "act", "dve", "pool", "sp")


class Buf:
    __slots__ = ("name", "w", "r", "rd", "extra", "grp")

    def __init__(self, name, grp=None):
        self.name = name
        self.w = None
        self.r = {}
        self.rd = []
        self.extra = []
        self.grp = grp


class Op:
    __slots__ = ("eng", "fn", "deps", "signal", "chan", "val", "idx")

    def __init__(self, eng, fn, chan):
        self.eng = eng
        self.fn = fn
        self.chan = chan
        self.deps = []
        self.signal = False
        self.val = 0
        self.idx = 0


class Prog:
    def __init__(self):
        self.q = {e: [] for e in ENGS}
        self.active = {}
        self.grp_bufs = {}
        self.chans = {}
        self.n = 0

    def buf(self, name, grp=None):
        b = Buf(name, grp)
        if grp is not None:
            self.grp_bufs.setdefault(grp, []).append(b)
        return b

    def _activate(self, b):
        if b.grp is None:
            return
        region = b.grp[0]
        cur = self.active.get(region)
        if cur == b.grp:
            return
        if cur is not None:
            pend = {}
            for ob in self.grp_bufs[cur]:
                for o in ([ob.w] if ob.w is not None else []) + list(ob.r.values()) + ob.rd + ob.extra:
                    key = o.eng if o.chan is None else ("dma", id(o))
                    if key not in pend or pend[key].idx < o.idx:
                        pend[key] = o
            pl = list(pend.values())
            for nb in self.grp_bufs[b.grp]:
                nb.extra = list(pl)
        self.active[region] = b.grp

    def op(self, eng, fn, reads=(), writes=(), chan=None):
        o = Op(eng, fn, chan)
        o.idx = self.n
        self.n += 1
        for b in reads:
            self._activate(b)
        for b in writes:
            self._activate(b)
        deps = {}
        for b in reads:
            if b.w is not None:
                deps[id(b.w)] = b.w
            for x in b.extra:
                deps[id(x)] = x
        for b in writes:
            if b.w is not None:
                deps[id(b.w)] = b.w
            for x in b.r.values():
                deps[id(x)] = x
            for x in b.rd:
                deps[id(x)] = x
            for x in b.extra:
                deps[id(x)] = x
        for d in deps.values():
            if d is o:
                continue
            if d.chan is None and d.eng == "pe" and eng == "pe" and chan is None:
                continue
            o.deps.append(d)
            d.signal = True
        for b in reads:
            if chan is None:
                b.r[eng] = o
            else:
                b.rd.append(o)
        for b in writes:
            b.w = o
            b.r = {}
            b.rd = []
            b.extra = []
        self.q[eng].append(o)
        return o

    def emit(self, nc, block, engines, esem, csem, final_chans):
        for e in ENGS:
            cnt = 0
            for o in self.q[e]:
                if o.chan is None:
                    if o.signal:
                        cnt += 1
                    o.val = cnt
        allops = sorted((o for e in ENGS for o in self.q[e]), key=lambda o: o.idx)
        ccount = {}
        for o in allops:
            if o.chan is not None:
                ccount[o.chan] = ccount.get(o.chan, 0) + 16
                o.val = ccount[o.chan]

        know = {e: {} for e in ENGS}
        opknow = {}
        waits = {}
        for o in allops:
            ke = know[o.eng]
            wl = []
            for d in sorted(o.deps, key=lambda d: -d.idx):
                key = ("e", d.eng) if d.chan is None else ("c", d.chan)
                if ke.get(key, 0) >= d.val:
                    continue
                wl.append((key, d.val))
                ke[key] = d.val
                for k2, v2 in opknow[id(d)].items():
                    if ke.get(k2, 0) < v2:
                        ke[k2] = v2
            waits[id(o)] = wl
            if o.chan is not None or o.signal:
                opknow[id(o)] = dict(ke)

        def run(e, eng):
            for o in self.q[e]:
                for key, val in waits[id(o)]:
                    sem = esem[key[1]] if key[0] == "e" else csem[key[1]]
                    eng.wait_ge(sem, val)
                ins = o.fn(eng)
                if o.chan is not None:
                    ins.then_inc(csem[o.chan], 16)
                elif o.signal:
                    ins.then_inc(esem[e], 1)
            if e == "sp":
                for c in sorted(ccount):
                    eng.wait_ge(csem[c], ccount[c])

        for e in ENGS:
            deco = getattr(block, engines[e])
            deco(lambda eng, e=e: run(e, eng))
        self.nwaits = {e: sum(len(waits[id(o)]) for o in self.q[e]) for e in ENGS}


def _kp(w, cols):
    k = w.shape[0] // 128
    return np.ascontiguousarray(w[:, cols].reshape(k, 128, -1).transpose(1, 0, 2))


def _pad(a):
    a = a.reshape(128, -1)
    out = np.zeros((128, SLOT), np.float32)
    out[:, : a.shape[1]] = a
    return out


def slot_plan():
    plan = []
    for i in range(DEPTH):
        kind, j = i % 3, i // 3
        if kind == 0:
            for jc in range(8):
                plan.append(("a_in", i, jc, 8 * 384))
            for q in range(2):
                plan.append(("a_out", i, q, 8 * 512))
        elif kind == 1:
            for h in range(2):
                plan.append(("b_v", i, h, 8 * 512))
            for q in range(2):
                plan.append(("b_u", i, q, 8 * 512))
            for q in range(2):
                plan.append(("b_out", i, q, 8 * 512))
        else:
            plan.append(("c_g", i, 0, 2048))
        for q in range(11):
            plan.append(("f_up", i, q, 8 * 512))
        for mo in range(8):
            plan.append(("f_dn", i, mo, MC * 128))
    return plan


def pack_stream(inp):
    slots = []
    for kind, i, x, _ in slot_plan():
        j = i // 3
        if kind == "a_in":
            w = inp["a_w_in"][j]
            cols = np.concatenate([np.arange(x * 128, x * 128 + 128) + o for o in (0, 1024, 2048)])
            slots.append(_pad(_kp(w, cols)))
        elif kind == "a_out":
            slots.append(_pad(_kp(inp["a_w_out"][j], np.arange(x * 512, x * 512 + 512))))
        elif kind == "b_v":
            slots.append(_pad(_kp(inp["b_w_in"][j], np.arange(1024 + x * 512, 1024 + x * 512 + 512))))
        elif kind == "b_u":
            slots.append(_pad(_kp(inp["b_w_in"][j], np.arange(x * 512, x * 512 + 512))))
        elif kind == "b_out":
            slots.append(_pad(_kp(inp["b_w_out"][j], np.arange(x * 512, x * 512 + 512))))
        elif kind == "c_g":
            w = inp["c_w_group"][j]
            a = w.reshape(4, 2, 128, 256).transpose(2, 0, 1, 3)
            slots.append(_pad(np.ascontiguousarray(a)))
        elif kind == "f_up":
            w = inp["f_w_up"][i]
            cols = np.concatenate([
                np.arange(2 * x * 128, 2 * x * 128 + 256),
                np.arange(DFF + 2 * x * 128, DFF + 2 * x * 128 + 256),
            ])
            slots.append(_pad(_kp(w, cols)))
        elif kind == "f_dn":
            slots.append(_pad(_kp(inp["f_w_down"][i], np.arange(x * 128, x * 128 + 128))))
    return np.stack(slots)


def fm(v):
    n = v.shape[-1] // 128
    a = v.reshape(v.shape[:-1] + (n, 128))
    return np.moveaxis(a, -1, 0)


def pack_pvec(inp):
    pv = np.zeros((128, PV_N), np.float32)
    pv[:, PV_GM:PV_GM + 32] = fm(inp["g_mix"]).reshape(128, 32)
    pv[:, PV_GF:PV_GF + 32] = fm(inp["g_ffn"]).reshape(128, 32)
    pv[:, PV_GFIN:PV_GFIN + 8] = fm(inp["g_final"]).reshape(128, 8)
    pv[:, PV_AC:PV_AC + 48] = fm(inp["a_conv"]).reshape(128, 48)
    pv[:, PV_CS:PV_CS + 8] = fm(inp["c_scale"]).reshape(128, 8)
    pv[:, PV_FC:PV_FC + 264] = fm(inp["f_conv"]).reshape(128, 264)
    pv[:, PV_FB:PV_FB + 88] = fm(inp["f_conv_b"]).reshape(128, 88)
    return pv


def pack_pbc(inp):
    pb = np.zeros((128, 2048), np.float32)
    pb[:, 0:1024] = np.broadcast_to(inp["b_g_v"][0][None, :], (128, 1024))
    bias = inp["b_bias"][0]
    pb[:, 1024:1536] = np.broadcast_to(bias.reshape(1, 512), (128, 512))
    bs = np.tile(bias[:, 0:8], (1, 16))
    pb[:, 1536:2048] = np.broadcast_to(bs.reshape(1, 512), (128, 512))
    return pb


def build_nc(nslots):
    nc = bass.Bass("TRN2", target_bir_lowering=False)

    def din(name, shape):
        return nc.dram_tensor(name, list(shape), F32, kind="ExternalInput").ap()

    def dout(name, shape):
        return nc.dram_tensor(name, list(shape), F32, kind="ExternalOutput").ap()

    xT = din("xT", (128, KC, SEQ))
    xsT = din("xsT", (128, KC, 128))
    st_a = din("st_a", (128, 2, KC, 34))
    st_f = din("st_f", (128, DEPTH, MC, 34))
    st_p = din("st_p", (128, KC, 255))
    pvec = din("pvec", (128, PV_N))
    pbc = din("pbc", (128, 2048))
    wsT = din("wsT", (128, 4, 128))
    wstream = din("wstream", (nslots, 128, SLOT))
    yT = dout("yT", (128, KC, SEQ + 128))
    o_a = dout("o_a", (128, 2, KC, 34))
    o_f = dout("o_f", (128, DEPTH, MC, 34))
    o_p = dout("o_p", (128, KC, 255))
    o_cv = dout("o_cv", (128, D))

    P = Prog()
    plan = slot_plan()

    with ExitStack() as es:
        def sb(name, shape, dt=F32):
            return es.enter_context(nc.sbuf_tensor(name, list(shape), dt))

        X = sb("X", (128, KC, T0))
        H = sb("H", (128, KC, T0), BF16)
        RA = sb("RA", (128, 12672))
        RB = sb("RB", (128, 6528))
        NWS = 3
        WS = [sb(f"WS{r}", (128, SLOT), BF16) for r in range(NWS)]
        RS = [sb(f"RS{r}", (128, 512)) for r in range(3)]
        STA = sb("STA", (128, 2, KC, 34))
        STF = sb("STF", (128, DEPTH, MC, 34))
        STP = sb("STP", (128, KC, 255))
        PV = sb("PV", (128, PV_N))
        PBC = sb("PBC", (128, 2048))
        ONES = sb("ONES", (128, 128), BF16)
        CORR = sb("CORR", (128, 4, 16))
        IOT = sb("IOT", (128, 16))
        WMF = sb("WMF", (128, 4, 128))
        WMSF = sb("WMSF", (128, 4, 128))
        WMT = sb("WMT", (128, 4, 128), BF16)
        WMTS = sb("WMTS", (128, 4, 128), BF16)
        SS = sb("SS", (128, 9, 4))
        JUNK = sb("JUNK", (128, 2))
        HB16H = sb("HB16H", (128, KC, 15), BF16)
        IDS = sb("IDS", (128, 8, 128), BF16)
        IDF = sb("IDF", (128, 128))
        EPSB = sb("EPSB", (128, 1))
        PS = es.enter_context(nc.psum_tensor("PS", [128, 8, 512], F32))

        RAb = RA[:, :].bitcast(BF16)
        INNER = RAb.rearrange("p (m t) -> p m t", m=MC)
        VN = RAb[:, 8 * T0:16 * T0].rearrange("p (n f) -> p n f", f=D)
        HB16 = RA[:, 0:4156].bitcast(BF16).rearrange("p (c t) -> p c t", c=KC)
        HF30 = RA[:, 4156:4396].rearrange("p (c t) -> p c t", c=KC)
        HFS = RA[:, 4396:4396 + 8 * 368].rearrange("p (c s t) -> p c s t", c=KC, s=NSEQ)
        G = [RB[:, 0:1026], RB[:, 1186:2212]]
        GS = [RB[:, 1026:1186].rearrange("p (s t) -> p s t", t=10),
              RB[:, 2212:2372].rearrange("p (s t) -> p s t", t=10)]
        TT = [RB[:, 2372 + 512 * r:2372 + 512 * (r + 1)] for r in range(6)]
        VNF = RB[:, 5444:6468]
        PSET = [[RB[:, 1039 * (2 * si + r):1039 * (2 * si + r + 1)] for r in range(2)] for si in range(2)]
        PSETS = [[RB[:, 4156 + 368 * (2 * si + r):4156 + 368 * (2 * si + r + 1)].rearrange("p (s t) -> p s t", s=NSEQ)
                  for r in range(2)] for si in range(2)]
        GVB = PBC[:, 0:1024]
        BB = PBC[:, 1024:1536].rearrange("p (h t) -> p h t", h=4)
        BBS = PBC[:, 1536:2048].rearrange("p (h t) -> p h t", h=4)

        XB = [[P.buf(f"X{c}_{t}") for t in range(3)] for c in range(KC)]
        HB = [[P.buf(f"H{c}_{t}") for t in range(3)] for c in range(KC)]
        INB = [[P.buf(f"IN{m}_{t}", ("RA", "inner")) for t in range(3)] for m in range(MC)]
        OBB = [[P.buf(f"OB{m}_{t}", ("RA", "mixb")) for t in range(3)] for m in range(KC)]
        VNB = [P.buf(f"VN{n}", ("RA", "mixb")) for n in range(9)]
        HFB = [[P.buf(f"HF{c}_{t}", ("RA", "poolh")) for t in range(3)] for c in range(KC)]
        HFH = P.buf("HFhalo", ("RA", "poolh"))
        F30B = P.buf("HF30", ("RA", "poolh"))
        C_IDS = P.buf("c_ids")
        JUNKB = P.buf("junk")
        HB16HB = P.buf("HB16H")
        GB = [P.buf(f"G{r}", ("RB", "ffn")) for r in range(2)]
        TB = [P.buf(f"T{r}", ("RB", "ffn")) for r in range(6)]
        VNFB = P.buf("VNF", ("RB", "ffn"))
        PAB = [[P.buf(f"PA{si}_{r}", ("RB", "pool")) for r in range(4)] for si in range(2)]
        WSB = [P.buf(f"WS{r}") for r in range(NWS)]
        RSB = [P.buf(f"RS{r}") for r in range(3)]
        BANKB = [P.buf(f"BK{r}") for r in range(8)]
        STAB = [[P.buf(f"STA{j}_{c}") for c in range(KC)] for j in range(2)]
        STFB = [[P.buf(f"STF{i}_{m}") for m in range(MC)] for i in range(DEPTH)]
        STPB = P.buf("STP")
        C_PV = P.buf("c_pv")
        C_PBC = P.buf("c_pbc")
        C_WM = P.buf("c_wm")
        C_WMS = P.buf("c_wms")
        C_WMSI = [P.buf(f"c_wmsi{s}") for s in range(NSEQ)]
        C_ONES = P.buf("c_ones")
        C_CORR = P.buf("c_corr")
        C_EPS = P.buf("c_eps")
        ALLST = [b for r in STAB for b in r] + [b for r in STFB for b in r] + [STPB]
        SSB = [P.buf(f"SS{n}") for n in range(9)]

        state = {"slot": 0, "rs": 0, "tt": 0, "issued": 0, "sp": 0, "lp": 0, "dp": 0, "down": None}

        SHORT = [0, 1, 2, 3]
        LONG = [4, 5, 6, 7]

        def set_rings(nshort, sp=0):
            SHORT[:] = list(range(nshort))
            LONG[:] = list(range(nshort, 8))
            state["sp"] = sp
            state["lp"] = 0

        def bank(kind="short"):
            if state["down"] is not None:
                ring = state["down"]
                b = ring[state["dp"] % len(ring)]
                state["dp"] += 1
            elif kind == "long":
                b = LONG[state["lp"] % len(LONG)]
                state["lp"] += 1
            else:
                b = SHORT[state["sp"] % len(SHORT)]
                state["sp"] += 1
            return PS[:, b, :], BANKB[b]

        def stats_begin(pi):
            tbs = tblocks(pi)
            nst = len(tbs)
            state["down"] = SHORT[nst:] + [LONG[(state["lp"] + i) % len(LONG)] for i in range(len(LONG))]
            state["dp"] = 0
            return {"pend": [], "banks": {tbi: (PS[:, SHORT[j], :], BANKB[SHORT[j]]) for j, (tbi, _, _) in enumerate(tbs)}}

        def stats_add(acc, c, tbi, t0, n, first, last):
            P.op("act", lambda e: e.activation(out=H[:, c, t0:t0 + n], in_=X[:, c, t0:t0 + n], func=AF.Square),
                 reads=[XB[c][tbi]], writes=[HB[c][tbi]])
            bk, bkb = acc["banks"][tbi]
            acc["pend"].append(lambda: P.op("pe", lambda e: e.matmul(bk[:, 0:n], lhsT=ONES[:, :], rhs=H[:, c, t0:t0 + n],
                                                                    start=first, stop=last),
                                            reads=[HB[c][tbi], C_ONES], writes=[bkb]))

        def stats_flush(acc, keep=1):
            pend = acc["pend"]
            n = max(0, len(pend) - keep)
            for f in pend[:n]:
                f()
            acc["pend"] = pend[n:]

        def stats_end(nst):
            state["down"] = None
            state["sp"] = nst % len(SHORT)

        def rsbuf():
            r = state["rs"] % 3
            state["rs"] += 1
            return RS[r], RSB[r]

        def ttpair():
            r = state["tt"] % 3
            state["tt"] += 1
            return (TT[2 * r], TB[2 * r]), (TT[2 * r + 1], TB[2 * r + 1])

        def issue_slot(s):
            kind, _, _, used = plan[s % len(plan)]
            r = s % NWS
            src = wstream[s % len(plan), :, 0:used].rearrange("p (a b) -> p a b", a=2)
            dst = WS[r][:, 0:used].rearrange("p (a b) -> p a b", a=2)
            P.op("pool", lambda e, dst=dst, src=src: e.dma_start(out=dst, in_=src),
                 writes=[WSB[r]], chan=f"ws{r}")

        def prefetch(n):
            total = 2 * len(plan)
            while state["issued"] < min(state["slot"] + n, total):
                issue_slot(state["issued"])
                state["issued"] += 1

        def next_slot(expect_kind):
            s = state["slot"]
            state["slot"] += 1
            kind, _, _, used = plan[s % len(plan)]
            assert kind == expect_kind, (kind, expect_kind)
            while state["issued"] <= s:
                issue_slot(state["issued"])
                state["issued"] += 1
            r = s % NWS
            return WS[r], WSB[r]

        P.op("sp", lambda e: e.dma_start(out=PV[:, :], in_=pvec[:, :]), writes=[C_PV], chan="s_pv")
        P.op("pool", lambda e: e.memset(ONES[:, :], 1.0), writes=[C_ONES])
        P.op("pool", lambda e: e.memset(EPSB[:, :], EPS), writes=[C_EPS])

        def state_loads():
            P.op("sp", lambda e: e.dma_start(out=STA[:, :, :, :], in_=st_a[:, :, :, :]),
                 writes=[b for r in STAB for b in r], chan="s_sta")
            P.op("sp", lambda e: e.dma_start(out=STF[:, :, :, :], in_=st_f[:, :, :, :]),
                 reads=[HB[KC - 1][0]], writes=[b for r in STFB for b in r], chan="s_stf")
            P.op("sp", lambda e: e.dma_start(out=STP[:, :, :], in_=st_p[:, :, :]),
                 reads=[HB[KC - 1][0]], writes=[STPB], chan="s_stp")

        def late_setup():
            P.op("sp", lambda e: e.dma_start(out=PBC[:, :], in_=pbc[:, :]), writes=[C_PBC], chan="s_pbc")
            P.op("sp", lambda e: e.dma_start(out=WMF[:, :, :], in_=wsT[:, :, :]), writes=[C_WM], chan="s_wm")
            P.op("pool", lambda e: e.memset(WMSF[:, :, :], 0.0), writes=C_WMSI)
            for s in range(NSEQ):
                P.op("sp", lambda e, s=s: e.dma_start(out=WMSF[8 * s:8 * s + 8, :, 8 * s:8 * s + 8],
                                                      in_=wsT[0:8, :, 0:8]), writes=[C_WMSI[s]], chan=f"s_wms{s}")
            P.op("pool", lambda e: e.affine_select(out=WMT[:, :, :], in_=WMF[:, :, :], pattern=[[0, 4], [1, 128]],
                                                   compare_op=ALU.is_ge, fill=0.0, base=0, channel_multiplier=-1),
                 reads=[C_WM], writes=[C_WM])
            P.op("pool", lambda e: e.affine_select(out=WMTS[:, :, :], in_=WMSF[:, :, :], pattern=[[0, 4], [1, 128]],
                                                   compare_op=ALU.is_ge, fill=0.0, base=0, channel_multiplier=-1),
                 reads=C_WMSI, writes=[C_WMS])
            P.op("pool", lambda e: e.memset(HB16H[:, :, :], 0.0), writes=[HB16HB])
            for g in range(4):
                for r, val in enumerate((1.0 / WIN[g] - 1.0, 1.0 / WIN[g])):
                    P.op("pool", lambda e, val=val: e.memset(IDF[:, :], val), reads=[C_IDS], writes=[C_IDS])
                    P.op("pool", lambda e, g=g, r=r: e.affine_select(
                        out=IDS[:, 2 * g + r, :], in_=IDF[:, :], pattern=[[1, 128]], compare_op=ALU.is_equal,
                        fill=0.0, base=0, channel_multiplier=-1), reads=[C_IDS], writes=[C_IDS])
            P.op("pool", lambda e: e.iota(IOT[:, :], [[1, 16]], base=1, channel_multiplier=0,
                                          allow_small_or_imprecise_dtypes=True), writes=[C_CORR])
            P.op("dve", lambda e: e.reciprocal(out=IOT[:, :], in_=IOT[:, :]), reads=[C_CORR], writes=[C_CORR])
            for g in range(4):
                P.op("dve", lambda e, g=g: e.tensor_scalar(out=CORR[:, g, :], in0=IOT[:, :], scalar1=float(WIN[g]),
                                                           scalar2=None, op0=ALU.mult), reads=[C_CORR], writes=[C_CORR])

        CONST_READ = [C_PV]

        def pvc(off):
            return PV[:, off:off + 1]

        def tblocks(pi):
            t = [(0, 0, 512), (1, 512, 512)]
            if pi == 0:
                t.append((2, 1024, 128))
            return t

        def norm(pi, goff, kind, acc=None):
            for (tbi, t0, n) in tblocks(pi):
                hb = [HB[c][tbi] for c in range(KC)]
                xb = [XB[c][tbi] for c in range(KC)]
                if acc is not None:
                    bk, bkb = acc["banks"][tbi]
                else:
                    bk, bkb = bank()
                    for c in range(KC):
                        P.op("act", lambda e, c=c, t0=t0, n=n: e.activation(out=H[:, c, t0:t0 + n],
                                                                            in_=X[:, c, t0:t0 + n], func=AF.Square),
                             reads=[xb[c]], writes=[hb[c]])
                    for c in range(KC):
                        P.op("pe", lambda e, c=c, t0=t0, n=n, bk=bk: e.matmul(bk[:, 0:n], lhsT=ONES[:, :],
                                                                             rhs=H[:, c, t0:t0 + n],
                                                                             start=(c == 0), stop=(c == KC - 1)),
                             reads=[hb[c], C_ONES], writes=[bkb])
                rs, rsb = rsbuf()
                P.op("act", lambda e, n=n, bk=bk, rs=rs: e.activation(out=rs[:, 0:n], in_=bk[:, 0:n], func=AF.Ln,
                                                                      bias=EPSB[:, 0:1], scale=1.0 / D),
                     reads=[bkb, C_EPS], writes=[rsb])
                P.op("act", lambda e, n=n, rs=rs: e.activation(out=rs[:, 0:n], in_=rs[:, 0:n], func=AF.Exp,
                                                               scale=-0.5),
                     reads=[rsb], writes=[rsb])
                if kind == "y":
                    col = pi * TP + t0 if t0 < TP else SEQ + (t0 - TP)
                    for c in range(KC):
                        P.op("dve", lambda e, c=c, t0=t0, n=n, rs=rs: e.scalar_tensor_tensor(
                            out=X[:, c, t0:t0 + n], in0=X[:, c, t0:t0 + n], scalar=pvc(goff + c), in1=rs[:, 0:n],
                            op0=ALU.mult, op1=ALU.mult), reads=[xb[c], rsb] + CONST_READ, writes=[xb[c]])
                        P.op("sp", lambda e, c=c, t0=t0, n=n, col=col: e.dma_start(out=yT[:, c, col:col + n],
                                                                                 in_=X[:, c, t0:t0 + n]),
                             reads=[xb[c]], chan=f"y{c}_{tbi}")
                elif kind == "hf":
                    for c in range(KC):
                        if t0 < TP:
                            dst = HB16[:, c, 15 + t0:15 + t0 + n]
                            i1 = rs[:, 0:n]
                            i0 = X[:, c, t0:t0 + n]
                        else:
                            dst = HFS[:, c, :, 15:23]
                            i1 = rs[:, 0:n].rearrange("p (s t) -> p s t", t=DSEQ)
                            i0 = X[:, c, t0:t0 + n].rearrange("p (s t) -> p s t", t=DSEQ)
                        P.op("dve", lambda e, c=c, dst=dst, i0=i0, i1=i1: e.scalar_tensor_tensor(
                            out=dst, in0=i0, scalar=pvc(goff + c), in1=i1, op0=ALU.mult, op1=ALU.mult),
                            reads=[xb[c], rsb] + CONST_READ, writes=[HFB[c][tbi]])
                    if pi == 0 and t0 == 0:
                        for c in range(KC):
                            P.op("dve", lambda e, c=c, rs=rs: e.scalar_tensor_tensor(
                                out=HF30[:, c, 15:30], in0=X[:, c, 0:15], scalar=pvc(goff + c), in1=rs[:, 0:15],
                                op0=ALU.mult, op1=ALU.mult), reads=[xb[c], rsb] + CONST_READ, writes=[F30B])
                    if pi == 1 and t0 == TP - 512:
                        for c in range(KC):
                            P.op("dve", lambda e, c=c, rs=rs: e.scalar_tensor_tensor(
                                out=STP[:, c, 0:15], in0=X[:, c, TP - 15:TP], scalar=pvc(goff + c),
                                in1=rs[:, 512 - 15:512], op0=ALU.mult, op1=ALU.mult),
                                reads=[xb[c], rsb] + CONST_READ, writes=[STPB])
                else:
                    for c in range(KC):
                        P.op("dve", lambda e, c=c, t0=t0, n=n, rs=rs: e.scalar_tensor_tensor(
                            out=H[:, c, t0:t0 + n], in0=X[:, c, t0:t0 + n], scalar=pvc(goff + c), in1=rs[:, 0:n],
                            op0=ALU.mult, op1=ALU.mult), reads=[xb[c], rsb] + CONST_READ, writes=[hb[c]])
            if acc is not None:
                stats_end(len(tblocks(pi)))

        def linear_out(pi, kind, src, srcb, nk, resid_scale_off=None):
            acc = stats_begin(pi)
            for q in range(2):
                ws, wsb = next_slot(kind)
                wv = ws[:, :].rearrange("p (k n) -> p k n", k=nk)
                for mo4 in range(4):
                    mo = q * 4 + mo4
                    for (tbi, t0, n) in tblocks(pi):
                        bk, bkb = bank()
                        for k in range(nk):
                            P.op("pe", lambda e, k=k, mo4=mo4, t0=t0, n=n, bk=bk, wv=wv: e.matmul(
                                bk[:, 0:n], lhsT=wv[:, k, mo4 * 128:(mo4 + 1) * 128], rhs=src[:, k, t0:t0 + n],
                                start=(k == 0), stop=(k == nk - 1)), reads=[wsb, srcb[k][tbi]], writes=[bkb])
                        stats_flush(acc)
                        P.op("dve", lambda e, mo=mo, t0=t0, n=n, bk=bk: e.tensor_tensor(
                            out=X[:, mo, t0:t0 + n], in0=bk[:, 0:n], in1=X[:, mo, t0:t0 + n], op=ALU.add),
                            reads=[bkb, XB[mo][tbi]], writes=[XB[mo][tbi]])
                        stats_add(acc, mo, tbi, t0, n, mo == 0, mo == KC - 1)
            stats_flush(acc, keep=0)
            return acc

        def conv_tail(pi, tbi, t0, n, gi, w0, w1, ta, tab, tbb_ap, tbbb, extra_reads):
            if t0 < TP:
                g1 = G[gi][:, 1 + t0:1 + t0 + n]
                g0 = G[gi][:, t0:t0 + n]
                a_ = ta[:, 0:n]
                b_ = tbb_ap[:, 0:n]
            else:
                g1 = GS[gi][:, :, 1:9]
                g0 = GS[gi][:, :, 0:8]
                a_ = ta[:, 0:n].rearrange("p (s t) -> p s t", t=DSEQ)
                b_ = tbb_ap[:, 0:n].rearrange("p (s t) -> p s t", t=DSEQ)
            P.op("dve", lambda e: e.scalar_tensor_tensor(out=b_, in0=g1, scalar=w1, in1=a_, op0=ALU.mult, op1=ALU.add),
                 reads=[GB[gi], tab] + extra_reads, writes=[tbbb])
            P.op("dve", lambda e: e.scalar_tensor_tensor(out=a_, in0=g0, scalar=w0, in1=b_, op0=ALU.mult, op1=ALU.add),
                 reads=[GB[gi], tbbb] + extra_reads, writes=[tab])

        def halo_in(pi, gi, st_ap, stb):
            P.op("act", lambda e: e.copy(out=G[gi][:, 0:2], in_=st_ap[:, 0:2]), reads=[stb], writes=[GB[gi]])
            if pi == 0:
                P.op("act", lambda e: e.copy(out=GS[gi][:, :, 0:2],
                                             in_=st_ap[:, 2:34].rearrange("p (s t) -> p s t", t=2)),
                     reads=[stb], writes=[GB[gi]])

        def halo_out(pi, gi, st_ap, stb):
            P.op("act", lambda e: e.copy(out=st_ap[:, 0:2], in_=G[gi][:, TP:TP + 2]), reads=[GB[gi]], writes=[stb])
            if pi == 0:
                P.op("act", lambda e: e.copy(out=st_ap[:, 2:34].rearrange("p (s t) -> p s t", t=2),
                                             in_=GS[gi][:, :, 8:10]), reads=[GB[gi]], writes=[stb])

        def ffn(pi, i):
            tail = [None]

            def unit(ws_v, wsb, ms, tbi, t0, n):
                bks = []
                for (m, mm, gi) in ms:
                    bks.append((bank(), bank("long")))
                for k in range(KC):
                    for (m, mm, gi), ((bg, bgb), (ba, bab)) in zip(ms, bks):
                        P.op("pe", lambda e, k=k, mm=mm, bg=bg: e.matmul(
                            bg[:, 0:n], lhsT=ws_v[:, k, mm * 128:(mm + 1) * 128], rhs=H[:, k, t0:t0 + n],
                            start=(k == 0), stop=(k == KC - 1)), reads=[wsb, HB[k][tbi]], writes=[bgb])
                        P.op("pe", lambda e, k=k, mm=mm, ba=ba: e.matmul(
                            ba[:, 0:n], lhsT=ws_v[:, k, 256 + mm * 128:256 + (mm + 1) * 128], rhs=H[:, k, t0:t0 + n],
                            start=(k == 0), stop=(k == KC - 1)), reads=[wsb, HB[k][tbi]], writes=[bab])
                for (m, mm, gi), ((bg, bgb), (ba, bab)) in zip(ms, bks):
                    w0, w1, w2 = (pvc(PV_FC + i * 66 + k * 22 + m) for k in range(3))
                    bcol = pvc(PV_FB + i * 22 + m)
                    (ta, tab), (tb_, tbb) = ttpair()
                    if t0 < TP:
                        gdst = G[gi][:, 2 + t0:2 + t0 + n]
                        gsrc = bg[:, 0:n]
                    else:
                        gdst = GS[gi][:, :, 2:10]
                        gsrc = bg[:, 0:n].rearrange("p (s t) -> p s t", t=DSEQ)
                    P.op("act", lambda e, gdst=gdst, gsrc=gsrc: e.copy(out=gdst, in_=gsrc),
                         reads=[bgb], writes=[GB[gi]])
                    P.op("act", lambda e, bg=bg, ta=ta, w2=w2, bcol=bcol: e.activation(
                        out=ta[:, 0:n], in_=bg[:, 0:n], func=AF.Identity, bias=bcol, scale=w2),
                        reads=[bgb] + CONST_READ, writes=[tab])
                    conv_tail(pi, tbi, t0, n, gi, w0, w1, ta, tab, tb_, tbb, CONST_READ)
                    if tail[0] is not None:
                        tail[0]()

                    def mk_tail(m=m, ta=ta, tab=tab, tb_=tb_, tbb=tbb, ba=ba, bab=bab):
                        P.op("act", lambda e: e.activation(out=tb_[:, 0:n], in_=ta[:, 0:n], func=AF.Silu),
                             reads=[tab], writes=[tbb])
                        P.op("dve", lambda e: e.tensor_tensor(out=INNER[:, m, t0:t0 + n], in0=ba[:, 0:n],
                                                              in1=tb_[:, 0:n], op=ALU.mult),
                             reads=[bab, tbb], writes=[INB[m][tbi]])
                    tail[0] = mk_tail

            set_rings(3)
            for q in range(11):
                ws, wsb = next_slot("f_up")
                wv = ws[:, :].rearrange("p (k n) -> p k n", k=KC)
                ms = [(2 * q + mm, mm, mm) for mm in range(2)]
                if MERGE_PAIRS or q == 0:
                    for (m, mm, gi) in ms:
                        halo_in(pi, gi, STF[:, i, m, :], STFB[i][m])
                    for (tbi, t0, n) in tblocks(pi):
                        unit(wv, wsb, ms, tbi, t0, n)
                    for (m, mm, gi) in ms:
                        halo_out(pi, gi, STF[:, i, m, :], STFB[i][m])
                else:
                    for (m, mm, gi) in ms:
                        halo_in(pi, gi, STF[:, i, m, :], STFB[i][m])
                        tbs = tblocks(pi)
                        for (tbi, t0, n) in tbs[2:] + tbs[:2]:
                            unit(wv, wsb, [(m, mm, gi)], tbi, t0, n)
                        halo_out(pi, gi, STF[:, i, m, :], STFB[i][m])
            tail[0]()
            if pi == 1 and i == DEPTH - 1:
                P.op("sp", lambda e: e.dma_start(out=o_f[:, :, :, :], in_=STF[:, :, :, :]),
                     reads=[b for r in STFB for b in r], chan="o_f")
            acc = stats_begin(pi)
            P.op("act", lambda e: e.activation(out=JUNK[:, 0:1], in_=EPSB[:, 0:1], func=AF.Ln),
                 reads=[C_EPS], writes=[JUNKB])
            for mo in range(KC):
                ws, wsb = next_slot("f_dn")
                wv = ws[:, 0:MC * 128].rearrange("p (m n) -> p m n", m=MC)
                for (tbi, t0, n) in tblocks(pi):
                    bk, bkb = bank()
                    for m in range(MC):
                        P.op("pe", lambda e, m=m, t0=t0, n=n, bk=bk, wv=wv: e.matmul(
                            bk[:, 0:n], lhsT=wv[:, m, :], rhs=INNER[:, m, t0:t0 + n],
                            start=(m == 0), stop=(m == MC - 1)), reads=[wsb, INB[m][tbi]], writes=[bkb])
                    stats_flush(acc, keep=0)
                    P.op("dve", lambda e, mo=mo, t0=t0, n=n, bk=bk: e.tensor_tensor(
                        out=X[:, mo, t0:t0 + n], in0=bk[:, 0:n], in1=X[:, mo, t0:t0 + n], op=ALU.add),
                        reads=[bkb, XB[mo][tbi]], writes=[XB[mo][tbi]])
                    stats_add(acc, mo, tbi, t0, n, mo == 0, mo == KC - 1)
            stats_flush(acc, keep=0)
            return acc

        def mixer_a(pi, i):
            j = i // 3
            tail = [None]
            for jc in range(KC):
                ws, wsb = next_slot("a_in")
                wv = ws[:, 0:KC * 384].rearrange("p (k n) -> p k n", k=KC)
                gi = jc % 2
                st_ap = STA[:, j, jc, :]
                stb = STAB[j][jc]
                halo_in(pi, gi, st_ap, stb)
                w0, w1, w2 = (pvc(PV_AC + j * 24 + k * 8 + jc) for k in range(3))
                for (tbi, t0, n) in tblocks(pi):
                    banks = [bank(), bank(), bank("long")]
                    for k in range(KC):
                        for bi, (bk, bkb) in enumerate(banks):
                            P.op("pe", lambda e, k=k, bi=bi, t0=t0, n=n, bk=bk, wv=wv: e.matmul(
                                bk[:, 0:n], lhsT=wv[:, k, (2 - bi) * 128:(3 - bi) * 128], rhs=H[:, k, t0:t0 + n],
                                start=(k == 0), stop=(k == KC - 1)), reads=[wsb, HB[k][tbi]], writes=[bkb])
                    (pv_, pvb), (pc, pcb), (pb, pbb) = banks
                    (ta, tab), (tb_, tbb) = ttpair()
                    P.op("act", lambda e, n=n, ta=ta, pv_=pv_: e.copy(out=ta[:, 0:n], in_=pv_[:, 0:n]),
                         reads=[pvb], writes=[tab])
                    if t0 < TP:
                        gdst = G[gi][:, 2 + t0:2 + t0 + n]
                        csrc = pc[:, 0:n]
                        vsrc = ta[:, 0:n]
                    else:
                        gdst = GS[gi][:, :, 2:10]
                        csrc = pc[:, 0:n].rearrange("p (s t) -> p s t", t=DSEQ)
                        vsrc = ta[:, 0:n].rearrange("p (s t) -> p s t", t=DSEQ)
                    P.op("dve", lambda e, gdst=gdst, csrc=csrc, vsrc=vsrc: e.tensor_tensor(
                        out=gdst, in0=csrc, in1=vsrc, op=ALU.mult), reads=[pcb, tab], writes=[GB[gi]])
                    P.op("act", lambda e, gdst=gdst, ta=ta, n=n, w2=w2, t0=t0: e.activation(
                        out=(ta[:, 0:n] if t0 < TP else ta[:, 0:n].rearrange("p (s t) -> p s t", t=DSEQ)),
                        in_=gdst, func=AF.Identity, scale=w2), reads=[GB[gi]] + CONST_READ, writes=[tab])
                    if tail[0] is not None:
                        tail[0]()

                    def mk_tail(jc=jc, tbi=tbi, t0=t0, n=n, gi=gi, w0=w0, w1=w1, ta=ta, tab=tab, tb_=tb_, tbb=tbb,
                                pb=pb, pbb=pbb):
                        conv_tail(pi, tbi, t0, n, gi, w0, w1, ta, tab, tb_, tbb, CONST_READ)
                        P.op("dve", lambda e: e.tensor_tensor(out=INNER[:, jc, t0:t0 + n], in0=pb[:, 0:n],
                                                              in1=ta[:, 0:n], op=ALU.mult),
                             reads=[pbb, tab], writes=[INB[jc][tbi]])
                    tail[0] = mk_tail
                halo_out(pi, gi, st_ap, stb)
            tail[0]()
            return linear_out(pi, "a_out", INNER, INB, KC)

        def mixer_b(pi, i):
            OUTB = INNER
            wvs = []
            for h in range(2):
                ws, wsb = next_slot("b_v")
                wvs.append((ws[:, :].rearrange("p (k n) -> p k n", k=KC), wsb))
            ntiles = 9 if pi == 0 else 8
            for nt in range(ntiles):
                t0 = nt * 128
                tbi = t0 // 512
                halves = []
                for h in range(2):
                    bk, bkb = bank("short" if h == 0 else "long")
                    wv, wsb = wvs[h]
                    for k in range(KC):
                        P.op("pe", lambda e, k=k, t0=t0, bk=bk, wv=wv: e.matmul(
                            bk[:, :], lhsT=H[:, k, t0:t0 + 128], rhs=wv[:, k, :],
                            start=(k == 0), stop=(k == KC - 1)), reads=[wsb, HB[k][tbi]], writes=[bkb])
                    halves.append((bk, bkb))
                (ta, tab), (tb_, tbb) = ttpair()
                for h in range(2):
                    bk, bkb = halves[h]
                    junk, junkb = (ta, tab) if h == 0 else (tb_, tbb)
                    P.op("act", lambda e, h=h, bk=bk, junk=junk, nt=nt: e.activation(
                        out=junk[:, :], in_=bk[:, :], func=AF.Square, accum_out=SS[:, nt, h:h + 1]),
                        reads=[bkb], writes=[junkb, SSB[nt]])
                P.op("dve", lambda e, nt=nt: e.tensor_tensor(out=SS[:, nt, 2:3], in0=SS[:, nt, 0:1],
                                                             in1=SS[:, nt, 1:2], op=ALU.add),
                     reads=[SSB[nt]], writes=[SSB[nt]])
                P.op("act", lambda e, nt=nt: e.activation(out=SS[:, nt, 3:4], in_=SS[:, nt, 2:3], func=AF.Ln,
                                                          bias=EPSB[:, 0:1], scale=1.0 / D),
                     reads=[SSB[nt], C_EPS], writes=[SSB[nt]])
                P.op("act", lambda e, nt=nt: e.activation(out=SS[:, nt, 2:3], in_=SS[:, nt, 3:4], func=AF.Exp,
                                                          scale=-0.5),
                     reads=[SSB[nt]], writes=[SSB[nt]])
                for h in range(2):
                    bk, bkb = halves[h]
                    P.op("dve", lambda e, h=h, bk=bk, nt=nt: e.scalar_tensor_tensor(
                        out=VN[:, nt, h * 512:(h + 1) * 512], in0=bk[:, :], scalar=SS[:, nt, 2:3],
                        in1=GVB[:, h * 512:(h + 1) * 512], op0=ALU.mult, op1=ALU.mult),
                        reads=[bkb, SSB[nt], C_PBC], writes=[VNB[nt]])
                    if nt == 8:
                        P.op("dve", lambda e, h=h, bk=bk, nt=nt: e.scalar_tensor_tensor(
                            out=VNF[:, h * 512:(h + 1) * 512], in0=bk[:, :], scalar=SS[:, nt, 2:3],
                            in1=GVB[:, h * 512:(h + 1) * 512], op0=ALU.mult, op1=ALU.mult),
                            reads=[bkb, SSB[nt], C_PBC], writes=[VNFB])
                if nt == 8:
                    P.op("sp", lambda e: e.dma_start(out=o_cv[:, :], in_=VNF), reads=[VNFB], chan="cvout")
            for q in range(2):
                ws, wsb = next_slot("b_u")
                wv = ws[:, :].rearrange("p (k n) -> p k n", k=KC)
                for d4 in range(4):
                    dc = q * 4 + d4
                    hd = dc // 2
                    for (tbi, t0, n) in tblocks(pi):
                        bs, bsb = bank()
                        nsub = n // 128
                        for sub in range(nsub):
                            nt = t0 // 128 + sub
                            wm = WMT if t0 < TP else WMTS
                            P.op("pe", lambda e, sub=sub, nt=nt, dc=dc, hd=hd, bs=bs, wm=wm: e.matmul(
                                bs[:, sub * 128:(sub + 1) * 128], lhsT=VN[:, nt, dc * 128:(dc + 1) * 128],
                                rhs=wm[:, hd, :], start=True, stop=True),
                                reads=[VNB[nt], C_WM, C_WMS], writes=[bsb])
                        (ta, tab), _ = ttpair()
                        bbv = (BB if t0 < TP else BBS)[:, hd, :]
                        P.op("dve", lambda e, n=n, nsub=nsub, bs=bs, ta=ta, bbv=bbv: e.tensor_tensor(
                            out=ta[:, 0:n].rearrange("p (a t) -> p a t", a=nsub),
                            in0=bs[:, 0:n].rearrange("p (a t) -> p a t", a=nsub),
                            in1=bbv.unsqueeze(1).to_broadcast([128, nsub, 128]), op=ALU.add),
                            reads=[bsb, C_PBC], writes=[tab])
                        bu, bub = bank("long")
                        for k in range(KC):
                            P.op("pe", lambda e, k=k, d4=d4, t0=t0, n=n, bu=bu, wv=wv: e.matmul(
                                bu[:, 0:n], lhsT=wv[:, k, d4 * 128:(d4 + 1) * 128], rhs=H[:, k, t0:t0 + n],
                                start=(k == 0), stop=(k == KC - 1)), reads=[wsb, HB[k][tbi]], writes=[bub])
                        P.op("dve", lambda e, dc=dc, t0=t0, n=n, bu=bu, ta=ta: e.tensor_tensor(
                            out=OUTB[:, dc, t0:t0 + n], in0=bu[:, 0:n], in1=ta[:, 0:n], op=ALU.mult),
                            reads=[bub, tab], writes=[OBB[dc][tbi]])
            return linear_out(pi, "b_out", OUTB, OBB, KC)

        def mixer_c(pi, i):
            ws, wsb = next_slot("c_g")
            wg = ws[:, 0:2048].rearrange("p (g c e) -> p g c e", g=4, c=2)
            prefetch(2)
            P.op("act", lambda e: e.copy(out=HB16[:, :, 0:15], in_=HB16H[:, :, :]), reads=[HB16HB], writes=[HFH])
            if pi == 0:
                P.op("act", lambda e: e.copy(out=HFS[:, :, :, 0:15],
                                             in_=STP[:, :, 15:255].rearrange("p c (s t) -> p c s t", t=15)),
                     reads=[STPB], writes=[HFH])
                P.op("pool", lambda e: e.memset(HF30[:, :, 0:15], 0.0), writes=[F30B])
                P.op("act", lambda e: e.copy(out=HB16H[:, :, :], in_=HB16[:, :, TP:TP + 15]),
                     reads=[HFB[c][1] for c in range(KC)], writes=[HB16HB])
            for (tbi, t0, n) in tblocks(pi):
                if t0 >= TP:
                    continue
                for c in range(KC):
                    g = c // 2
                    w = WIN[g]
                    bk, bkb = bank()
                    for jj in range(w):
                        P.op("pe", lambda e, c=c, g=g, jj=jj, w=w, t0=t0, n=n, bk=bk: e.matmul(
                            bk[:, 0:n], lhsT=IDS[:, 2 * g + (0 if jj == 0 else 1), :],
                            rhs=HB16[:, c, 15 + t0 - jj:15 + t0 - jj + n], start=(jj == 0), stop=(jj == w - 1)),
                            reads=[HFB[c][tbi], HFH, C_IDS] + ([HFB[c][tbi - 1]] if tbi > 0 else []), writes=[bkb])
                    P.op("act", lambda e, c=c, t0=t0, n=n, bk=bk: e.copy(out=H[:, c, t0:t0 + n], in_=bk[:, 0:n]),
                         reads=[bkb], writes=[HB[c][tbi]])
            for c in CH_ORDER:
                g = c // 2
                w = WIN[g]
                eng = SUM_ENG[c]
                si = 0 if eng == "dve" else 1
                if pi == 0:
                    cur, curb = HF30[:, c, :], [F30B]
                    bufs = [(PSET[si][0], [PAB[si][0]]), (PSET[si][1], [PAB[si][1]])]
                    sh = 1
                    for st in range(g + 1):
                        dst, dstb = bufs[st % 2]
                        lo = 2 * sh - 1
                        P.op(eng, lambda e, dst=dst, cur=cur, lo=lo, sh=sh: e.tensor_tensor(
                            out=dst[:, lo:30], in0=cur[:, lo:30], in1=cur[:, lo - sh:30 - sh], op=ALU.add),
                            reads=curb, writes=dstb)
                        cur, curb = dst, dstb
                        sh *= 2
                    P.op(eng, lambda e, cur=cur, g=g, w=w: e.tensor_tensor(
                        out=cur[:, 15:15 + w - 1], in0=cur[:, 15:15 + w - 1], in1=CORR[:, g, 0:w - 1], op=ALU.mult),
                        reads=curb + [C_CORR], writes=curb)
                    P.op("dve", lambda e, cur=cur, c=c, w=w: e.scalar_tensor_tensor(
                        out=H[:, c, 0:15], in0=cur[:, 15:30], scalar=1.0 / w, in1=HF30[:, c, 15:30],
                        op0=ALU.mult, op1=ALU.subtract), reads=curb + [F30B], writes=[HB[c][0]])
            for c in CH_ORDER:
                g = c // 2
                w = WIN[g]
                eng = SUM_ENG[c]
                si = 0 if eng == "dve" else 1
                if pi == 0:
                    hfc = [HFB[c][2], HFH]
                    cur, curb = HFS[:, c, :, :], hfc
                    bufs = [(PSETS[si][0], [PAB[si][2]]), (PSETS[si][1], [PAB[si][3]])]
                    sh = 1
                    for st in range(g + 1):
                        dst, dstb = bufs[st % 2]
                        lo = 2 * sh - 1
                        P.op(eng, lambda e, dst=dst, cur=cur, lo=lo, sh=sh: e.tensor_tensor(
                            out=dst[:, :, lo:23], in0=cur[:, :, lo:23], in1=cur[:, :, lo - sh:23 - sh],
                            op=ALU.add), reads=curb, writes=dstb)
                        cur, curb = dst, dstb
                        sh *= 2
                    if eng == "pool":
                        P.op("pool", lambda e, cur=cur, w=w: e.tensor_scalar(
                            out=cur[:, :, 15:23], in0=cur[:, :, 15:23], scalar1=1.0 / w, scalar2=0.0,
                            op0=ALU.mult, op1=ALU.add), reads=curb, writes=curb)
                        P.op("pool", lambda e, cur=cur, c=c: e.tensor_tensor(
                            out=H[:, c, TP:TP + 128].rearrange("p (s t) -> p s t", t=DSEQ),
                            in0=cur[:, :, 15:23], in1=HFS[:, c, :, 15:23], op=ALU.subtract),
                            reads=curb + [HFB[c][2]], writes=[HB[c][2]])
                    else:
                        P.op("dve", lambda e, cur=cur, c=c, w=w: e.scalar_tensor_tensor(
                            out=H[:, c, TP:TP + 128].rearrange("p (s t) -> p s t", t=DSEQ),
                            in0=cur[:, :, 15:23], scalar=1.0 / w, in1=HFS[:, c, :, 15:23],
                            op0=ALU.mult, op1=ALU.subtract), reads=curb + [HFB[c][2]], writes=[HB[c][2]])
            if pi == 0:
                P.op("act", lambda e: e.copy(out=STP[:, :, 15:255].rearrange("p c (s t) -> p c s t", t=15),
                                             in_=HFS[:, :, :, 8:23]),
                     reads=[HFB[c][2] for c in range(KC)] + [HFH], writes=[STPB])
            acc = stats_begin(pi)
            tbs = tblocks(pi)
            order = [tbs[1], tbs[0]] + tbs[2:]
            for (tbi, t0, n) in order:
                for g in range(4):
                    bks = []
                    for eh in range(2):
                        bk, bkb = bank()
                        bks.append((bk, bkb))
                        for ch in range(2):
                            P.op("pe", lambda e, g=g, ch=ch, eh=eh, t0=t0, n=n, bk=bk: e.matmul(
                                bk[:, 0:n], lhsT=wg[:, g, ch, eh * 128:(eh + 1) * 128], rhs=H[:, 2 * g + ch, t0:t0 + n],
                                start=(ch == 0), stop=(ch == 1)), reads=[wsb, HB[2 * g + ch][tbi]], writes=[bkb])
                    stats_flush(acc)
                    for eh in range(2):
                        c = 2 * g + eh
                        bk, bkb = bks[eh]
                        P.op("dve", lambda e, c=c, t0=t0, n=n, bk=bk: e.scalar_tensor_tensor(
                            out=X[:, c, t0:t0 + n], in0=bk[:, 0:n], scalar=pvc(PV_CS + c), in1=X[:, c, t0:t0 + n],
                            op0=ALU.mult, op1=ALU.add), reads=[bkb, XB[c][tbi]] + CONST_READ, writes=[XB[c][tbi]])
                    for eh in range(2):
                        c = 2 * g + eh
                        stats_add(acc, c, tbi, t0, n, c == 0, c == KC - 1)
            stats_flush(acc, keep=0)
            return acc

        for pi in range(2):
            for (tbi, t0, n) in tblocks(pi):
                if t0 < TP:
                    src = xT[:, :, pi * TP + t0:pi * TP + t0 + n]
                else:
                    src = xsT[:, :, :]
                for c in range(KC):
                    gate = [XB[KC - 1][0]] if (pi == 0 and tbi > 0) else []
                    P.op("sp", lambda e, c=c, t0=t0, n=n, src=src: e.dma_start(out=X[:, c, t0:t0 + n], in_=src[:, c, :]),
                         reads=gate, writes=[XB[c][tbi]], chan=f"xin{c}_{tbi}")
            if pi == 0:
                prefetch(1)
            acc = None
            for i in range(DEPTH):
                kind = i % 3
                norm(pi, PV_GM + i * 8, "hf" if kind == 2 else "h", acc)
                if pi == 0 and i == 0:
                    state_loads()
                set_rings(4, sp=len(tblocks(pi)) % 4)
                if kind == 0:
                    acc = mixer_a(pi, i)
                    if pi == 0 and i == 0:
                        late_setup()
                    if pi == 1 and i == 3:
                        P.op("sp", lambda e: e.dma_start(out=o_a[:, :, :, :], in_=STA[:, :, :, :]),
                             reads=[b for r in STAB for b in r], chan="o_a")
                elif kind == 1:
                    acc = mixer_b(pi, i)
                else:
                    acc = mixer_c(pi, i)
                    if pi == 1:
                        P.op("sp", lambda e: e.dma_start(out=o_p[:, :, :], in_=STP[:, :, :]), reads=[STPB], chan="o_p")
                norm(pi, PV_GF + i * 8, "h", acc)
                acc = ffn(pi, i)
            norm(pi, PV_GFIN, "y", acc)

        esem = {e: es.enter_context(nc.semaphore(f"sem_{e}")) for e in ENGS}
        chans = sorted({o.chan for e in ENGS for o in P.q[e] if o.chan is not None})
        csem = {c: es.enter_context(nc.semaphore(f"sem_c_{c}")) for c in chans}
        block = es.enter_context(nc.Block())
        engines = {"pe": "tensor", "act": "scalar", "dve": "vector", "pool": "gpsimd", "sp": "sync"}
        P.emit(nc, block, engines, esem, csem, final_chans=["yout", "cvout", "stout"])
    return nc


_NC_CACHE = {}


def kernel(**inp):
    inp = {k: np.asarray(v) for k, v in inp.items()}
    wstream = pack_stream(inp)
    pvec = pack_pvec(inp)
    pbc = pack_pbc(inp)
    wsT = np.ascontiguousarray(inp["b_w_s"][0].transpose(2, 0, 1))
    nslots = wstream.shape[0]
    if nslots not in _NC_CACHE:
        _NC_CACHE[nslots] = build_nc(nslots)
    nc = _NC_CACHE[nslots]
    in_maps = []
    for b in range(NCORES):
        xs = inp["x_sample"][NSEQ * b:NSEQ * (b + 1)].reshape(128, D)
        sa = inp["state_shortconv"][:, NSEQ * b:NSEQ * (b + 1)]
        sf = inp["state_ffnconv"][:, NSEQ * b:NSEQ * (b + 1)]
        sp_ = inp["state_pool"][0, NSEQ * b:NSEQ * (b + 1)]
        in_maps.append({
            "xT": np.ascontiguousarray(fm(inp["x_prompt"][b]).transpose(0, 2, 1)),
            "xsT": np.ascontiguousarray(fm(xs).transpose(0, 2, 1)),
            "st_a": np.ascontiguousarray(np.pad(fm(sa.reshape(2, 32, D)).transpose(0, 1, 3, 2),
                                                ((0, 0), (0, 0), (0, 0), (2, 0)))),
            "st_f": np.ascontiguousarray(np.pad(fm(sf.reshape(DEPTH, 32, DFF)).transpose(0, 1, 3, 2),
                                                ((0, 0), (0, 0), (0, 0), (2, 0)))),
            "st_p": np.ascontiguousarray(np.pad(fm(sp_.reshape(240, D)).transpose(0, 2, 1),
                                                ((0, 0), (0, 0), (15, 0)))),
            "pvec": pvec, "pbc": pbc, "wsT": wsT, "wstream": wstream,
        })
    res = run_bass_kernel_spmd(nc, in_maps, core_ids=list(range(NCORES)))
    B = NCORES
    y_prompt = np.zeros((B, SEQ, D), np.float32)
    y_sample = np.zeros((B * NSEQ, DSEQ, D), np.float32)
    sc_p = np.zeros((2, B, 2, D), np.float32)
    sc_s = np.zeros((2, B * NSEQ, 2, D), np.float32)
    pl_p = np.zeros((1, B, 15, D), np.float32)
    pl_s = np.zeros((1, B * NSEQ, 15, D), np.float32)
    ff_p = np.zeros((DEPTH, B, 2, DFF), np.float32)
    ff_s = np.zeros((DEPTH, B * NSEQ, 2, DFF), np.float32)
    cv_s = np.zeros((1, B * NSEQ, DSEQ, D), np.float32)

    def unfm(a):
        a = np.moveaxis(a, 0, -1)
        a = np.swapaxes(a, -3, -2)
        return a.reshape(a.shape[:-2] + (-1,))

    for b in range(B):
        r = res.results[b]
        yt = unfm(r["yT"])
        y_prompt[b] = yt[:SEQ]
        y_sample[NSEQ * b:NSEQ * (b + 1)] = yt[SEQ:].reshape(NSEQ, DSEQ, D)
        oa = unfm(r["o_a"])
        sc_p[:, b] = oa[:, 0:2]
        sc_s[:, NSEQ * b:NSEQ * (b + 1)] = oa[:, 2:34].reshape(2, NSEQ, 2, D)
        of = unfm(r["o_f"])
        ff_p[:, b] = of[:, 0:2]
        ff_s[:, NSEQ * b:NSEQ * (b + 1)] = of[:, 2:34].reshape(DEPTH, NSEQ, 2, DFF)
        op_ = unfm(r["o_p"])
        pl_p[0, b] = op_[0:15]
        pl_s[0, NSEQ * b:NSEQ * (b + 1)] = op_[15:255].reshape(NSEQ, 15, D)
        cv_s[0, NSEQ * b:NSEQ * (b + 1)] = r["o_cv"].reshape(NSEQ, DSEQ, D)
    return (y_prompt, y_sample, sc_p, sc_s, pl_p, pl_s, ff_p, ff_s, cv_s)
```
